# Optimizing a Trainium2 kernel written in Bass

```python
import jax, jax.numpy as jnp
from jax import lax
import numpy as np

D_MODEL = 2048
BATCH = 2
SEQ = 8192
DEPTH = 2

GRID_W = 64
CTX_LEN = 256
N_MOD = 9
FFN_DIM = 5632
HEAD_DIM = 128
N_FOURIER_GROUPS = 4
FOURIER_GROUP_DIM = 128
FOURIER_WIDTH = N_FOURIER_GROUPS * FOURIER_GROUP_DIM
N_Q_HEADS = (D_MODEL - FOURIER_WIDTH) // HEAD_DIM
N_KV_HEADS = 4
Q_GROUP = N_Q_HEADS // N_KV_HEADS
ATTN_Q_WIDTH = N_Q_HEADS * HEAD_DIM
ATTN_KV_WIDTH = N_KV_HEADS * HEAD_DIM
AB_IN_WIDTH = FOURIER_WIDTH + ATTN_Q_WIDTH + 2 * ATTN_KV_WIDTH
AB_OUT_WIDTH = FOURIER_WIDTH + ATTN_Q_WIDTH
ATTN_SCALE = HEAD_DIM ** -0.5
Q_BLOCK = 128
ROPE_AXIS_DIM = HEAD_DIM // 2
ROPE_THETA = 10000.0
HGRN_WIDTH = D_MODEL
HGRN_HEAD_DIM = 128
HGRN_HEADS = HGRN_WIDTH // HGRN_HEAD_DIM
HGRN_IN_WIDTH = 5 * HGRN_WIDTH
HGRN_CHUNK = 64
N_EVEN = (DEPTH + 1) // 2
N_ODD = DEPTH // 2
EPS = 1e-6

kernel_name = 'hybrid_fourier_gqa_hgrn2_macaron'


def rms_norm(x, gain):
    xf = x.astype(jnp.float32)
    y = xf * lax.rsqrt(jnp.mean(xf * xf, axis=-1, keepdims=True) + EPS)
    return (y * gain.astype(jnp.float32)).astype(x.dtype)


def ada_pre(h, gain, shift, scale):
    return rms_norm(h, gain) * (1 + scale) + shift


def swiglu(h, w_in, w_out):
    a, b = jnp.split(h @ w_in, 2, axis=-1)
    return (jax.nn.silu(a) * b) @ w_out


def axial_rope(n_tokens):
    rows = n_tokens // GRID_W
    row_id = jnp.repeat(jnp.arange(rows, dtype=jnp.float32), GRID_W)
    col_id = jnp.tile(jnp.arange(GRID_W, dtype=jnp.float32), rows)
    inv_freq = ROPE_THETA ** (-jnp.arange(0, ROPE_AXIS_DIM, 2, dtype=jnp.float32) / ROPE_AXIS_DIM)
    ang = jnp.concatenate([row_id[:, None] * inv_freq, col_id[:, None] * inv_freq], axis=-1)
    return jnp.cos(ang), jnp.sin(ang)


def apply_axial_rope(x, cos, sin):
    B, L, H, _ = x.shape
    nf = ROPE_AXIS_DIM // 2
    xf = x.astype(jnp.float32).reshape(B, L, H, 2, 2, nf)
    c = cos.reshape(L, 1, 2, nf)
    s = sin.reshape(L, 1, 2, nf)
    x1, x2 = xf[..., 0, :], xf[..., 1, :]
    y = jnp.stack([x1 * c - x2 * s, x1 * s + x2 * c], axis=-2)
    return y.reshape(B, L, H, HEAD_DIM).astype(x.dtype)


def fourier_mix(f):
    B, n, _ = f.shape
    g = f.astype(jnp.float32).reshape(B, n, N_FOURIER_GROUPS, FOURIER_GROUP_DIM)
    y = jnp.fft.fft2(g, axes=(1, 3), norm='ortho').real
    return y.reshape(B, n, FOURIER_WIDTH).astype(f.dtype)


def gqa_attend(q, k, v):
    B, n = q.shape[:2]
    qg = q.reshape(B, n, N_KV_HEADS, Q_GROUP, HEAD_DIM)
    s = jnp.einsum('bqkgd,bskd->bkgqs', qg, k, preferred_element_type=jnp.float32) * ATTN_SCALE
    p = jax.nn.softmax(s, axis=-1).astype(v.dtype)
    o = jnp.einsum('bkgqs,bskd->bqkgd', p, v)
    return o.reshape(B, n, N_Q_HEADS * HEAD_DIM)


def split_ab(p, qk_norm):
    B, n, _ = p.shape
    f, q, k, v = jnp.split(p, [FOURIER_WIDTH, FOURIER_WIDTH + ATTN_Q_WIDTH,
                               FOURIER_WIDTH + ATTN_Q_WIDTH + ATTN_KV_WIDTH], axis=-1)
    q = rms_norm(q.reshape(B, n, N_Q_HEADS, HEAD_DIM), qk_norm[0])
    k = rms_norm(k.reshape(B, n, N_KV_HEADS, HEAD_DIM), qk_norm[1])
    v = v.reshape(B, n, N_KV_HEADS, HEAD_DIM)
    return f, q, k, v


def fourier_gqa_mixer(a_lat, a_ctx, w_in, qk_norm, w_out, need_ctx_out):
    B, L, _ = a_lat.shape
    f_lat, q_lat, k_lat, v_lat = split_ab(a_lat @ w_in, qk_norm)
    f_ctx, q_ctx, k_ctx, v_ctx = split_ab(a_ctx @ w_in, qk_norm)
    cos, sin = axial_rope(L)
    q_lat = apply_axial_rope(q_lat, cos, sin)
    k_lat = apply_axial_rope(k_lat, cos, sin)
    k_all = jnp.concatenate([k_ctx, k_lat], axis=1)
    v_all = jnp.concatenate([v_ctx, v_lat], axis=1)
    n_blocks = L // Q_BLOCK
    q_blocks = q_lat.reshape(B, n_blocks, Q_BLOCK, N_Q_HEADS, HEAD_DIM).transpose(1, 0, 2, 3, 4)
    attn_lat = lax.map(lambda qb: gqa_attend(qb, k_all, v_all), q_blocks)
    attn_lat = attn_lat.transpose(1, 0, 2, 3).reshape(B, L, ATTN_Q_WIDTH)
    y_lat = jnp.concatenate([fourier_mix(f_lat), attn_lat], axis=-1) @ w_out
    y_ctx = None
    if need_ctx_out:
        attn_ctx = gqa_attend(q_ctx, k_ctx, v_ctx)
        y_ctx = jnp.concatenate([fourier_mix(f_ctx), attn_ctx], axis=-1) @ w_out
    return y_lat, y_ctx


def chunked_gated_scan(q, k, v, log_f, s0):
    B, n, H, _ = q.shape
    nc = n // HGRN_CHUNK

    def chunks(t):
        return t.reshape(B, nc, HGRN_CHUNK, H, t.shape[-1]).transpose(1, 0, 3, 2, 4)

    causal = jnp.tril(jnp.ones((HGRN_CHUNK, HGRN_CHUNK), dtype=bool))[:, :, None]

    def step(S, xs):
        qc, kc, vc, gc = xs
        b = jnp.cumsum(gc, axis=2)
        o_inter = jnp.einsum('bhtk,bhkv->bhtv', qc * jnp.exp(b), S)
        diff = b[:, :, :, None, :] - b[:, :, None, :, :]
        decay = jnp.where(causal, jnp.exp(jnp.where(causal, diff, 0.0)), 0.0)
        a = jnp.einsum('bhtk,bhsk,bhtsk->bhts', qc, kc, decay)
        o = o_inter + jnp.einsum('bhts,bhsv->bhtv', a, vc)
        b_last = b[:, :, -1:, :]
        S_new = jnp.exp(b_last[:, :, 0, :])[..., None] * S + jnp.einsum('bhsk,bhsv->bhkv', kc * jnp.exp(b_last - b), vc)
        return S_new, o

    s_fin, o = lax.scan(step, s0, (chunks(q), chunks(k), chunks(v), chunks(log_f)))
    o = o.transpose(1, 0, 3, 2, 4).reshape(B, n, H, v.shape[-1])
    return o, s_fin


def bidirectional_scan(q, v, k_fw, lf_fw, k_bw, lf_bw, s_fw0, s_bw0):
    o_fw, s_fw = chunked_gated_scan(q, k_fw, v, lf_fw, s_fw0)
    rev = lambda t: jnp.flip(t, axis=1)
    o_bw, s_bw = chunked_gated_scan(rev(q), rev(k_bw), rev(v), rev(lf_bw), s_bw0)
    return o_fw + rev(o_bw), s_fw, s_bw


def hgrn2_project(a, w_in, lower_bound):
    B, n, _ = a.shape
    q, f_fw, f_bw, i, g = jnp.split(a @ w_in, 5, axis=-1)
    heads = lambda t: t.astype(jnp.float32).reshape(B, n, HGRN_HEADS, HGRN_HEAD_DIM)

    def forget(fl):
        fl = fl.astype(jnp.float32)
        log_f = jnp.logaddexp(jnp.log(lower_bound), jnp.log1p(-lower_bound) + jax.nn.log_sigmoid(fl))
        key = (1.0 - lower_bound) * jax.nn.sigmoid(-fl)
        return heads(key), heads(log_f)

    k_fw, lf_fw = forget(f_fw)
    k_bw, lf_bw = forget(f_bw)
    return heads(jax.nn.silu(q)), heads(i), g, k_fw, lf_fw, k_bw, lf_bw


def hgrn2_mixer(a_lat, a_ctx, w_in, lower_bound, o_norm, w_out, need_ctx_out):
    B = a_lat.shape[0]
    s_zero = jnp.zeros((B, HGRN_HEADS, HGRN_HEAD_DIM, HGRN_HEAD_DIM), jnp.float32)
    q_c, i_c, g_c, k_fc, lf_fc, k_bc, lf_bc = hgrn2_project(a_ctx, w_in, lower_bound)
    o_c, s_fw, s_bw = bidirectional_scan(q_c, i_c, k_fc, lf_fc, k_bc, lf_bc, s_zero, s_zero)
    q_l, i_l, g_l, k_fl, lf_fl, k_bl, lf_bl = hgrn2_project(a_lat, w_in, lower_bound)
    o_l, _, _ = bidirectional_scan(q_l, i_l, k_fl, lf_fl, k_bl, lf_bl, s_fw, s_bw)

    def readout(o, g):
        Bo, n = o.shape[:2]
        o = rms_norm(o, o_norm).reshape(Bo, n, HGRN_WIDTH).astype(g.dtype)
        return (o * jax.nn.sigmoid(g)) @ w_out

    y_lat = readout(o_l, g_l)
    y_ctx = readout(o_c, g_c) if need_ctx_out else None
    return y_lat, y_ctx


def setup_inputs(seed: int = 0) -> dict:
    key = jax.random.key(seed)
    ks = jax.random.split(key, 17)
    d = D_MODEL
    nrm = lambda k, shape, std: std * jax.random.normal(k, shape, jnp.float32)
    return {
        'x': nrm(ks[0], (BATCH, SEQ, d), 1.0),
        'c': nrm(ks[1], (BATCH, d), 1.0),
        'ctx': nrm(ks[2], (BATCH, CTX_LEN, d), 1.0),
        'c_ctx': nrm(ks[3], (d,), 1.0),
        'w_mod': nrm(ks[4], (DEPTH, d, N_MOD * d), 0.5 * d ** -0.5),
        'b_mod': nrm(ks[5], (DEPTH, N_MOD * d), 0.01),
        'norm_gains': 1.0 + nrm(ks[6], (DEPTH, 3, d), 0.02),
        'ffn_w_in': nrm(ks[7], (DEPTH, 2, d, 2 * FFN_DIM), d ** -0.5),
        'ffn_w_out': nrm(ks[8], (DEPTH, 2, FFN_DIM, d), FFN_DIM ** -0.5),
        'ab_w_in': nrm(ks[9], (N_EVEN, d, AB_IN_WIDTH), d ** -0.5),
        'qk_norm': 1.0 + nrm(ks[10], (N_EVEN, 2, HEAD_DIM), 0.02),
        'ab_w_out': nrm(ks[11], (N_EVEN, AB_OUT_WIDTH, d), AB_OUT_WIDTH ** -0.5),
        'hgrn_w_in': nrm(ks[12], (N_ODD, d, HGRN_IN_WIDTH), d ** -0.5),
        'hgrn_lb_logits': nrm(ks[13], (DEPTH, HGRN_WIDTH), 0.1),
        'hgrn_o_norm': 1.0 + nrm(ks[14], (N_ODD, HGRN_HEAD_DIM), 0.02),
        'hgrn_w_out': nrm(ks[15], (N_ODD, HGRN_WIDTH, d), HGRN_WIDTH ** -0.5),
        'final_norm': 1.0 + nrm(ks[16], (d,), 0.02),
    }


def reference(x, c, ctx, c_ctx, w_mod, b_mod, norm_gains, ffn_w_in, ffn_w_out, ab_w_in, qk_norm, ab_w_out,
              hgrn_w_in, hgrn_lb_logits, hgrn_o_norm, hgrn_w_out, final_norm):
    lb_cum = jnp.cumsum(jax.nn.softmax(hgrn_lb_logits.astype(jnp.float32), axis=0), axis=0)
    lower_bounds = lb_cum - lb_cum[0]
    h_lat, h_ctx = x, ctx
    for layer in range(DEPTH):
        last = layer == DEPTH - 1
        m_lat = jnp.split((jax.nn.silu(c) @ w_mod[layer] + b_mod[layer])[:, None, :], N_MOD, axis=-1)
        m_ctx = jnp.split((jax.nn.silu(c_ctx) @ w_mod[layer] + b_mod[layer])[None, None, :], N_MOD, axis=-1)
        g0, g1, g2 = norm_gains[layer, 0], norm_gains[layer, 1], norm_gains[layer, 2]
        h_lat = h_lat + 0.5 * m_lat[2] * swiglu(ada_pre(h_lat, g0, m_lat[0], m_lat[1]), ffn_w_in[layer, 0], ffn_w_out[layer, 0])
        h_ctx = h_ctx + 0.5 * m_ctx[2] * swiglu(ada_pre(h_ctx, g0, m_ctx[0], m_ctx[1]), ffn_w_in[layer, 0], ffn_w_out[layer, 0])
        a_lat = ada_pre(h_lat, g1, m_lat[3], m_lat[4])
        a_ctx = ada_pre(h_ctx, g1, m_ctx[3], m_ctx[4])
        if layer % 2 == 0:
            e = layer // 2
            y_lat, y_ctx = fourier_gqa_mixer(a_lat, a_ctx, ab_w_in[e], qk_norm[e], ab_w_out[e], not last)
        else:
            o = layer // 2
            y_lat, y_ctx = hgrn2_mixer(a_lat, a_ctx, hgrn_w_in[o], lower_bounds[layer], hgrn_o_norm[o], hgrn_w_out[o], not last)
        h_lat = h_lat + m_lat[5] * y_lat
        h_lat = h_lat + 0.5 * m_lat[8] * swiglu(ada_pre(h_lat, g2, m_lat[6], m_lat[7]), ffn_w_in[layer, 1], ffn_w_out[layer, 1])
        if not last:
            h_ctx = h_ctx + m_ctx[5] * y_ctx
            h_ctx = h_ctx + 0.5 * m_ctx[8] * swiglu(ada_pre(h_ctx, g2, m_ctx[6], m_ctx[7]), ffn_w_in[layer, 1], ffn_w_out[layer, 1])
    return rms_norm(h_lat, final_norm)
```

```python
import numpy as np
import concourse.bass as bass
import concourse.mybir as mybir

F32 = mybir.dt.float32
BF16 = mybir.dt.bfloat16
I32 = mybir.dt.int32
AF = mybir.ActivationFunctionType
ALU = mybir.AluOpType

EPOCH = 24000
DMA_RING = 8


class Buf:
    __slots__ = ("t", "name", "lw", "rd")

    def __init__(self, t, name=""):
        self.t = t
        self.name = name
        self.lw = None
        self.rd = {}

    def __getitem__(self, idx):
        return self.t[idx]


class MK:
    def __init__(self, nc, same_engine_sync=True):
        self.nc = nc
        self.E = {"pe": nc.tensor, "act": nc.scalar, "dve": nc.vector, "pool": nc.gpsimd, "sp": nc.sync}
        self.cur_sem = {}
        self.cur_cnt = {}
        self.seen = {e: {} for e in self.E}
        self.same_engine_sync = same_engine_sync
        self.ring = {}
        self.ring_i = {}
        self.nsem = 0
        self._uid = 0
        self.all_dma_events = []

    def new_sem(self, name):
        self.nsem += 1
        cm = self.nc.semaphore(f"{name}_{self.nsem}")
        return cm.__enter__()

    def sb(self, name, shape, dtype):
        self._uid += 1
        return Buf(self.nc.alloc_sbuf_tensor(f"{name}_{self._uid}", list(shape), dtype), name)

    def ps(self, name, shape, dtype=F32):
        self._uid += 1
        return Buf(self.nc.alloc_psum_tensor(f"{name}_{self._uid}", list(shape), dtype), name)

    def dram(self, name, shape, dtype, kind="Internal"):
        return Buf(self.nc.dram_tensor(name, list(shape), dtype, kind=kind), name)

    def _need(self, eng, ev):
        if ev is None:
            return
        sem, val = ev
        seen = self.seen[eng]
        k = id(sem)
        if seen.get(k, (None, 0))[1] >= val:
            return
        self.E[eng].wait_ge(sem, val)
        seen[k] = (sem, val)

    def _deps(self, eng, reads, writes, skip_self=False):
        for b in reads:
            if b.lw is not None:
                self._need(eng, b.lw)
        for b in writes:
            if b.lw is not None:
                self._need(eng, b.lw)
            for ev in b.rd.values():
                self._need(eng, ev)

    def _tick(self, eng):
        if eng not in self.cur_sem or self.cur_cnt[eng] >= EPOCH:
            self.cur_sem[eng] = self.new_sem(f"s_{eng}")
            self.cur_cnt[eng] = 0
        self.cur_cnt[eng] += 1
        return (self.cur_sem[eng], self.cur_cnt[eng])

    def _commit(self, ev, reads, writes):
        for b in writes:
            b.lw = ev
            b.rd = {}
        for b in reads:
            if b not in writes:
                b.rd[id(ev[0])] = ev

    def op(self, eng, fn, reads=(), writes=(), track=True):
        self._deps(eng, reads, writes)
        ins = fn()
        if track:
            ev = self._tick(eng)
            ins.then_inc(ev[0], 1)
            if not self.same_engine_sync:
                self.seen[eng][id(ev[0])] = ev
            self._commit(ev, reads, writes)
        return ins

    def mm_group(self, out_buf, mms, reads):
        self._deps("pe", reads, [out_buf])
        n = len(mms)
        ins = None
        for i, (o, l, r) in enumerate(mms):
            ins = self.nc.tensor.matmul(o, l, r, start=(i == 0), stop=(i == n - 1))
        ev = self._tick("pe")
        ins.then_inc(ev[0], 1)
        self.seen["pe"][id(ev[0])] = ev
        self._commit(ev, reads, [out_buf])

    def dma(self, q, out_ap, in_ap, reads=(), writes=(), **kw):
        self._deps(q, reads, writes)
        ring = self.ring.setdefault(q, [[None, 0] for _ in range(DMA_RING)])
        i = self.ring_i.get(q, 0)
        self.ring_i[q] = (i + 1) % DMA_RING
        slot = ring[i]
        if slot[0] is None or slot[1] >= EPOCH:
            if slot[0] is not None:
                self._need(q, (slot[0], slot[1]))
            slot[0] = self.new_sem(f"d_{q}{i}")
            slot[1] = 0
        elif slot[1] > 0:
            self._need(q, (slot[0], slot[1]))
        ins = self.E[q].dma_start(out=out_ap, in_=in_ap, **kw)
        slot[1] += 16
        ev = (slot[0], slot[1])
        ins.then_inc(ev[0], 16)
        self._commit(ev, reads, writes)
        self.all_dma_events.append((q, ev))
        return ev

    def finish(self, out_bufs):
        for b in out_bufs:
            if b.lw is not None:
                self._need("sp", b.lw)
        for q, ring in self.ring.items():
            for sem, cnt in ring:
                if sem is not None and cnt > 0:
                    self._need("sp", (sem, cnt))


def _mm(self, out_bufs, mms, reads, start, stop):
    self._deps("pe", reads, out_bufs)
    ins = None
    for (o, l, r) in mms:
        ins = self.nc.tensor.matmul(o, l, r, start=start, stop=stop)
    ev = self._tick("pe")
    ins.then_inc(ev[0], 1)
    self.seen["pe"][id(ev[0])] = ev
    self._commit(ev, reads, out_bufs)


def _mm_multi(self, out_bufs, mms, reads, nk):
    self._deps("pe", reads, out_bufs)
    ins = None
    for (_, ki, o, l, r) in mms:
        ins = self.nc.tensor.matmul(o, l, r, start=(ki == 0), stop=(ki == nk - 1))
    ev = self._tick("pe")
    ins.then_inc(ev[0], 1)
    self.seen["pe"][id(ev[0])] = ev
    self._commit(ev, reads, out_bufs)


MK.mm = _mm
MK.mm_multi = _mm_multi


def _mmf(self, out_bufs, mms, reads):
    self._deps("pe", reads, out_bufs)
    ins = None
    for (o, l, r, st, sp) in mms:
        ins = self.nc.tensor.matmul(o, l, r, start=st, stop=sp)
    ev = self._tick("pe")
    ins.then_inc(ev[0], 1)
    self.seen["pe"][id(ev[0])] = ev
    self._commit(ev, reads, out_bufs)


def _barrier(self):
    evs = [(self.cur_sem[e], self.cur_cnt[e]) for e in self.cur_sem]
    for q, ring in self.ring.items():
        for sem, cnt in ring:
            if sem is not None and cnt > 0:
                evs.append((sem, cnt))
    for e in self.E:
        for ev in evs:
            self._need(e, ev)


MK.mmf = _mmf
MK.barrier = _barrier

D = 2048
KC = 16
FFN = 5632
EPS = 1e-6


def subtiles_of(ncols_lat, ncols_ctx):
    st = []
    off = 0
    while off < ncols_lat:
        sz = min(512, ncols_lat - off)
        st.append((off, sz, 0))
        off += sz
    if ncols_ctx:
        st.append((off, ncols_ctx, 1))
    return st


class Pools:
    def __init__(self, mk, PW, JH):
        self.PW = PW
        self.JH = JH
        self.aT = mk.sb("aT", [128, KC, PW], BF16)
        self.gT = [mk.sb(f"gT{j}", [128, PW], BF16) for j in range(JH)]
        self.hst = [mk.sb(f"hst{i}", [128, PW], F32) for i in range(3)]
        self.sq = [mk.sb(f"sq{i}", [128, PW], BF16) for i in range(2)]
        self.R = mk.sb("R", [128, PW], F32)
        self.tmp = [mk.sb(f"tmp{i}", [128, PW], F32) for i in range(2)]
        self.win = [mk.sb(f"win{i}", [128, KC, 256], BF16) for i in range(3)]
        self.wout = [mk.sb(f"wout{i}", [128, JH, 128], BF16) for i in range(2)]
        self.ones = mk.sb("ones", [128, 128], BF16)
        self.bank = [mk.ps(f"bank{i}", [128, 512], F32) for i in range(8)]
        self.cnt = {"hst": 0, "sq": 0, "tmp": 0, "win": 0, "wout": 0}
        mk.op("dve", lambda: mk.nc.vector.memset(self.ones.t[:, :], 1.0), writes=[self.ones])

    def nxt(self, kind):
        lst = getattr(self, kind)
        i = self.cnt[kind]
        self.cnt[kind] = i + 1
        return lst[i % len(lst)]


def emit_norm(mk, P, src_ap_fn, src_bufs, col0, subt, S_ap_fn, T_ap_fn, out_fn, post_fn=None):
    nc = mk.nc
    PW = sum(s[1] for s in subt)
    nst = len(subt)
    banks = P.bank[0:nst]
    for i in range(KC):
        hs = P.nxt("hst")
        mk.dma("sp", hs.t[:, 0:PW], src_ap_fn(i), reads=[src_bufs[i]], writes=[hs])
        sq = P.nxt("sq")
        mk.op("act", lambda: nc.scalar.activation(out=sq.t[:, 0:PW], in_=hs.t[:, 0:PW], func=AF.Square),
              reads=[hs], writes=[sq])
        mms = [(banks[k].t[:, 0:sz], P.ones.t[:, :], sq.t[:, off:off + sz]) for k, (off, sz, _) in enumerate(subt)]
        mk.mm(banks, mms, reads=[sq, P.ones], start=(i == 0), stop=(i == KC - 1))
    for k, (off, sz, _) in enumerate(subt):
        mk.op("dve", lambda: nc.vector.tensor_scalar(out=P.R.t[:, off:off + sz], in0=banks[k].t[:, 0:sz],
                                                     scalar1=1.0 / D, scalar2=EPS, op0=ALU.mult, op1=ALU.add),
              reads=[banks[k]], writes=[P.R])
    mk.op("act", lambda: nc.scalar.activation(out=P.R.t[:, 0:PW], in_=P.R.t[:, 0:PW], func=AF.Sqrt),
          reads=[P.R], writes=[P.R])
    mk.op("dve", lambda: nc.vector.reciprocal(out=P.R.t[:, 0:PW], in_=P.R.t[:, 0:PW]), reads=[P.R], writes=[P.R])
    for i in range(KC):
        hs = P.nxt("hst")
        mk.dma("sp", hs.t[:, 0:PW], src_ap_fn(i), reads=[src_bufs[i]], writes=[hs])
        tm = P.nxt("tmp")
        mk.op("dve", lambda: nc.vector.tensor_tensor(out=tm.t[:, 0:PW], in0=hs.t[:, 0:PW], in1=P.R.t[:, 0:PW],
                                                     op=ALU.mult), reads=[hs, P.R], writes=[tm])
        groups = {}
        for (off, sz, mc) in subt:
            if mc in groups and groups[mc][0] + groups[mc][1] == off:
                groups[mc][1] += sz
            else:
                groups[mc] = [off, sz]
        for mc, (off, sz) in groups.items():
            o_ap, o_buf = out_fn(i, off, sz)
            Tb = T_ap_fn(i, mc) if T_ap_fn is not None else 0.0
            mk.op("act", lambda: nc.scalar.activation(out=o_ap, in_=tm.t[:, off:off + sz], func=AF.Identity,
                                                      bias=Tb, scale=S_ap_fn(i, mc)),
                  reads=[tm], writes=[o_buf])
        if post_fn is not None:
            post_fn(i)


def emit_ffn(mk, P, src_ap_fn, dst_ap_fn, hbufs_src, hbufs_dst, subt, S_fn, T_fn, G_fn,
             w_in_d, w_out_d, J, NH, wbuf):
    nc = mk.nc
    PW = sum(s[1] for s in subt)
    nst = len(subt)
    JH = J // NH
    assert nst <= 3
    emit_norm(mk, P, src_ap_fn, hbufs_src, 0, subt, S_fn, T_fn,
              lambda i, off, sz: (P.aT.t[:, i, off:off + sz], P.aT))
    setA = P.bank[0:nst]
    setB = P.bank[3:3 + nst]
    for hf in range(NH):
        for jj in range(JH):
            j = hf * JH + jj
            w = P.nxt("win")
            mk.dma("pool", w.t[:, :, :], w_in_d[j], reads=[wbuf], writes=[w])
            for part, pset in ((0, setA), (1, setB)):
                mms = []
                for kc in range(KC):
                    for k, (off, sz, _) in enumerate(subt):
                        mms.append((k, kc, pset[k].t[:, 0:sz], w.t[:, kc, part * 128:(part + 1) * 128],
                                    P.aT.t[:, kc, off:off + sz]))
                mk.mm_multi(pset, mms, reads=[w, P.aT], nk=KC)
            tm = P.nxt("tmp")
            for k, (off, sz, _) in enumerate(subt):
                mk.op("act", lambda: nc.scalar.activation(out=tm.t[:, off:off + sz], in_=setA[k].t[:, 0:sz],
                                                          func=AF.Silu), reads=[setA[k]], writes=[tm])
            for k, (off, sz, _) in enumerate(subt):
                mk.op("dve", lambda: nc.vector.tensor_tensor(out=P.gT[jj].t[:, off:off + sz], in0=tm.t[:, off:off + sz],
                                                             in1=setB[k].t[:, 0:sz], op=ALU.mult),
                      reads=[tm, setB[k]], writes=[P.gT[jj]])
        emit_outproj(mk, P, lambda jj, off, sz: P.gT[jj].t[:, off:off + sz], P.gT[0:JH], JH,
                     lambda i: w_out_d[hf, i], wbuf,
                     (src_ap_fn if hf == 0 else dst_ap_fn), (hbufs_src if hf == 0 else hbufs_dst),
                     dst_ap_fn, hbufs_dst, subt, G_fn)


def emit_outproj(mk, P, rhs_fn, rhs_bufs, nk, w_d_fn, wbuf, src_ap_fn, hbufs_src, dst_ap_fn, hbufs_dst, subt, G_fn):
    nc = mk.nc
    PW = sum(s[1] for s in subt)
    nst = len(subt)
    setA = P.bank[0:nst]
    setB = P.bank[3:3 + nst]
    for i in range(KC):
        wo = P.nxt("wout")
        mk.dma("pool", wo.t[:, 0:nk, :], w_d_fn(i), reads=[wbuf], writes=[wo])
        pset = setA if (i % 2 == 0) else setB
        mms = []
        for jj in range(nk):
            for k, (off, sz, _) in enumerate(subt):
                mms.append((k, jj, pset[k].t[:, 0:sz], wo.t[:, jj, :], rhs_fn(jj, off, sz)))
        mk.mm_multi(pset, mms, reads=[wo] + list(rhs_bufs), nk=nk)
        hs = P.nxt("hst")
        mk.dma("sp", hs.t[:, 0:PW], src_ap_fn(i), reads=[hbufs_src[i]], writes=[hs])
        for k, (off, sz, mc) in enumerate(subt):
            mk.op("dve", lambda: nc.vector.scalar_tensor_tensor(out=hs.t[:, off:off + sz], in0=pset[k].t[:, 0:sz],
                                                                scalar=G_fn(i, mc), in1=hs.t[:, off:off + sz],
                                                                op0=ALU.mult, op1=ALU.add),
                  reads=[pset[k], hs], writes=[hs])
        mk.dma("sp", dst_ap_fn(i), hs.t[:, 0:PW], reads=[hs], writes=[hbufs_dst[i]])


def emit_prep_mod(mk, mod_sb, gain_sb, S_sb, G_sb):
    nc = mk.nc
    for k in range(3):
        for mc in range(2):
            mk.op("dve", lambda: nc.vector.scalar_tensor_tensor(
                out=S_sb.t[:, k, :, mc], in0=mod_sb.t[:, (3 * k + 1) * KC:(3 * k + 2) * KC, mc], scalar=1.0,
                in1=gain_sb.t[:, k, :], op0=ALU.add, op1=ALU.mult), reads=[mod_sb, gain_sb], writes=[S_sb])
            mk.op("dve", lambda: nc.vector.tensor_scalar(
                out=G_sb.t[:, k, :, mc], in0=mod_sb.t[:, (3 * k + 2) * KC:(3 * k + 3) * KC, mc],
                scalar1=(1.0 if k == 1 else 0.5), scalar2=None, op0=ALU.mult), reads=[mod_sb], writes=[G_sb])


def emit_mod(mk, P, cT_d, w_d, b_d, out_d, nch, wbuf):
    nc = mk.nc
    c_sb = mk.sb("mod_c", [128, KC, 3], F32)
    sc_sb = mk.sb("mod_sc", [128, KC, 3], F32)
    wsl = [mk.sb(f"mod_w{i}", [128, KC, 128], F32) for i in range(3)]
    b_sb = mk.sb("mod_b", [128, nch], F32)
    o_sb = mk.sb("mod_o", [128, nch, 3], F32)
    mk.dma("sp", c_sb.t[:, :, :], cT_d, reads=[wbuf], writes=[c_sb])
    mk.dma("sp", b_sb.t[:, :], b_d, reads=[wbuf], writes=[b_sb])
    mk.op("act", lambda: nc.scalar.activation(out=sc_sb.t[:, :, :], in_=c_sb.t[:, :, :], func=AF.Silu),
          reads=[c_sb], writes=[sc_sb])
    for ch in range(nch):
        wo = wsl[ch % 3]
        mk.dma("sp", wo.t[:, 0:KC, :], w_d[ch], reads=[wbuf], writes=[wo])
        bk = P.bank[ch % 8]
        mms = [(0, kc, bk.t[:, 0:3], wo.t[:, kc, :], sc_sb.t[:, kc, :]) for kc in range(KC)]
        mk.mm_multi([bk], mms, reads=[wo, sc_sb], nk=KC)
        mk.op("act", lambda: nc.scalar.activation(out=o_sb.t[:, ch, :], in_=bk.t[:, 0:3], func=AF.Identity,
                                                  bias=b_sb.t[:, ch:ch + 1], scale=1.0),
              reads=[bk, b_sb], writes=[o_sb])
    mk.dma("sp", out_d, o_sb.t[:, :, :], reads=[o_sb], writes=[wbuf])

import ml_dtypes
from concourse.bass_utils import run_bass_kernel_spmd

NCORES = 8
NL = 2
J_FFN = FFN // 128
NH_FFN = 2
NPASS = 2


def build_tok_launch(PW, subt, n_layers, steps, out_h=True):
    nc = bass.Bass("TRN2", target_bir_lowering=False)
    mk = MK(nc)
    JH = J_FFN // NH_FFN
    P = Pools(mk, PW, JH)
    NT = NPASS * PW
    hin = nc.dram_tensor("hT", [D, NT], F32, kind="ExternalInput").ap()
    hout = nc.dram_tensor("hT_out", [D, NT], F32, kind=("ExternalOutput" if out_h else "Internal")).ap()
    modT = nc.dram_tensor("modT", [n_layers, 128, 144, 2], F32, kind="ExternalInput").ap()
    gains = nc.dram_tensor("gains", [n_layers, 128, 3, KC], F32, kind="ExternalInput").ap()
    wbuf = Buf(None, "weights")
    mod_sb, S_sb, G_sb = [], [], []
    for li in range(n_layers):
        m = mk.sb(f"mod{li}", [128, 144, 2], F32)
        g = mk.sb(f"gain{li}", [128, 3, KC], F32)
        S = mk.sb(f"S{li}", [128, 3, KC, 2], F32)
        G = mk.sb(f"G{li}", [128, 3, KC, 2], F32)
        mk.dma("sp", m.t[:, :, :], modT[li], reads=[wbuf], writes=[m])
        mk.dma("sp", g.t[:, :, :], gains[li], reads=[wbuf], writes=[g])
        emit_prep_mod(mk, m, g, S, G)
        mod_sb.append(m); S_sb.append(S); G_sb.append(G)
    hb_in = [[Buf(None, f"hi{p}_{i}") for i in range(KC)] for p in range(NPASS)]
    hb_out = [[Buf(None, f"ho{p}_{i}") for i in range(KC)] for p in range(NPASS)]
    outs = [b for p in range(NPASS) for b in hb_out[p]]
    in_place = False
    for si, st in enumerate(steps):
        kind = st[0]
        if kind == "ffn":
            _, li, f, = st
            w_in_d = nc.dram_tensor(f"w{si}_in", [J_FFN, 128, KC, 256], F32, kind="ExternalInput").ap()
            w_out_d = nc.dram_tensor(f"w{si}_out", [NH_FFN, KC, 128, JH, 128], F32, kind="ExternalInput").ap()
            k = 0 if f == 0 else 2
        elif kind == "proj":
            _, li = st
            w_p_d = nc.dram_tensor(f"w{si}_proj", [KC, 128, KC, 128], F32, kind="ExternalInput").ap()
            cat_d = nc.dram_tensor(f"cat{si}", [D, NT], BF16, kind="ExternalInput").ap()
            catb = Buf(None, "cat")
        elif kind == "normout":
            _, li = st
            aT_d = nc.dram_tensor("aT_out", [D, NT], BF16, kind="ExternalOutput").ap()
            aTb = Buf(None, "aT_out")
            outs.append(aTb)
        elif kind == "final":
            fng_d = nc.dram_tensor("fng", [128, KC], F32, kind="ExternalInput").ap()
            y_d = nc.dram_tensor("yT", [D, NT], F32, kind="ExternalOutput").ap()
            fng = mk.sb("fng", [128, KC], F32)
            mk.dma("sp", fng.t[:, :], fng_d, reads=[wbuf], writes=[fng])
            yb = Buf(None, "yT")
            outs.append(yb)
        for p in range(NPASS):
            c0 = p * PW
            src = (hout if in_place else hin)
            sb_ = (hb_out if in_place else hb_in)[p]
            src_fn = (lambda i, src=src, c0=c0: src[i * 128:(i + 1) * 128, c0:c0 + PW])
            dst_fn = (lambda i, c0=c0: hout[i * 128:(i + 1) * 128, c0:c0 + PW])
            if kind == "ffn":
                emit_ffn(mk, P, src_fn, dst_fn, sb_, hb_out[p], subt,
                         lambda i, mc: S_sb[li].t[:, k, i, mc:mc + 1],
                         lambda i, mc: mod_sb[li].t[:, 3 * k * KC + i, mc:mc + 1],
                         lambda i, mc: G_sb[li].t[:, k, i, mc:mc + 1],
                         w_in_d, w_out_d, J_FFN, NH_FFN, wbuf)
            elif kind == "proj":
                mk.dma("sp", P.aT.t[:, :, 0:PW], cat_d[:, c0:c0 + PW].rearrange("(kc p) t -> p kc t", p=128),
                       reads=[catb], writes=[P.aT])
                emit_outproj(mk, P, lambda jj, off, sz: P.aT.t[:, jj, off:off + sz], [P.aT], KC,
                             lambda i: w_p_d[i], wbuf, src_fn, sb_, dst_fn, hb_out[p], subt,
                             lambda i, mc: G_sb[li].t[:, 1, i, mc:mc + 1])
            elif kind == "normout":
                emit_norm(mk, P, src_fn, sb_, 0, subt,
                          lambda i, mc: S_sb[li].t[:, 1, i, mc:mc + 1],
                          lambda i, mc: mod_sb[li].t[:, 3 * KC + i, mc:mc + 1],
                          lambda i, off, sz: (P.aT.t[:, i, off:off + sz], P.aT))
                mk.dma("sp", aT_d[:, c0:c0 + PW].rearrange("(kc p) t -> p kc t", p=128), P.aT.t[:, :, 0:PW],
                       reads=[P.aT], writes=[aTb])
            elif kind == "final":
                cur = {}

                def out_fn(i, off, sz):
                    if "t" not in cur or cur["i"] != i:
                        cur["t"] = P.nxt("hst"); cur["i"] = i
                    return cur["t"].t[:, off:off + sz], cur["t"]

                def post_fn(i, c0=c0):
                    mk.dma("sp", y_d[i * 128:(i + 1) * 128, c0:c0 + PW], cur["t"].t[:, 0:PW], reads=[cur["t"]], writes=[yb])
                emit_norm(mk, P, src_fn, sb_, 0, subt, lambda i, mc: fng.t[:, i:i + 1], None, out_fn, post_fn)
        if kind in ("ffn", "proj"):
            in_place = True
    mk.finish(outs)
    return nc


def build_mod_launch(nch):
    nc = bass.Bass("TRN2", target_bir_lowering=False)
    mk = MK(nc)
    P = Pools(mk, 64, KC)
    cT = nc.dram_tensor("cT", [128, KC, 3], F32, kind="ExternalInput").ap()
    w = nc.dram_tensor("w", [nch, 128, KC, 128], F32, kind="ExternalInput").ap()
    b = nc.dram_tensor("b", [128, nch], F32, kind="ExternalInput").ap()
    o = nc.dram_tensor("o", [128, nch, 3], F32, kind="ExternalOutput").ap()
    wbuf = Buf(None, "io")
    emit_mod(mk, P, cT, w, b, o, nch, wbuf)
    mk.finish([wbuf])
    return nc


def lay_w_in(w):
    F_ = w.shape[1] // 2
    Jn = F_ // 128
    return np.ascontiguousarray(w.reshape(KC, 128, 2, Jn, 128).transpose(3, 1, 0, 2, 4).reshape(Jn, 128, KC, 256))


def lay_w_out(w, NH):
    F_ = w.shape[0]
    JH = F_ // 128 // NH
    return np.ascontiguousarray(w.reshape(NH, JH, 128, KC, 128).transpose(0, 3, 2, 1, 4))


def lay_vec(v):
    n = v.shape[-1] // 128
    r = v.reshape(v.shape[:-1] + (n, 128))
    return np.ascontiguousarray(np.moveaxis(r, -1, 0))


def run_mod(c, c_ctx, w_mod, b_mod):
    ncols = 9 * D
    nch_total = NL * ncols // 128
    nch = nch_total // NCORES
    cT = lay_vec(np.stack([c[0], c[1], c_ctx], 0))
    cT = np.ascontiguousarray(cT.transpose(0, 2, 1))
    wl = w_mod.reshape(NL, KC, 128, ncols // 128, 128).transpose(0, 3, 2, 1, 4).reshape(nch_total, 128, KC, 128)
    bl = b_mod.reshape(nch_total, 128).T
    nc = build_mod_launch(nch)
    in_maps = []
    for cidx in range(NCORES):
        sl = slice(cidx * nch, (cidx + 1) * nch)
        in_maps.append({"cT": cT, "w": np.ascontiguousarray(wl[sl]), "b": np.ascontiguousarray(bl[:, sl])})
    res = run_bass_kernel_spmd(nc, in_maps, core_ids=list(range(NCORES)))
    o = np.concatenate([r["o"] for r in res.results], axis=1)
    mod = o.transpose(2, 1, 0).reshape(3, NL, ncols).transpose(1, 0, 2)
    return np.ascontiguousarray(mod)


def modT_for_core(mod, layers, b):
    out = []
    for l in layers:
        m = mod[l][[b, 2]]
        out.append(m.reshape(2, 144, 128).transpose(2, 1, 0))
    return np.ascontiguousarray(np.stack(out, 0))


def gains_for(norm_gains, layers):
    return np.ascontiguousarray(np.stack([lay_vec(norm_gains[l]) for l in layers], 0))


PWL = 1056
SUBT_L = subtiles_of(1024, 32)


def pack_tokens(lat, ctxa, dtype=None):
    outs = []
    for cidx in range(NCORES):
        b, j = divmod(cidx, 4)
        cols = []
        for p in range(NPASS):
            cols.append(lat[b, j * 2048 + p * 1024: j * 2048 + (p + 1) * 1024])
            cols.append(ctxa[b, j * 64 + p * 32: j * 64 + (p + 1) * 32])
        outs.append(np.ascontiguousarray(np.concatenate(cols, 0).T))
    return outs


def unpack_tokens(per_core, C):
    dt = per_core[0].dtype
    lat = np.empty((2, 8192, C), dt)
    ctxa = np.empty((2, 256, C), dt)
    for cidx in range(NCORES):
        b, j = divmod(cidx, 4)
        a = per_core[cidx].T
        for p in range(NPASS):
            base = p * PWL
            lat[b, j * 2048 + p * 1024: j * 2048 + (p + 1) * 1024] = a[base:base + 1024]
            ctxa[b, j * 64 + p * 32: j * 64 + (p + 1) * 32] = a[base + 1024:base + 1056]
    return lat, ctxa

NLAT = 8192
NCTX = 256
NTOK = NLAT + NCTX
HD = 128
ATTN_SCALE = HD ** -0.5


def emit_qknorm_rope(mk, W, ps, gain_ap, dst_ap, dst_buf, T, cs_aps=None):
    nc = mk.nc
    mk.op("act", lambda: nc.scalar.activation(out=T["sq"].t[:, 0:W], in_=ps.t[:, 0:W], func=AF.Square),
          reads=[ps], writes=[T["sq"]])
    mk.mm([T["ssq"]], [(T["ssq"].t[:, 0:W], T["ones"].t[:, :], T["sq"].t[:, 0:W])], reads=[T["sq"], T["ones"]],
          start=True, stop=True)
    mk.op("dve", lambda: nc.vector.tensor_scalar(out=T["r"].t[:, 0:W], in0=T["ssq"].t[:, 0:W], scalar1=1.0 / HD,
                                                 scalar2=EPS, op0=ALU.mult, op1=ALU.add), reads=[T["ssq"]], writes=[T["r"]])
    mk.op("act", lambda: nc.scalar.activation(out=T["r"].t[:, 0:W], in_=T["r"].t[:, 0:W], func=AF.Sqrt),
          reads=[T["r"]], writes=[T["r"]])
    mk.op("dve", lambda: nc.vector.reciprocal(out=T["r"].t[:, 0:W], in_=T["r"].t[:, 0:W]), reads=[T["r"]], writes=[T["r"]])
    if cs_aps is None:
        mk.op("dve", lambda: nc.vector.scalar_tensor_tensor(out=dst_ap, in0=ps.t[:, 0:W], scalar=gain_ap, in1=T["r"].t[:, 0:W],
                                                            op0=ALU.mult, op1=ALU.mult), reads=[ps, T["r"]], writes=[dst_buf])
        return
    C_ap, S_ap, csbuf = cs_aps
    mk.op("dve", lambda: nc.vector.scalar_tensor_tensor(out=T["qn"].t[:, 0:W], in0=ps.t[:, 0:W], scalar=gain_ap,
                                                        in1=T["r"].t[:, 0:W], op0=ALU.mult, op1=ALU.mult),
          reads=[ps, T["r"]], writes=[T["qn"]])
    mk.op("act", lambda: nc.scalar.copy(out=T["qnb"].t[:, 0:W], in_=T["qn"].t[:, 0:W]), reads=[T["qn"]], writes=[T["qnb"]])
    mk.mm([T["rot"]], [(T["rot"].t[:, 0:W], T["rmat"].t[:, 0, :], T["qnb"].t[:, 0:W])], reads=[T["qnb"], T["rmat"]],
          start=True, stop=True)
    mk.op("dve", lambda: nc.vector.tensor_tensor(out=T["t1"].t[:, 0:W], in0=T["qn"].t[:, 0:W], in1=C_ap, op=ALU.mult),
          reads=[T["qn"], csbuf], writes=[T["t1"]])
    mk.op("dve", lambda: nc.vector.tensor_tensor(out=T["t2"].t[:, 0:W], in0=T["rot"].t[:, 0:W], in1=S_ap, op=ALU.mult),
          reads=[T["rot"], csbuf], writes=[T["t2"]])
    mk.op("pool", lambda: nc.gpsimd.tensor_tensor(out=dst_ap, in0=T["t1"].t[:, 0:W], in1=T["t2"].t[:, 0:W], op=ALU.add),
          reads=[T["t1"], T["t2"]], writes=[dst_buf])


def build_mixa0_launch():
    nc = bass.Bass("TRN2", target_bir_lowering=False)
    mk = MK(nc)
    aT_d = nc.dram_tensor("aT", [D, NTOK], BF16, kind="ExternalInput").ap()
    wsel_d = nc.dram_tensor("wsel", [128, KC, 768], F32, kind="ExternalInput").ap()
    gqk_d = nc.dram_tensor("gqk", [128, 2], F32, kind="ExternalInput").ap()
    cos_d = nc.dram_tensor("cosT", [128, NLAT], F32, kind="ExternalInput").ap()
    sin_d = nc.dram_tensor("sinT", [128, NLAT], F32, kind="ExternalInput").ap()
    cmat_d = nc.dram_tensor("cmat", [128, 4, 128], BF16, kind="ExternalInput").ap()
    dftL_d = nc.dram_tensor("dftL", [16, 8, 2, 128, 8, 512], BF16, kind="ExternalInput").ap()
    dftC_d = nc.dram_tensor("dftC", [2, 128, 2, 256], BF16, kind="ExternalInput").ap()
    cat_d = nc.dram_tensor("cat", [512, NTOK], BF16, kind="ExternalOutput").ap()
    io = Buf(None, "io")
    catb = Buf(None, "cat")
    KT = mk.sb("KT", [128, NTOK], BF16)
    V = mk.sb("V", [128, 66, 128], BF16)
    UW = mk.sb("UW", [128, 66, 256], BF16)
    wsel = mk.sb("wsel", [128, KC, 768], BF16)
    cmat = mk.sb("cmat", [128, 4, 128], BF16)
    gqk = mk.sb("gqk", [128, 2], F32)
    gq = mk.sb("gq", [128, 1], F32)
    QT = mk.sb("QT", [128, NLAT], BF16)
    QC = mk.sb("QC", [128, NCTX], BF16)
    aslot = [mk.sb(f"aslot{i}", [128, KC, 512], BF16) for i in range(2)]
    cslot = [mk.sb(f"cslot{i}", [128, 2, 512], F32) for i in range(2)]
    T = {n: mk.sb(n, [128, 512], F32) for n in ("r", "qn", "t1", "t2")}
    T["sq"] = mk.sb("sq", [128, 512], BF16)
    T["qnb"] = mk.sb("qnb", [128, 512], BF16)
    T["ones"] = mk.sb("ones", [128, 128], BF16)
    fT = mk.sb("fT", [128, 512], BF16)
    PT = [mk.sb(f"PT{i}", [128, 512], BF16) for i in range(4)]
    rden = mk.sb("rden", [128, 512], F32)
    otile = [mk.sb(f"otile{i}", [128, 512], BF16) for i in range(2)]
    dslot = [mk.sb(f"dslot{i}", [128, 16, 512], BF16) for i in range(2)]
    dtile = mk.sb("dtile", [128, 512], BF16)
    print("mixa0 sbuf left", nc.sbuf_bytes_remaining)
    bank = [mk.ps(f"bank{i}", [128, 512], F32) for i in range(8)]
    T["ssq"] = bank[6]
    T["rot"] = bank[7]
    mk.op("dve", lambda: nc.vector.memset(T["ones"].t[:, :], 1.0), writes=[T["ones"]])
    mk.dma("pool", wsel.t[:, :, :], wsel_d, reads=[io], writes=[wsel])
    mk.dma("sp", cmat.t[:, :, :], cmat_d, reads=[io], writes=[cmat])
    mk.dma("sp", gqk.t[:, :], gqk_d, reads=[io], writes=[gqk])
    mk.op("dve", lambda: nc.vector.tensor_scalar(out=gq.t[:, :], in0=gqk.t[:, 0:1], scalar1=ATTN_SCALE, scalar2=None,
                                                 op0=ALU.mult), reads=[gqk], writes=[gq])
    T["rmat"] = cmat
    tiles = [(t * 512, 512, True) for t in range(16)] + [(NLAT, NCTX, False)]
    cnt = {"a": 0, "c": 0}

    def load_tile(c0, W, lat):
        a = aslot[cnt["a"] % 2]; cnt["a"] += 1
        mk.dma("sp", a.t[:, :, 0:W], aT_d[:, c0:c0 + W].rearrange("(kc p) t -> p kc t", p=128), reads=[io], writes=[a])
        cs = None
        if lat:
            c = cslot[cnt["c"] % 2]; cnt["c"] += 1
            mk.dma("sp", c.t[:, 0, 0:W], cos_d[:, c0:c0 + W], reads=[io], writes=[c])
            mk.dma("sp", c.t[:, 1, 0:W], sin_d[:, c0:c0 + W], reads=[io], writes=[c])
            cs = (c.t[:, 0, 0:W], c.t[:, 1, 0:W], c)
        return a, cs

    def proj(a, W, col0, pb):
        mms = [(0, kc, pb.t[:, 0:W], wsel.t[:, kc, col0:col0 + 128], a.t[:, kc, 0:W]) for kc in range(KC)]
        mk.mm_multi([pb], mms, reads=[wsel, a], nk=KC)

    for ti, (c0, W, lat) in enumerate(tiles):
        a, cs = load_tile(c0, W, lat)
        nsb = W // 128
        ch0 = c0 // 128
        pk = bank[ti % 2]
        proj(a, W, 512, pk)
        emit_qknorm_rope(mk, W, pk, gqk.t[:, 1:2], KT.t[:, c0:c0 + W], KT, T, cs)
        pv = bank[2]
        mms = []
        for sb_ in range(nsb):
            for kc in range(KC):
                mms.append((0, kc, pv.t[:, sb_ * 128:(sb_ + 1) * 128], a.t[:, kc, sb_ * 128:(sb_ + 1) * 128], wsel.t[:, kc, 640:768]))
        mk.mm_multi([pv], mms, reads=[wsel, a], nk=KC)
        mk.op("act", lambda: nc.scalar.copy(out=V.t[:, ch0:ch0 + nsb, :], in_=pv.t[:, 0:W].rearrange("p (s c) -> p s c", c=128)),
              reads=[pv], writes=[V])
        pf = bank[3]
        proj(a, W, 0, pf)
        mk.op("act", lambda: nc.scalar.copy(out=fT.t[:, 0:W], in_=pf.t[:, 0:W]), reads=[pf], writes=[fT])
        for half in range((nsb + 1) // 2):
            pu = bank[4 + half]
            nn = min(2, nsb - half * 2)
            for s2 in range(nn):
                sb_ = half * 2 + s2
                mk.mm([pu], [(pu.t[:, s2 * 256:(s2 + 1) * 256], fT.t[:, sb_ * 128:(sb_ + 1) * 128],
                              cmat.t[:, 1:3, :].rearrange("p a c -> p (a c)"))], reads=[fT, cmat], start=True, stop=True)
            mk.op("dve", lambda: nc.vector.tensor_copy(out=UW.t[:, ch0 + half * 2:ch0 + half * 2 + nn, :],
                                                       in_=pu.t[:, 0:nn * 256].rearrange("p (s c) -> p s c", c=256)),
                  reads=[pu], writes=[UW])
    T["ssq"] = bank[2]
    T["rot"] = bank[3]
    py = bank[7]

    def dft_units():
        units = [(kt, ng) for kt in range(16) for ng in range(8)]

        def load(u):
            kt, ng = units[u]
            ds_ = dslot[u % 2]
            for cs_ in range(2):
                mk.dma("sp", ds_.t[:, cs_ * 8:(cs_ + 1) * 8, :], dftL_d[kt, ng, cs_], reads=[io], writes=[ds_])
        load(0)
        load(1)
        yield
        for u, (kt, ng) in enumerate(units):
            ds_ = dslot[u % 2]
            mms = []
            for n8 in range(8):
                nch = ng * 8 + n8
                mms.append((py.t[:, :], UW.t[:, nch, 0:128], ds_.t[:, n8, :], (nch == 0), False))
                mms.append((py.t[:, :], UW.t[:, nch, 128:256], ds_.t[:, 8 + n8, :], False, (nch == 63)))
            mk.mmf([py], mms, reads=[UW, ds_])
            if u + 2 < len(units):
                load(u + 2)
            if ng == 7:
                mk.op("act", lambda: nc.scalar.copy(out=dtile.t[:, :], in_=py.t[:, :]), reads=[py], writes=[dtile])
                mk.dma("sp", cat_d[0:128, kt * 512:(kt + 1) * 512], dtile.t[:, :], reads=[dtile], writes=[catb])
            yield
    dgen = dft_units()
    next(dgen)

    def dft_step(n):
        for _ in range(n):
            try:
                next(dgen)
            except StopIteration:
                return
    for h in range(3):
        for ti, (c0, W, lat) in enumerate(tiles):
            a, cs = load_tile(c0, W, lat)
            pq = bank[ti % 2]
            proj(a, W, 128 + h * 128, pq)
            if lat:
                emit_qknorm_rope(mk, W, pq, gq.t[:, 0:1], QT.t[:, c0:c0 + W], QT, T, cs)
            else:
                emit_qknorm_rope(mk, W, pq, gq.t[:, 0:1], QC.t[:, 0:W], QC, T, None)
        qtiles = [(QT, t * 512, 512, t * 512, list(range(66))) for t in range(16)] + [(QC, 0, NCTX, NLAT, [64, 65])]
        for qi, (qbuf, q0, W, o0, kcs) in enumerate(qtiles):
            O = bank[3 + (qi % 2)]
            Dn = bank[5 + (qi % 2)]
            nk = len(kcs)

            def s_mm(idx):
                kc = kcs[idx]
                sb_ = bank[idx % 3]
                mk.mm([sb_], [(sb_.t[:, 0:W], KT.t[:, kc * 128:(kc + 1) * 128], qbuf.t[:, q0:q0 + W])], reads=[KT, qbuf],
                      start=True, stop=True)

            def e_act(idx):
                sb_ = bank[idx % 3]
                pt = PT[idx % 4]
                mk.op("act", lambda: nc.scalar.activation(out=pt.t[:, 0:W], in_=sb_.t[:, 0:W], func=AF.Exp),
                      reads=[sb_], writes=[pt])

            s_mm(0)
            if nk > 1:
                s_mm(1)
            e_act(0)
            for idx in range(nk):
                if nk > 2 and idx in (nk // 4, nk // 2, (3 * nk) // 4):
                    dft_step(1)
                if idx + 2 < nk:
                    s_mm(idx + 2)
                if idx + 1 < nk:
                    e_act(idx + 1)
                pt = PT[idx % 4]
                kc = kcs[idx]
                mk.mm([O, Dn], [(O.t[:, 0:W], V.t[:, kc, :], pt.t[:, 0:W]), (Dn.t[:, 0:W], T["ones"].t[:, :], pt.t[:, 0:W])],
                      reads=[pt, V, T["ones"]], start=(idx == 0), stop=(idx == nk - 1))
            mk.op("dve", lambda: nc.vector.reciprocal(out=rden.t[:, 0:W], in_=Dn.t[:, 0:W]), reads=[Dn], writes=[rden])
            ot = otile[qi % 2]
            mk.op("dve", lambda: nc.vector.tensor_tensor(out=ot.t[:, 0:W], in0=O.t[:, 0:W], in1=rden.t[:, 0:W], op=ALU.mult),
                  reads=[O, rden], writes=[ot])
            mk.dma("sp", cat_d[128 + h * 128:256 + h * 128, o0:o0 + W], ot.t[:, 0:W], reads=[ot], writes=[catb])
    dft_step(1000)
    dcnt = 0
    ds_ = dslot[0]
    for cs_ in range(2):
        mk.dma("sp", ds_.t[:, cs_ * 8:cs_ * 8 + 2, 0:256], dftC_d[cs_], reads=[io], writes=[ds_])
    py = bank[2]
    mms = []
    for n2 in range(2):
        mms.append((0, 2 * n2, py.t[:, 0:256], UW.t[:, 64 + n2, 0:128], ds_.t[:, n2, 0:256]))
        mms.append((0, 2 * n2 + 1, py.t[:, 0:256], UW.t[:, 64 + n2, 128:256], ds_.t[:, 8 + n2, 0:256]))
    mk.mm_multi([py], mms, reads=[UW, ds_], nk=4)
    ot = otile[0]
    mk.op("act", lambda: nc.scalar.copy(out=ot.t[:, 0:256], in_=py.t[:, 0:256]), reads=[py], writes=[ot])
    mk.dma("sp", cat_d[0:128, NLAT:NTOK], ot.t[:, 0:256], reads=[ot], writes=[catb])
    mk.finish([catb])
    return nc


_CONST = {}


def mixa0_consts():
    if "m0" in _CONST:
        return _CONST["m0"]
    bf = ml_dtypes.bfloat16
    rows = NLAT // 64
    row_id = np.repeat(np.arange(rows, dtype=np.float32), 64)
    col_id = np.tile(np.arange(64, dtype=np.float32), rows)
    inv_freq = (np.float32(10000.0) ** (-np.arange(0, 64, 2, dtype=np.float32) / np.float32(64))).astype(np.float32)
    ang = np.concatenate([row_id[:, None] * inv_freq, col_id[:, None] * inv_freq], axis=-1).astype(np.float32)
    cosL, sinL = np.cos(ang).astype(np.float32), np.sin(ang).astype(np.float32)
    pidx = np.arange(128)
    fidx = (pidx // 64) * 32 + (pidx % 32)
    cosT = np.ascontiguousarray(cosL[:, fidx].T)
    sinT = np.ascontiguousarray(sinL[:, fidx].T)
    rm = np.zeros((128, 128), np.float32)
    for p in range(128):
        if (p % 64) < 32:
            rm[p + 32, p] = -1.0
        else:
            rm[p - 32, p] = 1.0
    cc = np.arange(128)
    angc = 2 * np.pi * ((cc[:, None] * cc[None, :]) % 128) / 128.0
    Cc = np.cos(angc) / np.sqrt(128.0)
    Sc = np.sin(angc) / np.sqrt(128.0)
    cmat = np.stack([rm, Cc, Sc, np.zeros((128, 128))], 1).astype(bf)

    def dft(N):
        n = np.arange(N, dtype=np.int64)
        m = (n[:, None] * n[None, :]) % N
        a = m.astype(np.float32) * np.float32(2 * np.pi / N)
        s = np.float32(1.0 / np.sqrt(N))
        return (np.cos(a) * s).astype(bf), (-np.sin(a) * s).astype(bf)
    CL, SL = dft(NLAT)
    def layL(M):
        return M.reshape(8, 8, 128, 16, 512).transpose(3, 0, 2, 1, 4)
    dftL = np.ascontiguousarray(np.stack([layL(CL), layL(SL)], 2))
    Cx, Sx = dft(NCTX)
    layC = lambda M: M.reshape(2, 128, 256).transpose(1, 0, 2)
    dftC = np.ascontiguousarray(np.stack([layC(Cx), layC(Sx)], 0))
    _CONST["m0"] = dict(cosT=cosT, sinT=sinT, cmat=np.ascontiguousarray(cmat), dftL=dftL, dftC=dftC)
    return _CONST["m0"]


def mixa0_wsel(w_in, j):
    cols = np.concatenate([np.arange(j * 128, (j + 1) * 128),
                           512 + np.arange(3 * j * 128, (3 * j + 3) * 128),
                           2048 + np.arange(j * 128, (j + 1) * 128),
                           2560 + np.arange(j * 128, (j + 1) * 128)])
    w = w_in[:, cols]
    return np.ascontiguousarray(w.reshape(KC, 128, 768).transpose(1, 0, 2))


def run_mixa0(a_lat, a_ctx, ab_w_in, qk_norm, trace=False):
    C = mixa0_consts()
    nc = build_mixa0_launch()
    gqk = np.ascontiguousarray(qk_norm.T.astype(np.float32))
    aT = [np.ascontiguousarray(np.concatenate([a_lat[b], a_ctx[b]], 0).T) for b in range(2)]
    in_maps = []
    for cidx in range(NCORES):
        b, j = divmod(cidx, 4)
        in_maps.append(dict(aT=aT[b], wsel=mixa0_wsel(ab_w_in, j), gqk=gqk, cosT=C["cosT"], sinT=C["sinT"],
                            cmat=C["cmat"], dftL=C["dftL"], dftC=C["dftC"]))
    res = run_bass_kernel_spmd(nc, in_maps, core_ids=list(range(NCORES)), trace=trace)
    cat_lat = np.empty((2, NLAT, 2048), a_lat.dtype)
    cat_ctx = np.empty((2, NCTX, 2048), a_lat.dtype)
    for cidx in range(NCORES):
        b, j = divmod(cidx, 4)
        c = res.results[cidx]["cat"].T
        for dst, rows in ((cat_lat, slice(0, NLAT)), (cat_ctx, slice(NLAT, NTOK))):
            dst[b][:, j * 128:(j + 1) * 128] = c[rows, 0:128]
            dst[b][:, 512 + 3 * j * 128: 512 + (3 * j + 3) * 128] = c[rows, 128:512]
    return cat_lat, cat_ctx, res

CH = 64
NHL = 4


def build_mixa1_launch(NLAT=8192):
    NTOK = NLAT + NCTX
    nc = bass.Bass("TRN2", target_bir_lowering=False)
    mk = MK(nc)
    aT_d = nc.dram_tensor("aT", [D, NTOK], BF16, kind="ExternalInput").ap()
    w_d = nc.dram_tensor("w5", [5, 128, KC, 512], F32, kind="ExternalInput").ap()
    lbl_d = nc.dram_tensor("lbl", [128, 2, NHL], F32, kind="ExternalInput").ap()
    gon_d = nc.dram_tensor("gon", [128, 1], F32, kind="ExternalInput").ap()
    cm_d = nc.dram_tensor("cm", [128, 4, 128], BF16, kind="ExternalInput").ap()
    sm_d = nc.dram_tensor("sm", [128, 512], F32, kind="ExternalInput").ap()
    out_d = nc.dram_tensor("og", [NHL * 128, NLAT], BF16, kind="ExternalOutput").ap()
    ofw_d = nc.dram_tensor("ofw", [NHL, 128, NLAT], BF16, kind="Internal").ap()
    io = Buf(None, "io"); outb = Buf(None, "out"); ofwb = Buf(None, "ofw")
    W = mk.sb("W", [128, 4, KC, 512], BF16)
    aslot = [mk.sb(f"aslot{i}", [128, KC, 512], BF16) for i in range(2)]
    cm = mk.sb("cm", [128, 4, 128], BF16)
    sm = mk.sb("sm", [128, 512], F32)
    lbl = mk.sb("lbl", [128, 2, NHL], F32)
    lb = mk.sb("lb", [128, NHL], F32)
    oml = mk.sb("oml", [128, NHL], F32)
    gon = mk.sb("gon", [128, 1], F32)
    ones = mk.sb("ones", [128, 128], BF16)
    Vt = [mk.sb(f"Vt{i}", [128, 512], BF16) for i in range(8)]
    S32 = [mk.sb(f"S32_{h}", [128, 128], F32) for h in range(NHL)]
    Sbf = [[mk.sb(f"Sbf_{h}_{i}", [128, 128], BF16) for i in range(6)] for h in range(NHL)]
    F = {n: [mk.sb(f"{n}{i}", [128, 512], F32) for i in range(2)] for n in ("sg", "fg", "lf", "b", "eb", "enb", "kk", "qs")}
    QdT = [mk.sb(f"QdT{i}", [128, 512], BF16) for i in range(2)]
    KdT = [mk.sb(f"KdT{i}", [128, 512], BF16) for i in range(2)]
    Kdtok = [mk.sb(f"Kdtok{i}", [128, 128], BF16) for i in range(2)]
    ATm = [mk.sb(f"ATm{i}", [128, 128], BF16) for i in range(2)]
    tmpS = [mk.sb(f"tmpS{i}", [128, 128], F32) for i in range(2)]
    ofs = [mk.sb(f"ofs{i}", [128, 512], BF16) for i in range(2)]
    o32 = mk.sb("o32", [128, 512], F32)
    sq = mk.sb("sq", [128, 512], BF16)
    r32 = mk.sb("r32", [128, 512], F32)
    on32 = mk.sb("on32", [128, 512], F32)
    sgg = mk.sb("sgg", [128, 512], F32)
    otile = [mk.sb(f"otile{i}", [128, 512], BF16) for i in range(2)]
    cnt = {}

    def rot(lst, key):
        i = cnt.get(key, 0); cnt[key] = i + 1
        return lst[i % len(lst)]
    bank = [mk.ps(f"bank{i}", [128, 512], F32) for i in range(7)]
    bankT = mk.ps("bankT", [128, 1024], BF16)
    pO = bank[0]; pP = [bank[1], bank[2]]; pV = bank[3]; pSSQ = bank[3]
    rA = [(bank[4], bank[4].t[:, 0:128]), (bank[5], bank[5].t[:, 0:128])]
    rS = [(bank[6], bank[6].t[:, 0:128]), (bank[6], bank[6].t[:, 128:256])]
    rT = [(bankT, bankT.t[:, 0:128])]
    mk.op("dve", lambda: nc.vector.memset(ones.t[:, :], 1.0), writes=[ones])
    mk.dma("sp", cm.t[:, :, :], cm_d, reads=[io], writes=[cm])
    mk.dma("sp", sm.t[:, :], sm_d, reads=[io], writes=[sm])
    mk.dma("sp", lbl.t[:, :, :], lbl_d, reads=[io], writes=[lbl])
    mk.dma("sp", gon.t[:, :], gon_d, reads=[io], writes=[gon])
    mk.op("dve", lambda: nc.vector.tensor_tensor(out=lb.t[:, :], in0=lbl.t[:, 1, :], in1=lbl.t[:, 0, :], op=ALU.subtract),
          reads=[lbl], writes=[lb])
    mk.op("act", lambda: nc.scalar.activation(out=lb.t[:, :], in_=lb.t[:, :], func=AF.Sigmoid), reads=[lb], writes=[lb])
    mk.op("dve", lambda: nc.vector.tensor_scalar(out=oml.t[:, :], in0=lb.t[:, :], scalar1=-1.0, scalar2=1.0, op0=ALU.mult,
                                                 op1=ALU.add), reads=[lb], writes=[oml])
    IDENT = cm.t[:, 2, :]

    for direction in (0, 1):
        fw = direction == 0
        secs = [0, 1, 3] if fw else [0, 2, 3, 4]
        for si, s in enumerate(secs):
            mk.dma("pool", W.t[:, si, :, :], w_d[s], reads=[io], writes=[W])
        for h in range(NHL):
            mk.op("dve", lambda: nc.vector.memset(S32[h].t[:, :], 0.0), writes=[S32[h]])
            mk.op("pool", lambda: nc.gpsimd.memset(Sbf[h][0].t[:, :], 0.0), writes=[Sbf[h][0]])
        sidx = [0] * NHL
        ctx_tiles = [(NLAT, NCTX, False)]
        lat_tiles = [(t * 512, 512, True) for t in range(NLAT // 512)]
        tiles = (ctx_tiles + lat_tiles) if fw else (ctx_tiles + lat_tiles[::-1])
        MASK = cm.t[:, 0, :] if fw else cm.t[:, 1, :]
        pend = []

        def flush():
            for f_ in pend:
                f_()
            pend.clear()
        for (c0, Wd, lat) in tiles:
            nblk = Wd // 128
            a = rot(aslot, "a")
            mk.dma("sp", a.t[:, :, 0:Wd], aT_d[:, c0:c0 + Wd].rearrange("(kc p) t -> p kc t", p=128), reads=[io], writes=[a])
            vts = []
            for blk in range(nblk):
                mms = [(pV.t[:, :], a.t[:, kc, blk * 128:(blk + 1) * 128], W.t[:, 2, kc, :], kc == 0, kc == KC - 1) for kc in range(KC)]
                mk.mmf([pV], mms, reads=[a, W])
                vt = rot(Vt, "vt")
                mk.op("act", lambda: nc.scalar.copy(out=vt.t[:, :], in_=pV.t[:, :]), reads=[pV], writes=[vt])
                vts.append(vt)
            for h in range(NHL):
                pq = pP[0]; pf = pP[1]
                for (pb, si) in ((pq, 0), (pf, 1)):
                    mms = [(pb.t[:, 0:Wd], W.t[:, si, kc, h * 128:(h + 1) * 128], a.t[:, kc, 0:Wd], kc == 0, kc == KC - 1) for kc in range(KC)]
                    mk.mmf([pb], mms, reads=[a, W])
                t = {n: rot(F[n], n) for n in F}
                qd = rot(QdT, "qd"); kd = rot(KdT, "kd")
                sl = slice(0, Wd)
                mk.op("act", lambda: nc.scalar.activation(out=t["sg"].t[:, sl], in_=pf.t[:, sl], func=AF.Sigmoid), reads=[pf], writes=[t["sg"]])
                mk.op("act", lambda: nc.scalar.activation(out=t["qs"].t[:, sl], in_=pq.t[:, sl], func=AF.Silu), reads=[pq], writes=[t["qs"]])
                mk.op("dve", lambda: nc.vector.tensor_scalar(out=t["fg"].t[:, sl], in0=t["sg"].t[:, sl], scalar1=oml.t[:, h:h + 1],
                                                             scalar2=lb.t[:, h:h + 1], op0=ALU.mult, op1=ALU.add),
                      reads=[t["sg"], oml, lb], writes=[t["fg"]])
                mk.op("act", lambda: nc.scalar.activation(out=t["lf"].t[:, sl], in_=t["fg"].t[:, sl], func=AF.Ln), reads=[t["fg"]], writes=[t["lf"]])
                mk.op("dve", lambda: nc.vector.tensor_tensor_scan(out=t["b"].t[:, sl], data0=sm.t[:, sl], data1=t["lf"].t[:, sl],
                                                                  initial=0.0, op0=ALU.mult, op1=ALU.add),
                      reads=[sm, t["lf"]], writes=[t["b"]])
                nch = Wd // CH
                if not fw:
                    b3 = t["b"].t[:, sl].rearrange("p (c t) -> p c t", t=CH)
                    mk.op("dve", lambda: nc.vector.tensor_tensor(out=t["lf"].t[:, sl], in0=t["lf"].t[:, sl], in1=t["b"].t[:, sl], op=ALU.subtract),
                          reads=[t["lf"], t["b"]], writes=[t["lf"]])
                    mk.op("dve", lambda: nc.vector.tensor_tensor(out=t["b"].t[:, sl].rearrange("p (c t) -> p c t", t=CH),
                                                                 in0=t["lf"].t[:, sl].rearrange("p (c t) -> p c t", t=CH),
                                                                 in1=b3[:, :, CH - 1:CH].to_broadcast([128, nch, CH]), op=ALU.add),
                          reads=[t["lf"], t["b"]], writes=[t["b"]])
                mk.op("act", lambda: nc.scalar.activation(out=t["eb"].t[:, sl], in_=t["b"].t[:, sl], func=AF.Exp), reads=[t["b"]], writes=[t["eb"]])
                mk.op("act", lambda: nc.scalar.activation(out=t["enb"].t[:, sl], in_=t["b"].t[:, sl], func=AF.Exp, scale=-1.0), reads=[t["b"]], writes=[t["enb"]])
                mk.op("dve", lambda: nc.vector.tensor_scalar(out=t["kk"].t[:, sl], in0=t["fg"].t[:, sl], scalar1=-1.0, scalar2=1.0,
                                                             op0=ALU.mult, op1=ALU.add), reads=[t["fg"]], writes=[t["kk"]])
                mk.op("dve", lambda: nc.vector.tensor_tensor(out=qd.t[:, sl], in0=t["qs"].t[:, sl], in1=t["eb"].t[:, sl], op=ALU.mult),
                      reads=[t["qs"], t["eb"]], writes=[qd])
                mk.op("dve", lambda: nc.vector.tensor_tensor(out=kd.t[:, sl], in0=t["kk"].t[:, sl], in1=t["enb"].t[:, sl], op=ALU.mult),
                      reads=[t["kk"], t["enb"]], writes=[kd])
                eb = t["eb"]
                blks = list(range(nblk)) if fw else list(range(nblk))[::-1]
                first_in_tile = [True]
                for blk in blks:
                    bs = slice(blk * 128, (blk + 1) * 128)
                    vt = vts[blk]
                    rtb, rtap = rot(rT, "rt"); kt_ = rot(Kdtok, "kt")
                    mk.op("pe", lambda: nc.tensor.transpose(out=rtap, in_=kd.t[:, bs], identity=IDENT), reads=[kd, cm], writes=[rtb])
                    mk.op("act", lambda: nc.scalar.copy(out=kt_.t[:, :], in_=rtap), reads=[rtb], writes=[kt_])
                    rab, raap = rot(rA, "ra"); atm = rot(ATm, "atm")
                    mk.mmf([rab], [(raap, kd.t[:, bs], qd.t[:, bs], True, True)], reads=[kd, qd])
                    mk.op("dve", lambda: nc.vector.tensor_tensor(out=atm.t[:, :], in0=raap, in1=MASK, op=ALU.mult), reads=[rab, cm], writes=[atm])
                    chunks = [0, 1] if fw else [1, 0]
                    s_before = []
                    for ci in chunks:
                        s_before.append(Sbf[h][sidx[h]])
                        rsb, rsap = rot(rS, "rs"); tS = rot(tmpS, "ts")
                        rows = slice(ci * CH, (ci + 1) * CH)
                        mk.mmf([rsb], [(rsap, kt_.t[rows, :], vt.t[rows, h * 128:(h + 1) * 128], True, True)], reads=[kt_, vt])
                        cglob = blk * 2 + ci
                        col = cglob * CH + (CH - 1 if fw else 0)
                        et = eb.t[:, col:col + 1]
                        mk.op("act", lambda: nc.scalar.activation(out=tS.t[:, :], in_=rsap, func=AF.Copy, scale=et), reads=[rsb, eb], writes=[tS])
                        mk.op("dve", lambda: nc.vector.scalar_tensor_tensor(out=S32[h].t[:, :], in0=S32[h].t[:, :], scalar=et, in1=tS.t[:, :],
                                                                            op0=ALU.mult, op1=ALU.add), reads=[S32[h], tS, eb], writes=[S32[h]])
                        sidx[h] = (sidx[h] + 1) % 6
                        nxt = Sbf[h][sidx[h]]
                        mk.op("pool", lambda: nc.gpsimd.tensor_copy(out=nxt.t[:, :], in_=S32[h].t[:, :]), reads=[S32[h]], writes=[nxt])
                    if lat:
                        first = first_in_tile[0]
                        first_in_tile[0] = False
                        last = (blk == blks[-1])

                        def omm(bs=bs, vt=vt, atm=atm, qd=qd, s_before=s_before, chunks=chunks, blk=blk, h=h):
                            mms = [(pO.t[:, bs], vt.t[:, h * 128:(h + 1) * 128], atm.t[:, :], True, False)]
                            for n_, ci in enumerate(chunks):
                                cs_ = slice(blk * 128 + ci * CH, blk * 128 + (ci + 1) * CH)
                                mms.append((pO.t[:, cs_], s_before[n_].t[:, :], qd.t[:, cs_], False, n_ == 1))
                            mk.mmf([pO], mms, reads=[vt, atm, qd] + s_before)
                        pend.append(omm)
                        if len(pend) > 1:
                            pend.pop(0)()
                if not lat:
                    continue
                flush()
                if fw:
                    of = rot(ofs, "of")
                    mk.op("act", lambda: nc.scalar.copy(out=of.t[:, :], in_=pO.t[:, :]), reads=[pO], writes=[of])
                    mk.dma("sp", ofw_d[h][:, c0:c0 + 512], of.t[:, :], reads=[of], writes=[ofwb])
                else:
                    of = rot(ofs, "of")
                    mk.dma("sp", of.t[:, :], ofw_d[h][:, c0:c0 + 512], reads=[ofwb], writes=[of])
                    mk.op("dve", lambda: nc.vector.tensor_tensor(out=o32.t[:, :], in0=pO.t[:, :], in1=of.t[:, :], op=ALU.add),
                          reads=[pO, of], writes=[o32])
                    pg = pP[0]
                    mms = [(pg.t[:, :], W.t[:, 3, kc, h * 128:(h + 1) * 128], a.t[:, kc, :], kc == 0, kc == KC - 1) for kc in range(KC)]
                    mk.mmf([pg], mms, reads=[a, W])
                    mk.op("act", lambda: nc.scalar.activation(out=sgg.t[:, :], in_=pg.t[:, :], func=AF.Sigmoid), reads=[pg], writes=[sgg])
                    mk.op("act", lambda: nc.scalar.activation(out=sq.t[:, :], in_=o32.t[:, :], func=AF.Square), reads=[o32], writes=[sq])
                    mk.mmf([pSSQ], [(pSSQ.t[:, :], ones.t[:, :], sq.t[:, :], True, True)], reads=[ones, sq])
                    mk.op("dve", lambda: nc.vector.tensor_scalar(out=r32.t[:, :], in0=pSSQ.t[:, :], scalar1=1.0 / 128, scalar2=EPS,
                                                                 op0=ALU.mult, op1=ALU.add), reads=[pSSQ], writes=[r32])
                    mk.op("act", lambda: nc.scalar.activation(out=r32.t[:, :], in_=r32.t[:, :], func=AF.Sqrt), reads=[r32], writes=[r32])
                    mk.op("dve", lambda: nc.vector.reciprocal(out=r32.t[:, :], in_=r32.t[:, :]), reads=[r32], writes=[r32])
                    mk.op("dve", lambda: nc.vector.scalar_tensor_tensor(out=on32.t[:, :], in0=o32.t[:, :], scalar=gon.t[:, 0:1], in1=r32.t[:, :],
                                                                        op0=ALU.mult, op1=ALU.mult), reads=[o32, gon, r32], writes=[on32])
                    ot = rot(otile, "ot")
                    mk.op("dve", lambda: nc.vector.tensor_tensor(out=ot.t[:, :], in0=on32.t[:, :], in1=sgg.t[:, :], op=ALU.mult),
                          reads=[on32, sgg], writes=[ot])
                    mk.dma("sp", out_d[h * 128:(h + 1) * 128, c0:c0 + 512], ot.t[:, :], reads=[ot], writes=[outb])
    mk.finish([outb])
    return nc


def mixa1_consts():
    if "m1" in _CONST:
        return _CONST["m1"]
    bf = ml_dtypes.bfloat16
    s = np.arange(128)[:, None]; t = np.arange(128)[None, :]
    same = (s // CH) == (t // CH)
    mfw = (same & (s <= t)).astype(np.float32)
    mbw = (same & (s >= t)).astype(np.float32)
    cm = np.stack([mfw, mbw, np.eye(128, dtype=np.float32), np.zeros((128, 128), np.float32)], 1).astype(bf)
    smr = np.ones((128, 512), np.float32); smr[:, ::CH] = 0.0
    _CONST["m1"] = dict(cm=np.ascontiguousarray(cm), sm=smr)
    return _CONST["m1"]


def run_mixa1(a_lat, a_ctx, hgrn_w_in, lb_logits, o_norm, trace=False):
    C = mixa1_consts()
    nc = build_mixa1_launch()
    aT = [np.ascontiguousarray(np.concatenate([a_lat[b], a_ctx[b]], 0).T) for b in range(2)]
    gon = np.ascontiguousarray(o_norm.reshape(128, 1).astype(np.float32))
    in_maps = []
    for cidx in range(NCORES):
        b, j = divmod(cidx, 4)
        cols = np.arange(j * 512, (j + 1) * 512)
        w5 = np.stack([hgrn_w_in[:, s * 2048 + cols].reshape(KC, 128, 512).transpose(1, 0, 2) for s in range(5)], 0)
        lbl = lb_logits[:, cols].reshape(2, NHL, 128).transpose(2, 0, 1)
        in_maps.append(dict(aT=aT[b], w5=np.ascontiguousarray(w5), lbl=np.ascontiguousarray(lbl), gon=gon, cm=C["cm"], sm=C["sm"]))
    res = run_bass_kernel_spmd(nc, in_maps, core_ids=list(range(NCORES)), trace=trace)
    og = np.empty((2, NLAT, 2048), a_lat.dtype)
    for cidx in range(NCORES):
        b, j = divmod(cidx, 4)
        og[b][:, j * 512:(j + 1) * 512] = res.results[cidx]["og"].T
    return og, res


def _run(nc, in_maps):
    return run_bass_kernel_spmd(nc, in_maps, core_ids=list(range(NCORES))).results


def pack_lat(lat):
    return [np.ascontiguousarray(lat[c // 4, (c % 4) * 2048:(c % 4 + 1) * 2048].T) for c in range(NCORES)]


def kernel(x, c, ctx, c_ctx, w_mod, b_mod, norm_gains, ffn_w_in, ffn_w_out, ab_w_in, qk_norm, ab_w_out,
           hgrn_w_in, hgrn_lb_logits, hgrn_o_norm, hgrn_w_out, final_norm):
    f32 = np.float32
    x, c, ctx, c_ctx = (np.asarray(a, f32) for a in (x, c, ctx, c_ctx))
    mod = run_mod(c, c_ctx, np.asarray(w_mod, f32), np.asarray(b_mod, f32))
    norm_gains = np.asarray(norm_gains, f32)
    nc = build_tok_launch(PWL, SUBT_L, 1, [("ffn", 0, 0), ("normout", 0)])
    hT = pack_tokens(x, ctx)
    w_in = lay_w_in(np.asarray(ffn_w_in[0, 0], f32)); w_out = lay_w_out(np.asarray(ffn_w_out[0, 0], f32), NH_FFN)
    g0 = gains_for(norm_gains, [0])
    res = _run(nc, [{"hT": hT[k], "modT": modT_for_core(mod, [0], k // 4), "gains": g0, "w0_in": w_in, "w0_out": w_out}
                    for k in range(NCORES)])
    hT = [r["hT_out"] for r in res]
    a_lat, a_ctx = unpack_tokens([r["aT_out"] for r in res], D)
    cat_lat, cat_ctx, _ = run_mixa0(a_lat, a_ctx, np.asarray(ab_w_in[0], f32), np.asarray(qk_norm[0], f32))
    nc = build_tok_launch(PWL, SUBT_L, 2, [("proj", 0), ("ffn", 0, 1), ("ffn", 1, 0), ("normout", 1)])
    cat = pack_tokens(cat_lat, cat_ctx)
    wp = lay_w_out(np.asarray(ab_w_out[0], f32), 1)[0]
    w1_in = lay_w_in(np.asarray(ffn_w_in[0, 1], f32)); w1_out = lay_w_out(np.asarray(ffn_w_out[0, 1], f32), NH_FFN)
    w2_in = lay_w_in(np.asarray(ffn_w_in[1, 0], f32)); w2_out = lay_w_out(np.asarray(ffn_w_out[1, 0], f32), NH_FFN)
    g01 = gains_for(norm_gains, [0, 1])
    res = _run(nc, [{"hT": hT[k], "modT": modT_for_core(mod, [0, 1], k // 4), "gains": g01, "w0_proj": wp, "cat0": cat[k],
                     "w1_in": w1_in, "w1_out": w1_out, "w2_in": w2_in, "w2_out": w2_out} for k in range(NCORES)])
    h_lat, _ = unpack_tokens([r["hT_out"] for r in res], D)
    a_lat, a_ctx = unpack_tokens([r["aT_out"] for r in res], D)
    og, _ = run_mixa1(a_lat, a_ctx, np.asarray(hgrn_w_in[0], f32), np.asarray(hgrn_lb_logits, f32), np.asarray(hgrn_o_norm[0], f32))
    nc = build_tok_launch(1024, subtiles_of(1024, 0), 1, [("proj", 0), ("ffn", 0, 1), ("final",)])
    hT = pack_lat(h_lat)
    ogT = pack_lat(og)
    wp = lay_w_out(np.asarray(hgrn_w_out[0], f32), 1)[0]
    w1_in = lay_w_in(np.asarray(ffn_w_in[1, 1], f32)); w1_out = lay_w_out(np.asarray(ffn_w_out[1, 1], f32), NH_FFN)
    g1 = gains_for(norm_gains, [1])
    fng = lay_vec(np.asarray(final_norm, f32))
    res = _run(nc, [{"hT": hT[k], "modT": modT_for_core(mod, [1], k // 4), "gains": g1, "w0_proj": wp, "cat0": ogT[k],
                     "w1_in": w1_in, "w1_out": w1_out, "fng": fng} for k in range(NCORES)])
    out = np.empty((2, 8192, D), f32)
    for k in range(NCORES):
        out[k // 4, (k % 4) * 2048:(k % 4 + 1) * 2048] = res[k]["yT"].T
    return out
```

```python
import numpy as np
import concourse.bass as bass
import concourse.mybir as mybir

F32 = mybir.dt.float32
BF16 = mybir.dt.bfloat16
I32 = mybir.dt.int32
AF = mybir.ActivationFunctionType
ALU = mybir.AluOpType

EPOCH = 24000
DMA_RING = 8


class Buf:
    __slots__ = ("t", "name", "lw", "rd")

    def __init__(self, t, name=""):
        self.t = t
        self.name = name
        self.lw = None
        self.rd = {}

    def __getitem__(self, idx):
        return self.t[idx]


class MK:
    def __init__(self, nc, same_engine_sync=True):
        self.nc = nc
        self.E = {"pe": nc.tensor, "act": nc.scalar, "dve": nc.vector, "pool": nc.gpsimd, "sp": nc.sync}
        self.cur_sem = {}
        self.cur_cnt = {}
        self.seen = {e: {} for e in self.E}
        self.same_engine_sync = same_engine_sync
        self.ring = {}
        self.ring_i = {}
        self.nsem = 0
        self._uid = 0
        self.all_dma_events = []

    def new_sem(self, name):
        self.nsem += 1
        cm = self.nc.semaphore(f"{name}_{self.nsem}")
        return cm.__enter__()

    def sb(self, name, shape, dtype):
        self._uid += 1
        return Buf(self.nc.alloc_sbuf_tensor(f"{name}_{self._uid}", list(shape), dtype), name)

    def ps(self, name, shape, dtype=F32):
        self._uid += 1
        return Buf(self.nc.alloc_psum_tensor(f"{name}_{self._uid}", list(shape), dtype), name)

    def dram(self, name, shape, dtype, kind="Internal"):
        return Buf(self.nc.dram_tensor(name, list(shape), dtype, kind=kind), name)

    def _need(self, eng, ev):
        if ev is None:
            return
        sem, val = ev
        seen = self.seen[eng]
        k = id(sem)
        if seen.get(k, (None, 0))[1] >= val:
            return
        self.E[eng].wait_ge(sem, val)
        seen[k] = (sem, val)

    def _deps(self, eng, reads, writes, skip_self=False):
        for b in reads:
            if b.lw is not None:
                self._need(eng, b.lw)
        for b in writes:
            if b.lw is not None:
                self._need(eng, b.lw)
            for ev in b.rd.values():
                self._need(eng, ev)

    def _tick(self, eng):
        if eng not in self.cur_sem or self.cur_cnt[eng] >= EPOCH:
            self.cur_sem[eng] = self.new_sem(f"s_{eng}")
            self.cur_cnt[eng] = 0
        self.cur_cnt[eng] += 1
        return (self.cur_sem[eng], self.cur_cnt[eng])

    def _commit(self, ev, reads, writes):
        for b in writes:
            b.lw = ev
            b.rd = {}
        for b in reads:
            if b not in writes:
                b.rd[id(ev[0])] = ev

    def op(self, eng, fn, reads=(), writes=(), track=True):
        self._deps(eng, reads, writes)
        ins = fn()
        if track:
            ev = self._tick(eng)
            ins.then_inc(ev[0], 1)
            if not self.same_engine_sync:
                self.seen[eng][id(ev[0])] = ev
            self._commit(ev, reads, writes)
        return ins

    def mm_group(self, out_buf, mms, reads):
        self._deps("pe", reads, [out_buf])
        n = len(mms)
        ins = None
        for i, (o, l, r) in enumerate(mms):
            ins = self.nc.tensor.matmul(o, l, r, start=(i == 0), stop=(i == n - 1))
        ev = self._tick("pe")
        ins.then_inc(ev[0], 1)
        self.seen["pe"][id(ev[0])] = ev
        self._commit(ev, reads, [out_buf])

    def dma(self, q, out_ap, in_ap, reads=(), writes=(), **kw):
        self._deps(q, reads, writes)
        ring = self.ring.setdefault(q, [[None, 0] for _ in range(DMA_RING)])
        i = self.ring_i.get(q, 0)
        self.ring_i[q] = (i + 1) % DMA_RING
        slot = ring[i]
        if slot[0] is None or slot[1] >= EPOCH:
            if slot[0] is not None:
                self._need(q, (slot[0], slot[1]))
            slot[0] = self.new_sem(f"d_{q}{i}")
            slot[1] = 0
        elif slot[1] > 0:
            self._need(q, (slot[0], slot[1]))
        ins = self.E[q].dma_start(out=out_ap, in_=in_ap, **kw)
        slot[1] += 16
        ev = (slot[0], slot[1])
        ins.then_inc(ev[0], 16)
        self._commit(ev, reads, writes)
        self.all_dma_events.append((q, ev))
        return ev

    def finish(self, out_bufs):
        for b in out_bufs:
            if b.lw is not None:
                self._need("sp", b.lw)
        for q, ring in self.ring.items():
            for sem, cnt in ring:
                if sem is not None and cnt > 0:
                    self._need("sp", (sem, cnt))


def _mm(self, out_bufs, mms, reads, start, stop):
    self._deps("pe", reads, out_bufs)
    ins = None
    for (o, l, r) in mms:
        ins = self.nc.tensor.matmul(o, l, r, start=start, stop=stop)
    ev = self._tick("pe")
    ins.then_inc(ev[0], 1)
    self.seen["pe"][id(ev[0])] = ev
    self._commit(ev, reads, out_bufs)


def _mm_multi(self, out_bufs, mms, reads, nk):
    self._deps("pe", reads, out_bufs)
    ins = None
    for (_, ki, o, l, r) in mms:
        ins = self.nc.tensor.matmul(o, l, r, start=(ki == 0), stop=(ki == nk - 1))
    ev = self._tick("pe")
    ins.then_inc(ev[0], 1)
    self.seen["pe"][id(ev[0])] = ev
    self._commit(ev, reads, out_bufs)


MK.mm = _mm
MK.mm_multi = _mm_multi


def _mmf(self, out_bufs, mms, reads):
    self._deps("pe", reads, out_bufs)
    ins = None
    for (o, l, r, st, sp) in mms:
        ins = self.nc.tensor.matmul(o, l, r, start=st, stop=sp)
    ev = self._tick("pe")
    ins.then_inc(ev[0], 1)
    self.seen["pe"][id(ev[0])] = ev
    self._commit(ev, reads, out_bufs)


def _barrier(self):
    evs = [(self.cur_sem[e], self.cur_cnt[e]) for e in self.cur_sem]
    for q, ring in self.ring.items():
        for sem, cnt in ring:
            if sem is not None and cnt > 0:
                evs.append((sem, cnt))
    for e in self.E:
        for ev in evs:
            self._need(e, ev)


MK.mmf = _mmf
MK.barrier = _barrier

D = 2048
KC = 16
FFN = 5632
EPS = 1e-6


def subtiles_of(ncols_lat, ncols_ctx):
    st = []
    off = 0
    while off < ncols_lat:
        sz = min(512, ncols_lat - off)
        st.append((off, sz, 0))
        off += sz
    if ncols_ctx:
        st.append((off, ncols_ctx, 1))
    return st


class Pools:
    def __init__(self, mk, PW, JH):
        self.PW = PW
        self.JH = JH
        self.aT = mk.sb("aT", [128, KC, PW], BF16)
        self.gT = [mk.sb(f"gT{j}", [128, PW], BF16) for j in range(JH)]
        self.hst = [mk.sb(f"hst{i}", [128, PW], F32) for i in range(3)]
        self.sq = [mk.sb(f"sq{i}", [128, PW], BF16) for i in range(2)]
        self.R = mk.sb("R", [128, PW], F32)
        self.tmp = [mk.sb(f"tmp{i}", [128, PW], F32) for i in range(2)]
        self.win = [mk.sb(f"win{i}", [128, KC, 256], BF16) for i in range(3)]
        self.wout = [mk.sb(f"wout{i}", [128, JH, 128], BF16) for i in range(2)]
        self.ones = mk.sb("ones", [128, 128], BF16)
        self.bank = [mk.ps(f"bank{i}", [128, 512], F32) for i in range(8)]
        self.cnt = {"hst": 0, "sq": 0, "tmp": 0, "win": 0, "wout": 0}
        mk.op("dve", lambda: mk.nc.vector.memset(self.ones.t[:, :], 1.0), writes=[self.ones])

    def nxt(self, kind):
        lst = getattr(self, kind)
        i = self.cnt[kind]
        self.cnt[kind] = i + 1
        return lst[i % len(lst)]


def emit_norm(mk, P, src_ap_fn, src_bufs, col0, subt, S_ap_fn, T_ap_fn, out_fn, post_fn=None):
    nc = mk.nc
    PW = sum(s[1] for s in subt)
    nst = len(subt)
    banks = P.bank[0:nst]
    for i in range(KC):
        hs = P.nxt("hst")
        mk.dma("sp", hs.t[:, 0:PW], src_ap_fn(i), reads=[src_bufs[i]], writes=[hs])
        sq = P.nxt("sq")
        mk.op("act", lambda: nc.scalar.activation(out=sq.t[:, 0:PW], in_=hs.t[:, 0:PW], func=AF.Square),
              reads=[hs], writes=[sq])
        mms = [(banks[k].t[:, 0:sz], P.ones.t[:, :], sq.t[:, off:off + sz]) for k, (off, sz, _) in enumerate(subt)]
        mk.mm(banks, mms, reads=[sq, P.ones], start=(i == 0), stop=(i == KC - 1))
    for k, (off, sz, _) in enumerate(subt):
        mk.op("dve", lambda: nc.vector.tensor_scalar(out=P.R.t[:, off:off + sz], in0=banks[k].t[:, 0:sz],
                                                     scalar1=1.0 / D, scalar2=EPS, op0=ALU.mult, op1=ALU.add),
              reads=[banks[k]], writes=[P.R])
    mk.op("act", lambda: nc.scalar.activation(out=P.R.t[:, 0:PW], in_=P.R.t[:, 0:PW], func=AF.Sqrt),
          reads=[P.R], writes=[P.R])
    mk.op("dve", lambda: nc.vector.reciprocal(out=P.R.t[:, 0:PW], in_=P.R.t[:, 0:PW]), reads=[P.R], writes=[P.R])
    for i in range(KC):
        hs = P.nxt("hst")
        mk.dma("sp", hs.t[:, 0:PW], src_ap_fn(i), reads=[src_bufs[i]], writes=[hs])
        tm = P.nxt("tmp")
        mk.op("dve", lambda: nc.vector.tensor_tensor(out=tm.t[:, 0:PW], in0=hs.t[:, 0:PW], in1=P.R.t[:, 0:PW],
                                                     op=ALU.mult), reads=[hs, P.R], writes=[tm])
        groups = {}
        for (off, sz, mc) in subt:
            if mc in groups and groups[mc][0] + groups[mc][1] == off:
                groups[mc][1] += sz
            else:
                groups[mc] = [off, sz]
        for mc, (off, sz) in groups.items():
            o_ap, o_buf = out_fn(i, off, sz)
            Tb = T_ap_fn(i, mc) if T_ap_fn is not None else 0.0
            mk.op("act", lambda: nc.scalar.activation(out=o_ap, in_=tm.t[:, off:off + sz], func=AF.Identity,
                                                      bias=Tb, scale=S_ap_fn(i, mc)),
                  reads=[tm], writes=[o_buf])
        if post_fn is not None:
            post_fn(i)


def emit_ffn(mk, P, src_ap_fn, dst_ap_fn, hbufs_src, hbufs_dst, subt, S_fn, T_fn, G_fn,
             w_in_d, w_out_d, J, NH, wbuf):
    nc = mk.nc
    PW = sum(s[1] for s in subt)
    nst = len(subt)
    JH = J // NH
    assert nst <= 3
    emit_norm(mk, P, src_ap_fn, hbufs_src, 0, subt, S_fn, T_fn,
              lambda i, off, sz: (P.aT.t[:, i, off:off + sz], P.aT))
    setA = P.bank[0:nst]
    setB = P.bank[3:3 + nst]
    for hf in range(NH):
        for jj in range(JH):
            j = hf * JH + jj
            w = P.nxt("win")
            mk.dma("pool", w.t[:, :, :], w_in_d[j], reads=[wbuf], writes=[w])
            for part, pset in ((0, setA), (1, setB)):
                mms = []
                for kc in range(KC):
                    for k, (off, sz, _) in enumerate(subt):
                        mms.append((k, kc, pset[k].t[:, 0:sz], w.t[:, kc, part * 128:(part + 1) * 128],
                                    P.aT.t[:, kc, off:off + sz]))
                mk.mm_multi(pset, mms, reads=[w, P.aT], nk=KC)
            tm = P.nxt("tmp")
            for k, (off, sz, _) in enumerate(subt):
                mk.op("act", lambda: nc.scalar.activation(out=tm.t[:, off:off + sz], in_=setA[k].t[:, 0:sz],
                                                          func=AF.Silu), reads=[setA[k]], writes=[tm])
            for k, (off, sz, _) in enumerate(subt):
                mk.op("dve", lambda: nc.vector.tensor_tensor(out=P.gT[jj].t[:, off:off + sz], in0=tm.t[:, off:off + sz],
                                                             in1=setB[k].t[:, 0:sz], op=ALU.mult),
                      reads=[tm, setB[k]], writes=[P.gT[jj]])
        emit_outproj(mk, P, lambda jj, off, sz: P.gT[jj].t[:, off:off + sz], P.gT[0:JH], JH,
                     lambda i: w_out_d[hf, i], wbuf,
                     (src_ap_fn if hf == 0 else dst_ap_fn), (hbufs_src if hf == 0 else hbufs_dst),
                     dst_ap_fn, hbufs_dst, subt, G_fn)


def emit_outproj(mk, P, rhs_fn, rhs_bufs, nk, w_d_fn, wbuf, src_ap_fn, hbufs_src, dst_ap_fn, hbufs_dst, subt, G_fn):
    nc = mk.nc
    PW = sum(s[1] for s in subt)
    nst = len(subt)
    setA = P.bank[0:nst]
    setB = P.bank[3:3 + nst]
    for i in range(KC):
        wo = P.nxt("wout")
        mk.dma("pool", wo.t[:, 0:nk, :], w_d_fn(i), reads=[wbuf], writes=[wo])
        pset = setA if (i % 2 == 0) else setB
        mms = []
        for jj in range(nk):
            for k, (off, sz, _) in enumerate(subt):
                mms.append((k, jj, pset[k].t[:, 0:sz], wo.t[:, jj, :], rhs_fn(jj, off, sz)))
        mk.mm_multi(pset, mms, reads=[wo] + list(rhs_bufs), nk=nk)
        hs = P.nxt("hst")
        mk.dma("sp", hs.t[:, 0:PW], src_ap_fn(i), reads=[hbufs_src[i]], writes=[hs])
        for k, (off, sz, mc) in enumerate(subt):
            mk.op("dve", lambda: nc.vector.scalar_tensor_tensor(out=hs.t[:, off:off + sz], in0=pset[k].t[:, 0:sz],
                                                                scalar=G_fn(i, mc), in1=hs.t[:, off:off + sz],
                                                                op0=ALU.mult, op1=ALU.add),
                  reads=[pset[k], hs], writes=[hs])
        mk.dma("sp", dst_ap_fn(i), hs.t[:, 0:PW], reads=[hs], writes=[hbufs_dst[i]])


def emit_prep_mod(mk, mod_sb, gain_sb, S_sb, G_sb):
    nc = mk.nc
    for k in range(3):
        for mc in range(2):
            mk.op("dve", lambda: nc.vector.scalar_tensor_tensor(
                out=S_sb.t[:, k, :, mc], in0=mod_sb.t[:, (3 * k + 1) * KC:(3 * k + 2) * KC, mc], scalar=1.0,
                in1=gain_sb.t[:, k, :], op0=ALU.add, op1=ALU.mult), reads=[mod_sb, gain_sb], writes=[S_sb])
            mk.op("dve", lambda: nc.vector.tensor_scalar(
                out=G_sb.t[:, k, :, mc], in0=mod_sb.t[:, (3 * k + 2) * KC:(3 * k + 3) * KC, mc],
                scalar1=(1.0 if k == 1 else 0.5), scalar2=None, op0=ALU.mult), reads=[mod_sb], writes=[G_sb])


def emit_mod(mk, P, cT_d, w_d, b_d, out_d, nch, wbuf):
    nc = mk.nc
    c_sb = mk.sb("mod_c", [128, KC, 3], F32)
    sc_sb = mk.sb("mod_sc", [128, KC, 3], F32)
    wsl = [mk.sb(f"mod_w{i}", [128, KC, 128], F32) for i in range(3)]
    b_sb = mk.sb("mod_b", [128, nch], F32)
    o_sb = mk.sb("mod_o", [128, nch, 3], F32)
    mk.dma("sp", c_sb.t[:, :, :], cT_d, reads=[wbuf], writes=[c_sb])
    mk.dma("sp", b_sb.t[:, :], b_d, reads=[wbuf], writes=[b_sb])
    mk.op("act", lambda: nc.scalar.activation(out=sc_sb.t[:, :, :], in_=c_sb.t[:, :, :], func=AF.Silu),
          reads=[c_sb], writes=[sc_sb])
    for ch in range(nch):
        wo = wsl[ch % 3]
        mk.dma("sp", wo.t[:, 0:KC, :], w_d[ch], reads=[wbuf], writes=[wo])
        bk = P.bank[ch % 8]
        mms = [(0, kc, bk.t[:, 0:3], wo.t[:, kc, :], sc_sb.t[:, kc, :]) for kc in range(KC)]
        mk.mm_multi([bk], mms, reads=[wo, sc_sb], nk=KC)
        mk.op("act", lambda: nc.scalar.activation(out=o_sb.t[:, ch, :], in_=bk.t[:, 0:3], func=AF.Identity,
                                                  bias=b_sb.t[:, ch:ch + 1], scale=1.0),
              reads=[bk, b_sb], writes=[o_sb])
    mk.dma("sp", out_d, o_sb.t[:, :, :], reads=[o_sb], writes=[wbuf])

import ml_dtypes
from concourse.bass_utils import run_bass_kernel_spmd

NCORES = 8
NL = 2
J_FFN = FFN // 128
NH_FFN = 2
NPASS = 2


def build_tok_launch(PW, subt, n_layers, steps, out_h=True):
    nc = bass.Bass("TRN2", target_bir_lowering=False)
    mk = MK(nc)
    JH = J_FFN // NH_FFN
    P = Pools(mk, PW, JH)
    NT = NPASS * PW
    hin = nc.dram_tensor("hT", [D, NT], F32, kind="ExternalInput").ap()
    hout = nc.dram_tensor("hT_out", [D, NT], F32, kind=("ExternalOutput" if out_h else "Internal")).ap()
    modT = nc.dram_tensor("modT", [n_layers, 128, 144, 2], F32, kind="ExternalInput").ap()
    gains = nc.dram_tensor("gains", [n_layers, 128, 3, KC], F32, kind="ExternalInput").ap()
    wbuf = Buf(None, "weights")
    mod_sb, S_sb, G_sb = [], [], []
    for li in range(n_layers):
        m = mk.sb(f"mod{li}", [128, 144, 2], F32)
        g = mk.sb(f"gain{li}", [128, 3, KC], F32)
        S = mk.sb(f"S{li}", [128, 3, KC, 2], F32)
        G = mk.sb(f"G{li}", [128, 3, KC, 2], F32)
        mk.dma("sp", m.t[:, :, :], modT[li], reads=[wbuf], writes=[m])
        mk.dma("sp", g.t[:, :, :], gains[li], reads=[wbuf], writes=[g])
        emit_prep_mod(mk, m, g, S, G)
        mod_sb.append(m); S_sb.append(S); G_sb.append(G)
    hb_in = [[Buf(None, f"hi{p}_{i}") for i in range(KC)] for p in range(NPASS)]
    hb_out = [[Buf(None, f"ho{p}_{i}") for i in range(KC)] for p in range(NPASS)]
    outs = [b for p in range(NPASS) for b in hb_out[p]]
    in_place = False
    for si, st in enumerate(steps):
        kind = st[0]
        if kind == "ffn":
            _, li, f, = st
            w_in_d = nc.dram_tensor(f"w{si}_in", [J_FFN, 128, KC, 256], F32, kind="ExternalInput").ap()
            w_out_d = nc.dram_tensor(f"w{si}_out", [NH_FFN, KC, 128, JH, 128], F32, kind="ExternalInput").ap()
            k = 0 if f == 0 else 2
        elif kind == "proj":
            _, li = st
            w_p_d = nc.dram_tensor(f"w{si}_proj", [KC, 128, KC, 128], F32, kind="ExternalInput").ap()
            cat_d = nc.dram_tensor(f"cat{si}", [D, NT], BF16, kind="ExternalInput").ap()
            catb = Buf(None, "cat")
        elif kind == "normout":
            _, li = st
            aT_d = nc.dram_tensor("aT_out", [D, NT], BF16, kind="ExternalOutput").ap()
            aTb = Buf(None, "aT_out")
            outs.append(aTb)
        elif kind == "final":
            fng_d = nc.dram_tensor("fng", [128, KC], F32, kind="ExternalInput").ap()
            y_d = nc.dram_tensor("yT", [D, NT], F32, kind="ExternalOutput").ap()
            fng = mk.sb("fng", [128, KC], F32)
            mk.dma("sp", fng.t[:, :], fng_d, reads=[wbuf], writes=[fng])
            yb = Buf(None, "yT")
            outs.append(yb)
        for p in range(NPASS):
            c0 = p * PW
            src = (hout if in_place else hin)
            sb_ = (hb_out if in_place else hb_in)[p]
            src_fn = (lambda i, src=src, c0=c0: src[i * 128:(i + 1) * 128, c0:c0 + PW])
            dst_fn = (lambda i, c0=c0: hout[i * 128:(i + 1) * 128, c0:c0 + PW])
            if kind == "ffn":
                emit_ffn(mk, P, src_fn, dst_fn, sb_, hb_out[p], subt,
                         lambda i, mc: S_sb[li].t[:, k, i, mc:mc + 1],
                         lambda i, mc: mod_sb[li].t[:, 3 * k * KC + i, mc:mc + 1],
                         lambda i, mc: G_sb[li].t[:, k, i, mc:mc + 1],
                         w_in_d, w_out_d, J_FFN, NH_FFN, wbuf)
            elif kind == "proj":
                mk.dma("sp", P.aT.t[:, :, 0:PW], cat_d[:, c0:c0 + PW].rearrange("(kc p) t -> p kc t", p=128),
                       reads=[catb], writes=[P.aT])
                emit_outproj(mk, P, lambda jj, off, sz: P.aT.t[:, jj, off:off + sz], [P.aT], KC,
                             lambda i: w_p_d[i], wbuf, src_fn, sb_, dst_fn, hb_out[p], subt,
                             lambda i, mc: G_sb[li].t[:, 1, i, mc:mc + 1])
            elif kind == "normout":
                emit_norm(mk, P, src_fn, sb_, 0, subt,
                          lambda i, mc: S_sb[li].t[:, 1, i, mc:mc + 1],
                          lambda i, mc: mod_sb[li].t[:, 3 * KC + i, mc:mc + 1],
                          lambda i, off, sz: (P.aT.t[:, i, off:off + sz], P.aT))
                mk.dma("sp", aT_d[:, c0:c0 + PW].rearrange("(kc p) t -> p kc t", p=128), P.aT.t[:, :, 0:PW],
                       reads=[P.aT], writes=[aTb])
            elif kind == "final":
                cur = {}

                def out_fn(i, off, sz):
                    if "t" not in cur or cur["i"] != i:
                        cur["t"] = P.nxt("hst"); cur["i"] = i
                    return cur["t"].t[:, off:off + sz], cur["t"]

                def post_fn(i, c0=c0):
                    mk.dma("sp", y_d[i * 128:(i + 1) * 128, c0:c0 + PW], cur["t"].t[:, 0:PW], reads=[cur["t"]], writes=[yb])
                emit_norm(mk, P, src_fn, sb_, 0, subt, lambda i, mc: fng.t[:, i:i + 1], None, out_fn, post_fn)
        if kind in ("ffn", "proj"):
            in_place = True
    mk.finish(outs)
    return nc


def build_mod_launch(nch):
    nc = bass.Bass("TRN2", target_bir_lowering=False)
    mk = MK(nc)
    P = Pools(mk, 64, KC)
    cT = nc.dram_tensor("cT", [128, KC, 3], F32, kind="ExternalInput").ap()
    w = nc.dram_tensor("w", [nch, 128, KC, 128], F32, kind="ExternalInput").ap()
    b = nc.dram_tensor("b", [128, nch], F32, kind="ExternalInput").ap()
    o = nc.dram_tensor("o", [128, nch, 3], F32, kind="ExternalOutput").ap()
    wbuf = Buf(None, "io")
    emit_mod(mk, P, cT, w, b, o, nch, wbuf)
    mk.finish([wbuf])
    return nc


def lay_w_in(w):
    F_ = w.shape[1] // 2
    Jn = F_ // 128
    return np.ascontiguousarray(w.reshape(KC, 128, 2, Jn, 128).transpose(3, 1, 0, 2, 4).reshape(Jn, 128, KC, 256))


def lay_w_out(w, NH):
    F_ = w.shape[0]
    JH = F_ // 128 // NH
    return np.ascontiguousarray(w.reshape(NH, JH, 128, KC, 128).transpose(0, 3, 2, 1, 4))


def lay_vec(v):
    n = v.shape[-1] // 128
    r = v.reshape(v.shape[:-1] + (n, 128))
    return np.ascontiguousarray(np.moveaxis(r, -1, 0))


def run_mod(c, c_ctx, w_mod, b_mod):
    ncols = 9 * D
    nch_total = NL * ncols // 128
    nch = nch_total // NCORES
    cT = lay_vec(np.stack([c[0], c[1], c_ctx], 0))
    cT = np.ascontiguousarray(cT.transpose(0, 2, 1))
    wl = w_mod.reshape(NL, KC, 128, ncols // 128, 128).transpose(0, 3, 2, 1, 4).reshape(nch_total, 128, KC, 128)
    bl = b_mod.reshape(nch_total, 128).T
    nc = build_mod_launch(nch)
    in_maps = []
    for cidx in range(NCORES):
        sl = slice(cidx * nch, (cidx + 1) * nch)
        in_maps.append({"cT": cT, "w": np.ascontiguousarray(wl[sl]), "b": np.ascontiguousarray(bl[:, sl])})
    res = run_bass_kernel_spmd(nc, in_maps, core_ids=list(range(NCORES)))
    o = np.concatenate([r["o"] for r in res.results], axis=1)
    mod = o.transpose(2, 1, 0).reshape(3, NL, ncols).transpose(1, 0, 2)
    return np.ascontiguousarray(mod)


def modT_for_core(mod, layers, b):
    out = []
    for l in layers:
        m = mod[l][[b, 2]]
        out.append(m.reshape(2, 144, 128).transpose(2, 1, 0))
    return np.ascontiguousarray(np.stack(out, 0))


def gains_for(norm_gains, layers):
    return np.ascontiguousarray(np.stack([lay_vec(norm_gains[l]) for l in layers], 0))


PWL = 1056
SUBT_L = subtiles_of(1024, 32)


def pack_tokens(lat, ctxa, dtype=None):
    outs = []
    for cidx in range(NCORES):
        b, j = divmod(cidx, 4)
        cols = []
        for p in range(NPASS):
            cols.append(lat[b, j * 2048 + p * 1024: j * 2048 + (p + 1) * 1024])
            cols.append(ctxa[b, j * 64 + p * 32: j * 64 + (p + 1) * 32])
        outs.append(np.ascontiguousarray(np.concatenate(cols, 0).T))
    return outs


def unpack_tokens(per_core, C):
    dt = per_core[0].dtype
    lat = np.empty((2, 8192, C), dt)
    ctxa = np.empty((2, 256, C), dt)
    for cidx in range(NCORES):
        b, j = divmod(cidx, 4)
        a = per_core[cidx].T
        for p in range(NPASS):
            base = p * PWL
            lat[b, j * 2048 + p * 1024: j * 2048 + (p + 1) * 1024] = a[base:base + 1024]
            ctxa[b, j * 64 + p * 32: j * 64 + (p + 1) * 32] = a[base + 1024:base + 1056]
    return lat, ctxa

NLAT = 8192
NCTX = 256
NTOK = NLAT + NCTX
HD = 128
ATTN_SCALE = HD ** -0.5


def emit_qknorm_rope(mk, W, ps, gain_ap, dst_ap, dst_buf, T, cs_aps=None):
    nc = mk.nc
    mk.op("act", lambda: nc.scalar.activation(out=T["sq"].t[:, 0:W], in_=ps.t[:, 0:W], func=AF.Square),
          reads=[ps], writes=[T["sq"]])
    mk.mm([T["ssq"]], [(T["ssq"].t[:, 0:W], T["ones"].t[:, :], T["sq"].t[:, 0:W])], reads=[T["sq"], T["ones"]],
          start=True, stop=True)
    mk.op("dve", lambda: nc.vector.tensor_scalar(out=T["r"].t[:, 0:W], in0=T["ssq"].t[:, 0:W], scalar1=1.0 / HD,
                                                 scalar2=EPS, op0=ALU.mult, op1=ALU.add), reads=[T["ssq"]], writes=[T["r"]])
    mk.op("act", lambda: nc.scalar.activation(out=T["r"].t[:, 0:W], in_=T["r"].t[:, 0:W], func=AF.Sqrt),
          reads=[T["r"]], writes=[T["r"]])
    mk.op("dve", lambda: nc.vector.reciprocal(out=T["r"].t[:, 0:W], in_=T["r"].t[:, 0:W]), reads=[T["r"]], writes=[T["r"]])
    if cs_aps is None:
        mk.op("dve", lambda: nc.vector.scalar_tensor_tensor(out=dst_ap, in0=ps.t[:, 0:W], scalar=gain_ap, in1=T["r"].t[:, 0:W],
                                                            op0=ALU.mult, op1=ALU.mult), reads=[ps, T["r"]], writes=[dst_buf])
        return
    C_ap, S_ap, csbuf = cs_aps
    mk.op("dve", lambda: nc.vector.scalar_tensor_tensor(out=T["qn"].t[:, 0:W], in0=ps.t[:, 0:W], scalar=gain_ap,
                                                        in1=T["r"].t[:, 0:W], op0=ALU.mult, op1=ALU.mult),
          reads=[ps, T["r"]], writes=[T["qn"]])
    mk.op("act", lambda: nc.scalar.copy(out=T["qnb"].t[:, 0:W], in_=T["qn"].t[:, 0:W]), reads=[T["qn"]], writes=[T["qnb"]])
    mk.mm([T["rot"]], [(T["rot"].t[:, 0:W], T["rmat"].t[:, 0, :], T["qnb"].t[:, 0:W])], reads=[T["qnb"], T["rmat"]],
          start=True, stop=True)
    mk.op("dve", lambda: nc.vector.tensor_tensor(out=T["t1"].t[:, 0:W], in0=T["qn"].t[:, 0:W], in1=C_ap, op=ALU.mult),
          reads=[T["qn"], csbuf], writes=[T["t1"]])
    mk.op("dve", lambda: nc.vector.tensor_tensor(out=T["t2"].t[:, 0:W], in0=T["rot"].t[:, 0:W], in1=S_ap, op=ALU.mult),
          reads=[T["rot"], csbuf], writes=[T["t2"]])
    mk.op("pool", lambda: nc.gpsimd.tensor_tensor(out=dst_ap, in0=T["t1"].t[:, 0:W], in1=T["t2"].t[:, 0:W], op=ALU.add),
          reads=[T["t1"], T["t2"]], writes=[dst_buf])


def build_mixa0_launch():
    nc = bass.Bass("TRN2", target_bir_lowering=False)
    mk = MK(nc)
    aT_d = nc.dram_tensor("aT", [D, NTOK], BF16, kind="ExternalInput").ap()
    wsel_d = nc.dram_tensor("wsel", [128, KC, 768], F32, kind="ExternalInput").ap()
    gqk_d = nc.dram_tensor("gqk", [128, 2], F32, kind="ExternalInput").ap()
    cos_d = nc.dram_tensor("cosT", [128, NLAT], F32, kind="ExternalInput").ap()
    sin_d = nc.dram_tensor("sinT", [128, NLAT], F32, kind="ExternalInput").ap()
    cmat_d = nc.dram_tensor("cmat", [128, 4, 128], BF16, kind="ExternalInput").ap()
    dftL_d = nc.dram_tensor("dftL", [16, 8, 2, 128, 8, 512], BF16, kind="ExternalInput").ap()
    dftC_d = nc.dram_tensor("dftC", [2, 128, 2, 256], BF16, kind="ExternalInput").ap()
    cat_d = nc.dram_tensor("cat", [512, NTOK], BF16, kind="ExternalOutput").ap()
    io = Buf(None, "io")
    catb = Buf(None, "cat")
    KT = mk.sb("KT", [128, NTOK], BF16)
    V = mk.sb("V", [128, 66, 128], BF16)
    UW = mk.sb("UW", [128, 66, 256], BF16)
    wsel = mk.sb("wsel", [128, KC, 768], BF16)
    cmat = mk.sb("cmat", [128, 4, 128], BF16)
    gqk = mk.sb("gqk", [128, 2], F32)
    gq = mk.sb("gq", [128, 1], F32)
    QT = mk.sb("QT", [128, NLAT], BF16)
    QC = mk.sb("QC", [128, NCTX], BF16)
    aslot = [mk.sb(f"aslot{i}", [128, KC, 512], BF16) for i in range(2)]
    cslot = [mk.sb(f"cslot{i}", [128, 2, 512], F32) for i in range(2)]
    T = {n: mk.sb(n, [128, 512], F32) for n in ("r", "qn", "t1", "t2")}
    T["sq"] = mk.sb("sq", [128, 512], BF16)
    T["qnb"] = mk.sb("qnb", [128, 512], BF16)
    T["ones"] = mk.sb("ones", [128, 128], BF16)
    fT = mk.sb("fT", [128, 512], BF16)
    PT = [mk.sb(f"PT{i}", [128, 512], BF16) for i in range(4)]
    rden = mk.sb("rden", [128, 512], F32)
    otile = [mk.sb(f"otile{i}", [128, 512], BF16) for i in range(2)]
    dslot = [mk.sb(f"dslot{i}", [128, 16, 512], BF16) for i in range(2)]
    dtile = mk.sb("dtile", [128, 512], BF16)
    print("mixa0 sbuf left", nc.sbuf_bytes_remaining)
    bank = [mk.ps(f"bank{i}", [128, 512], F32) for i in range(8)]
    T["ssq"] = bank[6]
    T["rot"] = bank[7]
    mk.op("dve", lambda: nc.vector.memset(T["ones"].t[:, :], 1.0), writes=[T["ones"]])
    mk.dma("pool", wsel.t[:, :, :], wsel_d, reads=[io], writes=[wsel])
    mk.dma("sp", cmat.t[:, :, :], cmat_d, reads=[io], writes=[cmat])
    mk.dma("sp", gqk.t[:, :], gqk_d, reads=[io], writes=[gqk])
    mk.op("dve", lambda: nc.vector.tensor_scalar(out=gq.t[:, :], in0=gqk.t[:, 0:1], scalar1=ATTN_SCALE, scalar2=None,
                                                 op0=ALU.mult), reads=[gqk], writes=[gq])
    T["rmat"] = cmat
    tiles = [(t * 512, 512, True) for t in range(16)] + [(NLAT, NCTX, False)]
    cnt = {"a": 0, "c": 0}

    def load_tile(c0, W, lat):
        a = aslot[cnt["a"] % 2]; cnt["a"] += 1
        mk.dma("sp", a.t[:, :, 0:W], aT_d[:, c0:c0 + W].rearrange("(kc p) t -> p kc t", p=128), reads=[io], writes=[a])
        cs = None
        if lat:
            c = cslot[cnt["c"] % 2]; cnt["c"] += 1
            mk.dma("sp", c.t[:, 0, 0:W], cos_d[:, c0:c0 + W], reads=[io], writes=[c])
            mk.dma("sp", c.t[:, 1, 0:W], sin_d[:, c0:c0 + W], reads=[io], writes=[c])
            cs = (c.t[:, 0, 0:W], c.t[:, 1, 0:W], c)
        return a, cs

    def proj(a, W, col0, pb):
        mms = [(0, kc, pb.t[:, 0:W], wsel.t[:, kc, col0:col0 + 128], a.t[:, kc, 0:W]) for kc in range(KC)]
        mk.mm_multi([pb], mms, reads=[wsel, a], nk=KC)

    for ti, (c0, W, lat) in enumerate(tiles):
        a, cs = load_tile(c0, W, lat)
        nsb = W // 128
        ch0 = c0 // 128
        pk = bank[ti % 2]
        proj(a, W, 512, pk)
        emit_qknorm_rope(mk, W, pk, gqk.t[:, 1:2], KT.t[:, c0:c0 + W], KT, T, cs)
        pv = bank[2]
        mms = []
        for sb_ in range(nsb):
            for kc in range(KC):
                mms.append((0, kc, pv.t[:, sb_ * 128:(sb_ + 1) * 128], a.t[:, kc, sb_ * 128:(sb_ + 1) * 128], wsel.t[:, kc, 640:768]))
        mk.mm_multi([pv], mms, reads=[wsel, a], nk=KC)
        mk.op("act", lambda: nc.scalar.copy(out=V.t[:, ch0:ch0 + nsb, :], in_=pv.t[:, 0:W].rearrange("p (s c) -> p s c", c=128)),
              reads=[pv], writes=[V])
        pf = bank[3]
        proj(a, W, 0, pf)
        mk.op("act", lambda: nc.scalar.copy(out=fT.t[:, 0:W], in_=pf.t[:, 0:W]), reads=[pf], writes=[fT])
        for half in range((nsb + 1) // 2):
            pu = bank[4 + half]
            nn = min(2, nsb - half * 2)
            for s2 in range(nn):
                sb_ = half * 2 + s2
                mk.mm([pu], [(pu.t[:, s2 * 256:(s2 + 1) * 256], fT.t[:, sb_ * 128:(sb_ + 1) * 128],
                              cmat.t[:, 1:3, :].rearrange("p a c -> p (a c)"))], reads=[fT, cmat], start=True, stop=True)
            mk.op("dve", lambda: nc.vector.tensor_copy(out=UW.t[:, ch0 + half * 2:ch0 + half * 2 + nn, :],
                                                       in_=pu.t[:, 0:nn * 256].rearrange("p (s c) -> p s c", c=256)),
                  reads=[pu], writes=[UW])
    T["ssq"] = bank[2]
    T["rot"] = bank[3]
    py = bank[7]

    def dft_units():
        units = [(kt, ng) for kt in range(16) for ng in range(8)]

        def load(u):
            kt, ng = units[u]
            ds_ = dslot[u % 2]
            for cs_ in range(2):
                mk.dma("sp", ds_.t[:, cs_ * 8:(cs_ + 1) * 8, :], dftL_d[kt, ng, cs_], reads=[io], writes=[ds_])
        load(0)
        load(1)
        yield
        for u, (kt, ng) in enumerate(units):
            ds_ = dslot[u % 2]
            mms = []
            for n8 in range(8):
                nch = ng * 8 + n8
                mms.append((py.t[:, :], UW.t[:, nch, 0:128], ds_.t[:, n8, :], (nch == 0), False))
                mms.append((py.t[:, :], UW.t[:, nch, 128:256], ds_.t[:, 8 + n8, :], False, (nch == 63)))
            mk.mmf([py], mms, reads=[UW, ds_])
            if u + 2 < len(units):
                load(u + 2)
            if ng == 7:
                mk.op("act", lambda: nc.scalar.copy(out=dtile.t[:, :], in_=py.t[:, :]), reads=[py], writes=[dtile])
                mk.dma("sp", cat_d[0:128, kt * 512:(kt + 1) * 512], dtile.t[:, :], reads=[dtile], writes=[catb])
            yield
    dgen = dft_units()
    next(dgen)

    def dft_step(n):
        for _ in range(n):
            try:
                next(dgen)
            except StopIteration:
                return
    for h in range(3):
        for ti, (c0, W, lat) in enumerate(tiles):
            a, cs = load_tile(c0, W, lat)
            pq = bank[ti % 2]
            proj(a, W, 128 + h * 128, pq)
            if lat:
                emit_qknorm_rope(mk, W, pq, gq.t[:, 0:1], QT.t[:, c0:c0 + W], QT, T, cs)
            else:
                emit_qknorm_rope(mk, W, pq, gq.t[:, 0:1], QC.t[:, 0:W], QC, T, None)
        qtiles = [(QT, t * 512, 512, t * 512, list(range(66))) for t in range(16)] + [(QC, 0, NCTX, NLAT, [64, 65])]
        for qi, (qbuf, q0, W, o0, kcs) in enumerate(qtiles):
            O = bank[3 + (qi % 2)]
            Dn = bank[5 + (qi % 2)]
            nk = len(kcs)

            def s_mm(idx):
                kc = kcs[idx]
                sb_ = bank[idx % 3]
                mk.mm([sb_], [(sb_.t[:, 0:W], KT.t[:, kc * 128:(kc + 1) * 128], qbuf.t[:, q0:q0 + W])], reads=[KT, qbuf],
                      start=True, stop=True)

            def e_act(idx):
                sb_ = bank[idx % 3]
                pt = PT[idx % 4]
                mk.op("act", lambda: nc.scalar.activation(out=pt.t[:, 0:W], in_=sb_.t[:, 0:W], func=AF.Exp),
                      reads=[sb_], writes=[pt])

            s_mm(0)
            if nk > 1:
                s_mm(1)
            e_act(0)
            for idx in range(nk):
                if nk > 2 and idx in (nk // 4, nk // 2, (3 * nk) // 4):
                    dft_step(1)
                if idx + 2 < nk:
                    s_mm(idx + 2)
                if idx + 1 < nk:
                    e_act(idx + 1)
                pt = PT[idx % 4]
                kc = kcs[idx]
                mk.mm([O, Dn], [(O.t[:, 0:W], V.t[:, kc, :], pt.t[:, 0:W]), (Dn.t[:, 0:W], T["ones"].t[:, :], pt.t[:, 0:W])],
                      reads=[pt, V, T["ones"]], start=(idx == 0), stop=(idx == nk - 1))
            mk.op("dve", lambda: nc.vector.reciprocal(out=rden.t[:, 0:W], in_=Dn.t[:, 0:W]), reads=[Dn], writes=[rden])
            ot = otile[qi % 2]
            mk.op("dve", lambda: nc.vector.tensor_tensor(out=ot.t[:, 0:W], in0=O.t[:, 0:W], in1=rden.t[:, 0:W], op=ALU.mult),
                  reads=[O, rden], writes=[ot])
            mk.dma("sp", cat_d[128 + h * 128:256 + h * 128, o0:o0 + W], ot.t[:, 0:W], reads=[ot], writes=[catb])
    dft_step(1000)
    dcnt = 0
    ds_ = dslot[0]
    for cs_ in range(2):
        mk.dma("sp", ds_.t[:, cs_ * 8:cs_ * 8 + 2, 0:256], dftC_d[cs_], reads=[io], writes=[ds_])
    py = bank[2]
    mms = []
    for n2 in range(2):
        mms.append((0, 2 * n2, py.t[:, 0:256], UW.t[:, 64 + n2, 0:128], ds_.t[:, n2, 0:256]))
        mms.append((0, 2 * n2 + 1, py.t[:, 0:256], UW.t[:, 64 + n2, 128:256], ds_.t[:, 8 + n2, 0:256]))
    mk.mm_multi([py], mms, reads=[UW, ds_], nk=4)
    ot = otile[0]
    mk.op("act", lambda: nc.scalar.copy(out=ot.t[:, 0:256], in_=py.t[:, 0:256]), reads=[py], writes=[ot])
    mk.dma("sp", cat_d[0:128, NLAT:NTOK], ot.t[:, 0:256], reads=[ot], writes=[catb])
    mk.finish([catb])
    return nc


_CONST = {}


def mixa0_consts():
    if "m0" in _CONST:
        return _CONST["m0"]
    bf = ml_dtypes.bfloat16
    rows = NLAT // 64
    row_id = np.repeat(np.arange(rows, dtype=np.float32), 64)
    col_id = np.tile(np.arange(64, dtype=np.float32), rows)
    inv_freq = (np.float32(10000.0) ** (-np.arange(0, 64, 2, dtype=np.float32) / np.float32(64))).astype(np.float32)
    ang = np.concatenate([row_id[:, None] * inv_freq, col_id[:, None] * inv_freq], axis=-1).astype(np.float32)
    cosL, sinL = np.cos(ang).astype(np.float32), np.sin(ang).astype(np.float32)
    pidx = np.arange(128)
    fidx = (pidx // 64) * 32 + (pidx % 32)
    cosT = np.ascontiguousarray(cosL[:, fidx].T)
    sinT = np.ascontiguousarray(sinL[:, fidx].T)
    rm = np.zeros((128, 128), np.float32)
    for p in range(128):
        if (p % 64) < 32:
            rm[p + 32, p] = -1.0
        else:
            rm[p - 32, p] = 1.0
    cc = np.arange(128)
    angc = 2 * np.pi * ((cc[:, None] * cc[None, :]) % 128) / 128.0
    Cc = np.cos(angc) / np.sqrt(128.0)
    Sc = np.sin(angc) / np.sqrt(128.0)
    cmat = np.stack([rm, Cc, Sc, np.zeros((128, 128))], 1).astype(bf)

    def dft(N):
        n = np.arange(N, dtype=np.int64)
        m = (n[:, None] * n[None, :]) % N
        a = m.astype(np.float32) * np.float32(2 * np.pi / N)
        s = np.float32(1.0 / np.sqrt(N))
        return (np.cos(a) * s).astype(bf), (-np.sin(a) * s).astype(bf)
    CL, SL = dft(NLAT)
    def layL(M):
        return M.reshape(8, 8, 128, 16, 512).transpose(3, 0, 2, 1, 4)
    dftL = np.ascontiguousarray(np.stack([layL(CL), layL(SL)], 2))
    Cx, Sx = dft(NCTX)
    layC = lambda M: M.reshape(2, 128, 256).transpose(1, 0, 2)
    dftC = np.ascontiguousarray(np.stack([layC(Cx), layC(Sx)], 0))
    _CONST["m0"] = dict(cosT=cosT, sinT=sinT, cmat=np.ascontiguousarray(cmat), dftL=dftL, dftC=dftC)
    return _CONST["m0"]


def mixa0_wsel(w_in, j):
    cols = np.concatenate([np.arange(j * 128, (j + 1) * 128),
                           512 + np.arange(3 * j * 128, (3 * j + 3) * 128),
                           2048 + np.arange(j * 128, (j + 1) * 128),
                           2560 + np.arange(j * 128, (j + 1) * 128)])
    w = w_in[:, cols]
    return np.ascontiguousarray(w.reshape(KC, 128, 768).transpose(1, 0, 2))


def run_mixa0(a_lat, a_ctx, ab_w_in, qk_norm, trace=False):
    C = mixa0_consts()
    nc = build_mixa0_launch()
    gqk = np.ascontiguousarray(qk_norm.T.astype(np.float32))
    aT = [np.ascontiguousarray(np.concatenate([a_lat[b], a_ctx[b]], 0).T) for b in range(2)]
    in_maps = []
    for cidx in range(NCORES):
        b, j = divmod(cidx, 4)
        in_maps.append(dict(aT=aT[b], wsel=mixa0_wsel(ab_w_in, j), gqk=gqk, cosT=C["cosT"], sinT=C["sinT"],
                            cmat=C["cmat"], dftL=C["dftL"], dftC=C["dftC"]))
    res = run_bass_kernel_spmd(nc, in_maps, core_ids=list(range(NCORES)), trace=trace)
    cat_lat = np.empty((2, NLAT, 2048), a_lat.dtype)
    cat_ctx = np.empty((2, NCTX, 2048), a_lat.dtype)
    for cidx in range(NCORES):
        b, j = divmod(cidx, 4)
        c = res.results[cidx]["cat"].T
        for dst, rows in ((cat_lat, slice(0, NLAT)), (cat_ctx, slice(NLAT, NTOK))):
            dst[b][:, j * 128:(j + 1) * 128] = c[rows, 0:128]
            dst[b][:, 512 + 3 * j * 128: 512 + (3 * j + 3) * 128] = c[rows, 128:512]
    return cat_lat, cat_ctx, res

CH = 64
NHL = 4


def build_mixa1_launch(NLAT=8192):
    NTOK = NLAT + NCTX
    nc = bass.Bass("TRN2", target_bir_lowering=False)
    mk = MK(nc)
    aT_d = nc.dram_tensor("aT", [D, NTOK], BF16, kind="ExternalInput").ap()
    w_d = nc.dram_tensor("w5", [5, 128, KC, 512], F32, kind="ExternalInput").ap()
    lbl_d = nc.dram_tensor("lbl", [128, 2, NHL], F32, kind="ExternalInput").ap()
    gon_d = nc.dram_tensor("gon", [128, 1], F32, kind="ExternalInput").ap()
    cm_d = nc.dram_tensor("cm", [128, 4, 128], BF16, kind="ExternalInput").ap()
    sm_d = nc.dram_tensor("sm", [128, 512], F32, kind="ExternalInput").ap()
    out_d = nc.dram_tensor("og", [NHL * 128, NLAT], BF16, kind="ExternalOutput").ap()
    ofw_d = nc.dram_tensor("ofw", [NHL, 128, NLAT], BF16, kind="Internal").ap()
    io = Buf(None, "io"); outb = Buf(None, "out"); ofwb = Buf(None, "ofw")
    W = mk.sb("W", [128, 4, KC, 512], BF16)
    aslot = [mk.sb(f"aslot{i}", [128, KC, 512], BF16) for i in range(2)]
    cm = mk.sb("cm", [128, 4, 128], BF16)
    sm = mk.sb("sm", [128, 512], F32)
    lbl = mk.sb("lbl", [128, 2, NHL], F32)
    lb = mk.sb("lb", [128, NHL], F32)
    oml = mk.sb("oml", [128, NHL], F32)
    gon = mk.sb("gon", [128, 1], F32)
    ones = mk.sb("ones", [128, 128], BF16)
    Vt = [mk.sb(f"Vt{i}", [128, 512], BF16) for i in range(8)]
    S32 = [mk.sb(f"S32_{h}", [128, 128], F32) for h in range(NHL)]
    Sbf = [[mk.sb(f"Sbf_{h}_{i}", [128, 128], BF16) for i in range(6)] for h in range(NHL)]
    F = {n: [mk.sb(f"{n}{i}", [128, 512], F32) for i in range(2)] for n in ("sg", "fg", "lf", "b", "eb", "enb", "kk", "qs")}
    QdT = [mk.sb(f"QdT{i}", [128, 512], BF16) for i in range(2)]
    KdT = [mk.sb(f"KdT{i}", [128, 512], BF16) for i in range(2)]
    Kdtok = [mk.sb(f"Kdtok{i}", [128, 128], BF16) for i in range(2)]
    ATm = [mk.sb(f"ATm{i}", [128, 128], BF16) for i in range(4)]
    tmpS = [mk.sb(f"tmpS{i}", [128, 128], F32) for i in range(2)]
    ofs = [mk.sb(f"ofs{i}", [128, 512], BF16) for i in range(2)]
    o32 = mk.sb("o32", [128, 512], F32)
    sq = mk.sb("sq", [128, 512], BF16)
    r32 = mk.sb("r32", [128, 512], F32)
    on32 = mk.sb("on32", [128, 512], F32)
    sgg = mk.sb("sgg", [128, 512], F32)
    otile = [mk.sb(f"otile{i}", [128, 512], BF16) for i in range(2)]
    cnt = {}

    def rot(lst, key):
        i = cnt.get(key, 0); cnt[key] = i + 1
        return lst[i % len(lst)]
    bank = [mk.ps(f"bank{i}", [128, 512], F32) for i in range(7)]
    bankT = mk.ps("bankT", [128, 1024], BF16)
    pO2 = [bank[0], bank[3]]; pP = [bank[1], bank[2]]; pV = bank[1]; pSSQ = bank[1]
    rA = [(bank[4], bank[4].t[:, 0:128]), (bank[5], bank[5].t[:, 0:128])]
    rS = [(bank[6], bank[6].t[:, 0:128]), (bank[6], bank[6].t[:, 128:256])]
    rT = [(bankT, bankT.t[:, 0:128])]
    mk.op("dve", lambda: nc.vector.memset(ones.t[:, :], 1.0), writes=[ones])
    mk.dma("sp", cm.t[:, :, :], cm_d, reads=[io], writes=[cm])
    mk.dma("sp", sm.t[:, :], sm_d, reads=[io], writes=[sm])
    mk.dma("sp", lbl.t[:, :, :], lbl_d, reads=[io], writes=[lbl])
    mk.dma("sp", gon.t[:, :], gon_d, reads=[io], writes=[gon])
    mk.op("dve", lambda: nc.vector.tensor_tensor(out=lb.t[:, :], in0=lbl.t[:, 1, :], in1=lbl.t[:, 0, :], op=ALU.subtract),
          reads=[lbl], writes=[lb])
    mk.op("act", lambda: nc.scalar.activation(out=lb.t[:, :], in_=lb.t[:, :], func=AF.Sigmoid), reads=[lb], writes=[lb])
    mk.op("dve", lambda: nc.vector.tensor_scalar(out=oml.t[:, :], in0=lb.t[:, :], scalar1=-1.0, scalar2=1.0, op0=ALU.mult,
                                                 op1=ALU.add), reads=[lb], writes=[oml])
    IDENT = cm.t[:, 2, :]

    for direction in (0, 1):
        fw = direction == 0
        secs = [0, 1, 3] if fw else [0, 2, 3, 4]
        for si, s in enumerate(secs):
            mk.dma("pool", W.t[:, si, :, :], w_d[s], reads=[io], writes=[W])
        for h in range(NHL):
            mk.op("dve", lambda: nc.vector.memset(S32[h].t[:, :], 0.0), writes=[S32[h]])
            mk.op("pool", lambda: nc.gpsimd.memset(Sbf[h][0].t[:, :], 0.0), writes=[Sbf[h][0]])
        sidx = [0] * NHL
        ctx_tiles = [(NLAT, NCTX, False)]
        lat_tiles = [(t * 512, 512, True) for t in range(NLAT // 512)]
        tiles = (ctx_tiles + lat_tiles) if fw else (ctx_tiles + lat_tiles[::-1])
        MASK = cm.t[:, 0, :] if fw else cm.t[:, 1, :]
        for (c0, Wd, lat) in tiles:
            nblk = Wd // 128
            a = rot(aslot, "a")
            mk.dma("sp", a.t[:, :, 0:Wd], aT_d[:, c0:c0 + Wd].rearrange("(kc p) t -> p kc t", p=128), reads=[io], writes=[a])
            vts = []
            for blk in range(nblk):
                mms = [(pV.t[:, :], a.t[:, kc, blk * 128:(blk + 1) * 128], W.t[:, 2, kc, :], kc == 0, kc == KC - 1) for kc in range(KC)]
                mk.mmf([pV], mms, reads=[a, W])
                vt = rot(Vt, "vt")
                mk.op("act", lambda: nc.scalar.copy(out=vt.t[:, :], in_=pV.t[:, :]), reads=[pV], writes=[vt])
                vts.append(vt)
            sl = slice(0, Wd)
            nch = Wd // CH
            blks = list(range(nblk)) if fw else list(range(nblk))[::-1]

            def stageA(h):
                pq = pP[0]; pf = pP[1]
                for (pb, si) in ((pq, 0), (pf, 1)):
                    mms = [(pb.t[:, 0:Wd], W.t[:, si, kc, h * 128:(h + 1) * 128], a.t[:, kc, 0:Wd], kc == 0, kc == KC - 1) for kc in range(KC)]
                    mk.mmf([pb], mms, reads=[a, W])
                t = {n: rot(F[n], n) for n in F}
                qd = rot(QdT, "qd"); kd = rot(KdT, "kd")
                mk.op("act", lambda: nc.scalar.activation(out=t["sg"].t[:, sl], in_=pf.t[:, sl], func=AF.Sigmoid), reads=[pf], writes=[t["sg"]])
                mk.op("act", lambda: nc.scalar.activation(out=t["qs"].t[:, sl], in_=pq.t[:, sl], func=AF.Silu), reads=[pq], writes=[t["qs"]])
                mk.op("dve", lambda: nc.vector.tensor_scalar(out=t["fg"].t[:, sl], in0=t["sg"].t[:, sl], scalar1=oml.t[:, h:h + 1],
                                                             scalar2=lb.t[:, h:h + 1], op0=ALU.mult, op1=ALU.add),
                      reads=[t["sg"], oml, lb], writes=[t["fg"]])
                mk.op("act", lambda: nc.scalar.activation(out=t["lf"].t[:, sl], in_=t["fg"].t[:, sl], func=AF.Ln), reads=[t["fg"]], writes=[t["lf"]])
                mk.op("dve", lambda: nc.vector.tensor_tensor_scan(out=t["b"].t[:, sl], data0=sm.t[:, sl], data1=t["lf"].t[:, sl],
                                                                  initial=0.0, op0=ALU.mult, op1=ALU.add),
                      reads=[sm, t["lf"]], writes=[t["b"]])
                if not fw:
                    b3 = t["b"].t[:, sl].rearrange("p (c t) -> p c t", t=CH)
                    mk.op("dve", lambda: nc.vector.tensor_tensor(out=t["lf"].t[:, sl], in0=t["lf"].t[:, sl], in1=t["b"].t[:, sl], op=ALU.subtract),
                          reads=[t["lf"], t["b"]], writes=[t["lf"]])
                    mk.op("dve", lambda: nc.vector.tensor_tensor(out=t["b"].t[:, sl].rearrange("p (c t) -> p c t", t=CH),
                                                                 in0=t["lf"].t[:, sl].rearrange("p (c t) -> p c t", t=CH),
                                                                 in1=b3[:, :, CH - 1:CH].to_broadcast([128, nch, CH]), op=ALU.add),
                          reads=[t["lf"], t["b"]], writes=[t["b"]])
                mk.op("act", lambda: nc.scalar.activation(out=t["eb"].t[:, sl], in_=t["b"].t[:, sl], func=AF.Exp), reads=[t["b"]], writes=[t["eb"]])
                mk.op("act", lambda: nc.scalar.activation(out=t["enb"].t[:, sl], in_=t["b"].t[:, sl], func=AF.Exp, scale=-1.0), reads=[t["b"]], writes=[t["enb"]])
                mk.op("pool", lambda: nc.gpsimd.tensor_scalar(out=t["kk"].t[:, sl], in0=t["fg"].t[:, sl], scalar1=-1.0, scalar2=1.0,
                                                              op0=ALU.mult, op1=ALU.add), reads=[t["fg"]], writes=[t["kk"]])
                mk.op("dve", lambda: nc.vector.tensor_tensor(out=qd.t[:, sl], in0=t["qs"].t[:, sl], in1=t["eb"].t[:, sl], op=ALU.mult),
                      reads=[t["qs"], t["eb"]], writes=[qd])
                mk.op("dve", lambda: nc.vector.tensor_tensor(out=kd.t[:, sl], in0=t["kk"].t[:, sl], in1=t["enb"].t[:, sl], op=ALU.mult),
                      reads=[t["kk"], t["enb"]], writes=[kd])
                return dict(qd=qd, kd=kd, eb=t["eb"])

            def block_ops(h, blk, A_, pend, pOh):
                qd, kd, eb = A_["qd"], A_["kd"], A_["eb"]
                bs = slice(blk * 128, (blk + 1) * 128)
                vt = vts[blk]
                rtb, rtap = rot(rT, "rt"); kt_ = rot(Kdtok, "kt")
                mk.op("pe", lambda: nc.tensor.transpose(out=rtap, in_=kd.t[:, bs], identity=IDENT), reads=[kd, cm], writes=[rtb])
                mk.op("act", lambda: nc.scalar.copy(out=kt_.t[:, :], in_=rtap), reads=[rtb], writes=[kt_])
                rab, raap = rot(rA, "ra"); atm = rot(ATm, "atm")
                mk.mmf([rab], [(raap, kd.t[:, bs], qd.t[:, bs], True, True)], reads=[kd, qd])
                mk.op("dve", lambda: nc.vector.tensor_tensor(out=atm.t[:, :], in0=raap, in1=MASK, op=ALU.mult), reads=[rab, cm], writes=[atm])
                chunks = [0, 1] if fw else [1, 0]
                s_before = []
                for ci in chunks:
                    s_before.append(Sbf[h][sidx[h]])
                    rsb, rsap = rot(rS, "rs"); tS = rot(tmpS, "ts")
                    rows = slice(ci * CH, (ci + 1) * CH)
                    mk.mmf([rsb], [(rsap, kt_.t[rows, :], vt.t[rows, h * 128:(h + 1) * 128], True, True)], reads=[kt_, vt])
                    cglob = blk * 2 + ci
                    col = cglob * CH + (CH - 1 if fw else 0)
                    et = eb.t[:, col:col + 1]
                    mk.op("act", lambda: nc.scalar.activation(out=tS.t[:, :], in_=rsap, func=AF.Copy, scale=et), reads=[rsb, eb], writes=[tS])
                    mk.op("dve", lambda: nc.vector.scalar_tensor_tensor(out=S32[h].t[:, :], in0=S32[h].t[:, :], scalar=et, in1=tS.t[:, :],
                                                                        op0=ALU.mult, op1=ALU.add), reads=[S32[h], tS, eb], writes=[S32[h]])
                    sidx[h] = (sidx[h] + 1) % 6
                    nxt = Sbf[h][sidx[h]]
                    mk.op("pool", lambda: nc.gpsimd.tensor_copy(out=nxt.t[:, :], in_=S32[h].t[:, :]), reads=[S32[h]], writes=[nxt])
                if lat:
                    def omm():
                        mms = [(pOh.t[:, bs], vt.t[:, h * 128:(h + 1) * 128], atm.t[:, :], True, False)]
                        for n_, ci in enumerate(chunks):
                            cs_ = slice(blk * 128 + ci * CH, blk * 128 + (ci + 1) * CH)
                            mms.append((pOh.t[:, cs_], s_before[n_].t[:, :], qd.t[:, cs_], False, n_ == 1))
                        mk.mmf([pOh], mms, reads=[vt, atm, qd] + s_before)
                    pend.append(omm)
                    if len(pend) > 1:
                        pend.pop(0)()

            def evac(h, pOh):
                if fw:
                    of = rot(ofs, "of")
                    mk.op("act", lambda: nc.scalar.copy(out=of.t[:, :], in_=pOh.t[:, :]), reads=[pOh], writes=[of])
                    mk.dma("sp", ofw_d[h][:, c0:c0 + 512], of.t[:, :], reads=[of], writes=[ofwb])
                    return
                of = rot(ofs, "of")
                mk.dma("sp", of.t[:, :], ofw_d[h][:, c0:c0 + 512], reads=[ofwb], writes=[of])
                mk.op("dve", lambda: nc.vector.tensor_tensor(out=o32.t[:, :], in0=pOh.t[:, :], in1=of.t[:, :], op=ALU.add),
                      reads=[pOh, of], writes=[o32])
                pg = pP[1]
                mms = [(pg.t[:, :], W.t[:, 3, kc, h * 128:(h + 1) * 128], a.t[:, kc, :], kc == 0, kc == KC - 1) for kc in range(KC)]
                mk.mmf([pg], mms, reads=[a, W])
                mk.op("act", lambda: nc.scalar.activation(out=sgg.t[:, :], in_=pg.t[:, :], func=AF.Sigmoid), reads=[pg], writes=[sgg])
                mk.op("act", lambda: nc.scalar.activation(out=sq.t[:, :], in_=o32.t[:, :], func=AF.Square), reads=[o32], writes=[sq])
                mk.mmf([pSSQ], [(pSSQ.t[:, :], ones.t[:, :], sq.t[:, :], True, True)], reads=[ones, sq])
                mk.op("dve", lambda: nc.vector.tensor_scalar(out=r32.t[:, :], in0=pSSQ.t[:, :], scalar1=1.0 / 128, scalar2=EPS,
                                                             op0=ALU.mult, op1=ALU.add), reads=[pSSQ], writes=[r32])
                mk.op("act", lambda: nc.scalar.activation(out=r32.t[:, :], in_=r32.t[:, :], func=AF.Sqrt), reads=[r32], writes=[r32])
                mk.op("dve", lambda: nc.vector.reciprocal(out=r32.t[:, :], in_=r32.t[:, :]), reads=[r32], writes=[r32])
                mk.op("dve", lambda: nc.vector.scalar_tensor_tensor(out=on32.t[:, :], in0=o32.t[:, :], scalar=gon.t[:, 0:1], in1=r32.t[:, :],
                                                                    op0=ALU.mult, op1=ALU.mult), reads=[o32, gon, r32], writes=[on32])
                ot = rot(otile, "ot")
                mk.op("dve", lambda: nc.vector.tensor_tensor(out=ot.t[:, :], in0=on32.t[:, :], in1=sgg.t[:, :], op=ALU.mult),
                      reads=[on32, sgg], writes=[ot])
                mk.dma("sp", out_d[h * 128:(h + 1) * 128, c0:c0 + 512], ot.t[:, :], reads=[ot], writes=[outb])

            for hp in range(0, NHL, 2):
                hs = [hp, hp + 1]
                A = {h: stageA(h) for h in hs}
                pend = {h: [] for h in hs}
                for blk in blks:
                    for k_, h in enumerate(hs):
                        block_ops(h, blk, A[h], pend[h], pO2[k_])
                if lat:
                    for k_, h in enumerate(hs):
                        for f_ in pend[h]:
                            f_()
                        evac(h, pO2[k_])
    mk.finish([outb])
    return nc


def mixa1_consts():
    if "m1" in _CONST:
        return _CONST["m1"]
    bf = ml_dtypes.bfloat16
    s = np.arange(128)[:, None]; t = np.arange(128)[None, :]
    same = (s // CH) == (t // CH)
    mfw = (same & (s <= t)).astype(np.float32)
    mbw = (same & (s >= t)).astype(np.float32)
    cm = np.stack([mfw, mbw, np.eye(128, dtype=np.float32), np.zeros((128, 128), np.float32)], 1).astype(bf)
    smr = np.ones((128, 512), np.float32); smr[:, ::CH] = 0.0
    _CONST["m1"] = dict(cm=np.ascontiguousarray(cm), sm=smr)
    return _CONST["m1"]


def run_mixa1(a_lat, a_ctx, hgrn_w_in, lb_logits, o_norm, trace=False):
    C = mixa1_consts()
    nc = build_mixa1_launch()
    aT = [np.ascontiguousarray(np.concatenate([a_lat[b], a_ctx[b]], 0).T) for b in range(2)]
    gon = np.ascontiguousarray(o_norm.reshape(128, 1).astype(np.float32))
    in_maps = []
    for cidx in range(NCORES):
        b, j = divmod(cidx, 4)
        cols = np.arange(j * 512, (j + 1) * 512)
        w5 = np.stack([hgrn_w_in[:, s * 2048 + cols].reshape(KC, 128, 512).transpose(1, 0, 2) for s in range(5)], 0)
        lbl = lb_logits[:, cols].reshape(2, NHL, 128).transpose(2, 0, 1)
        in_maps.append(dict(aT=aT[b], w5=np.ascontiguousarray(w5), lbl=np.ascontiguousarray(lbl), gon=gon, cm=C["cm"], sm=C["sm"]))
    res = run_bass_kernel_spmd(nc, in_maps, core_ids=list(range(NCORES)), trace=trace)
    og = np.empty((2, NLAT, 2048), a_lat.dtype)
    for cidx in range(NCORES):
        b, j = divmod(cidx, 4)
        og[b][:, j * 512:(j + 1) * 512] = res.results[cidx]["og"].T
    return og, res


def _run(nc, in_maps):
    return run_bass_kernel_spmd(nc, in_maps, core_ids=list(range(NCORES))).results


def pack_lat(lat):
    return [np.ascontiguousarray(lat[c // 4, (c % 4) * 2048:(c % 4 + 1) * 2048].T) for c in range(NCORES)]


def kernel(x, c, ctx, c_ctx, w_mod, b_mod, norm_gains, ffn_w_in, ffn_w_out, ab_w_in, qk_norm, ab_w_out,
           hgrn_w_in, hgrn_lb_logits, hgrn_o_norm, hgrn_w_out, final_norm):
    f32 = np.float32
    x, c, ctx, c_ctx = (np.asarray(a, f32) for a in (x, c, ctx, c_ctx))
    mod = run_mod(c, c_ctx, np.asarray(w_mod, f32), np.asarray(b_mod, f32))
    norm_gains = np.asarray(norm_gains, f32)
    nc = build_tok_launch(PWL, SUBT_L, 1, [("ffn", 0, 0), ("normout", 0)])
    hT = pack_tokens(x, ctx)
    w_in = lay_w_in(np.asarray(ffn_w_in[0, 0], f32)); w_out = lay_w_out(np.asarray(ffn_w_out[0, 0], f32), NH_FFN)
    g0 = gains_for(norm_gains, [0])
    res = _run(nc, [{"hT": hT[k], "modT": modT_for_core(mod, [0], k // 4), "gains": g0, "w0_in": w_in, "w0_out": w_out}
                    for k in range(NCORES)])
    hT = [r["hT_out"] for r in res]
    a_lat, a_ctx = unpack_tokens([r["aT_out"] for r in res], D)
    cat_lat, cat_ctx, _ = run_mixa0(a_lat, a_ctx, np.asarray(ab_w_in[0], f32), np.asarray(qk_norm[0], f32))
    nc = build_tok_launch(PWL, SUBT_L, 2, [("proj", 0), ("ffn", 0, 1), ("ffn", 1, 0), ("normout", 1)])
    cat = pack_tokens(cat_lat, cat_ctx)
    wp = lay_w_out(np.asarray(ab_w_out[0], f32), 1)[0]
    w1_in = lay_w_in(np.asarray(ffn_w_in[0, 1], f32)); w1_out = lay_w_out(np.asarray(ffn_w_out[0, 1], f32), NH_FFN)
    w2_in = lay_w_in(np.asarray(ffn_w_in[1, 0], f32)); w2_out = lay_w_out(np.asarray(ffn_w_out[1, 0], f32), NH_FFN)
    g01 = gains_for(norm_gains, [0, 1])
    res = _run(nc, [{"hT": hT[k], "modT": modT_for_core(mod, [0, 1], k // 4), "gains": g01, "w0_proj": wp, "cat0": cat[k],
                     "w1_in": w1_in, "w1_out": w1_out, "w2_in": w2_in, "w2_out": w2_out} for k in range(NCORES)])
    h_lat, _ = unpack_tokens([r["hT_out"] for r in res], D)
    a_lat, a_ctx = unpack_tokens([r["aT_out"] for r in res], D)
    og, _ = run_mixa1(a_lat, a_ctx, np.asarray(hgrn_w_in[0], f32), np.asarray(hgrn_lb_logits, f32), np.asarray(hgrn_o_norm[0], f32))
    nc = build_tok_launch(1024, subtiles_of(1024, 0), 1, [("proj", 0), ("ffn", 0, 1), ("final",)])
    hT = pack_lat(h_lat)
    ogT = pack_lat(og)
    wp = lay_w_out(np.asarray(hgrn_w_out[0], f32), 1)[0]
    w1_in = lay_w_in(np.asarray(ffn_w_in[1, 1], f32)); w1_out = lay_w_out(np.asarray(ffn_w_out[1, 1], f32), NH_FFN)
    g1 = gains_for(norm_gains, [1])
    fng = lay_vec(np.asarray(final_norm, f32))
    res = _run(nc, [{"hT": hT[k], "modT": modT_for_core(mod, [1], k // 4), "gains": g1, "w0_proj": wp, "cat0": ogT[k],
                     "w1_in": w1_in, "w1_out": w1_out, "fng": fng} for k in range(NCORES)])
    out = np.empty((2, 8192, D), f32)
    for k in range(NCORES):
        out[k // 4, (k % 4) * 2048:(k % 4 + 1) * 2048] = res[k]["yT"].T
    return out
```

```python
import numpy as np
import concourse.bass as bass
import concourse.mybir as mybir

F32 = mybir.dt.float32
BF16 = mybir.dt.bfloat16
I32 = mybir.dt.int32
AF = mybir.ActivationFunctionType
ALU = mybir.AluOpType

EPOCH = 24000
DMA_RING = 8


class Buf:
    __slots__ = ("t", "name", "lw", "rd")

    def __init__(self, t, name=""):
        self.t = t
        self.name = name
        self.lw = None
        self.rd = {}

    def __getitem__(self, idx):
        return self.t[idx]


class MK:
    def __init__(self, nc, same_engine_sync=True):
        self.nc = nc
        self.E = {"pe": nc.tensor, "act": nc.scalar, "dve": nc.vector, "pool": nc.gpsimd, "sp": nc.sync}
        self.cur_sem = {}
        self.cur_cnt = {}
        self.seen = {e: {} for e in self.E}
        self.same_engine_sync = same_engine_sync
        self.ring = {}
        self.ring_i = {}
        self.nsem = 0
        self._uid = 0
        self.all_dma_events = []

    def new_sem(self, name):
        self.nsem += 1
        cm = self.nc.semaphore(f"{name}_{self.nsem}")
        return cm.__enter__()

    def sb(self, name, shape, dtype):
        self._uid += 1
        return Buf(self.nc.alloc_sbuf_tensor(f"{name}_{self._uid}", list(shape), dtype), name)

    def ps(self, name, shape, dtype=F32):
        self._uid += 1
        return Buf(self.nc.alloc_psum_tensor(f"{name}_{self._uid}", list(shape), dtype), name)

    def dram(self, name, shape, dtype, kind="Internal"):
        return Buf(self.nc.dram_tensor(name, list(shape), dtype, kind=kind), name)

    def _need(self, eng, ev):
        if ev is None:
            return
        sem, val = ev
        seen = self.seen[eng]
        k = id(sem)
        if seen.get(k, (None, 0))[1] >= val:
            return
        self.E[eng].wait_ge(sem, val)
        seen[k] = (sem, val)

    def _deps(self, eng, reads, writes, skip_self=False):
        for b in reads:
            if b.lw is not None:
                self._need(eng, b.lw)
        for b in writes:
            if b.lw is not None:
                self._need(eng, b.lw)
            for ev in b.rd.values():
                self._need(eng, ev)

    def _tick(self, eng):
        if eng not in self.cur_sem or self.cur_cnt[eng] >= EPOCH:
            self.cur_sem[eng] = self.new_sem(f"s_{eng}")
            self.cur_cnt[eng] = 0
        self.cur_cnt[eng] += 1
        return (self.cur_sem[eng], self.cur_cnt[eng])

    def _commit(self, ev, reads, writes):
        for b in writes:
            b.lw = ev
            b.rd = {}
        for b in reads:
            if b not in writes:
                b.rd[id(ev[0])] = ev

    def op(self, eng, fn, reads=(), writes=(), track=True):
        self._deps(eng, reads, writes)
        ins = fn()
        if track:
            ev = self._tick(eng)
            ins.then_inc(ev[0], 1)
            if not self.same_engine_sync:
                self.seen[eng][id(ev[0])] = ev
            self._commit(ev, reads, writes)
        return ins

    def mm_group(self, out_buf, mms, reads):
        self._deps("pe", reads, [out_buf])
        n = len(mms)
        ins = None
        for i, (o, l, r) in enumerate(mms):
            ins = self.nc.tensor.matmul(o, l, r, start=(i == 0), stop=(i == n - 1))
        ev = self._tick("pe")
        ins.then_inc(ev[0], 1)
        self.seen["pe"][id(ev[0])] = ev
        self._commit(ev, reads, [out_buf])

    def dma(self, q, out_ap, in_ap, reads=(), writes=(), **kw):
        self._deps(q, reads, writes)
        ring = self.ring.setdefault(q, [[None, 0] for _ in range(DMA_RING)])
        i = self.ring_i.get(q, 0)
        self.ring_i[q] = (i + 1) % DMA_RING
        slot = ring[i]
        if slot[0] is None or slot[1] >= EPOCH:
            if slot[0] is not None:
                self._need(q, (slot[0], slot[1]))
            slot[0] = self.new_sem(f"d_{q}{i}")
            slot[1] = 0
        elif slot[1] > 0:
            self._need(q, (slot[0], slot[1]))
        ins = self.E[q].dma_start(out=out_ap, in_=in_ap, **kw)
        slot[1] += 16
        ev = (slot[0], slot[1])
        ins.then_inc(ev[0], 16)
        self._commit(ev, reads, writes)
        self.all_dma_events.append((q, ev))
        return ev

    def finish(self, out_bufs):
        for b in out_bufs:
            if b.lw is not None:
                self._need("sp", b.lw)
        for q, ring in self.ring.items():
            for sem, cnt in ring:
                if sem is not None and cnt > 0:
                    self._need("sp", (sem, cnt))


def _mm(self, out_bufs, mms, reads, start, stop):
    self._deps("pe", reads, out_bufs)
    ins = None
    for (o, l, r) in mms:
        ins = self.nc.tensor.matmul(o, l, r, start=start, stop=stop)
    ev = self._tick("pe")
    ins.then_inc(ev[0], 1)
    self.seen["pe"][id(ev[0])] = ev
    self._commit(ev, reads, out_bufs)


def _mm_multi(self, out_bufs, mms, reads, nk):
    self._deps("pe", reads, out_bufs)
    ins = None
    for (_, ki, o, l, r) in mms:
        ins = self.nc.tensor.matmul(o, l, r, start=(ki == 0), stop=(ki == nk - 1))
    ev = self._tick("pe")
    ins.then_inc(ev[0], 1)
    self.seen["pe"][id(ev[0])] = ev
    self._commit(ev, reads, out_bufs)


MK.mm = _mm
MK.mm_multi = _mm_multi


def _mmf(self, out_bufs, mms, reads):
    self._deps("pe", reads, out_bufs)
    ins = None
    for (o, l, r, st, sp) in mms:
        ins = self.nc.tensor.matmul(o, l, r, start=st, stop=sp)
    ev = self._tick("pe")
    ins.then_inc(ev[0], 1)
    self.seen["pe"][id(ev[0])] = ev
    self._commit(ev, reads, out_bufs)


def _barrier(self):
    evs = [(self.cur_sem[e], self.cur_cnt[e]) for e in self.cur_sem]
    for q, ring in self.ring.items():
        for sem, cnt in ring:
            if sem is not None and cnt > 0:
                evs.append((sem, cnt))
    for e in self.E:
        for ev in evs:
            self._need(e, ev)


MK.mmf = _mmf
MK.barrier = _barrier

D = 2048
KC = 16
FFN = 5632
EPS = 1e-6


def subtiles_of(ncols_lat, ncols_ctx):
    st = []
    off = 0
    while off < ncols_lat:
        sz = min(512, ncols_lat - off)
        st.append((off, sz, 0))
        off += sz
    if ncols_ctx:
        st.append((off, ncols_ctx, 1))
    return st


class Pools:
    def __init__(self, mk, PW, JH):
        self.PW = PW
        self.JH = JH
        self.aT = mk.sb("aT", [128, KC, PW], BF16)
        self.gT = [mk.sb(f"gT{j}", [128, PW], BF16) for j in range(JH)]
        self.hst = [mk.sb(f"hst{i}", [128, PW], F32) for i in range(3)]
        self.sq = [mk.sb(f"sq{i}", [128, PW], BF16) for i in range(2)]
        self.R = mk.sb("R", [128, PW], F32)
        self.tmp = [mk.sb(f"tmp{i}", [128, PW], F32) for i in range(2)]
        self.win = [mk.sb(f"win{i}", [128, KC, 256], BF16) for i in range(3)]
        self.wout = [mk.sb(f"wout{i}", [128, JH, 128], BF16) for i in range(2)]
        self.ones = mk.sb("ones", [128, 128], BF16)
        self.bank = [mk.ps(f"bank{i}", [128, 512], F32) for i in range(8)]
        self.cnt = {"hst": 0, "sq": 0, "tmp": 0, "win": 0, "wout": 0}
        mk.op("dve", lambda: mk.nc.vector.memset(self.ones.t[:, :], 1.0), writes=[self.ones])

    def nxt(self, kind):
        lst = getattr(self, kind)
        i = self.cnt[kind]
        self.cnt[kind] = i + 1
        return lst[i % len(lst)]


def emit_norm(mk, P, src_ap_fn, src_bufs, col0, subt, S_ap_fn, T_ap_fn, out_fn, post_fn=None):
    nc = mk.nc
    PW = sum(s[1] for s in subt)
    nst = len(subt)
    banks = P.bank[0:nst]
    for i in range(KC):
        hs = P.nxt("hst")
        mk.dma("sp", hs.t[:, 0:PW], src_ap_fn(i), reads=[src_bufs[i]], writes=[hs])
        sq = P.nxt("sq")
        mk.op("act", lambda: nc.scalar.activation(out=sq.t[:, 0:PW], in_=hs.t[:, 0:PW], func=AF.Square),
              reads=[hs], writes=[sq])
        mms = [(banks[k].t[:, 0:sz], P.ones.t[:, :], sq.t[:, off:off + sz]) for k, (off, sz, _) in enumerate(subt)]
        mk.mm(banks, mms, reads=[sq, P.ones], start=(i == 0), stop=(i == KC - 1))
    for k, (off, sz, _) in enumerate(subt):
        mk.op("dve", lambda: nc.vector.tensor_scalar(out=P.R.t[:, off:off + sz], in0=banks[k].t[:, 0:sz],
                                                     scalar1=1.0 / D, scalar2=EPS, op0=ALU.mult, op1=ALU.add),
              reads=[banks[k]], writes=[P.R])
    mk.op("act", lambda: nc.scalar.activation(out=P.R.t[:, 0:PW], in_=P.R.t[:, 0:PW], func=AF.Sqrt),
          reads=[P.R], writes=[P.R])
    mk.op("dve", lambda: nc.vector.reciprocal(out=P.R.t[:, 0:PW], in_=P.R.t[:, 0:PW]), reads=[P.R], writes=[P.R])
    for i in range(KC):
        hs = P.nxt("hst")
        mk.dma("sp", hs.t[:, 0:PW], src_ap_fn(i), reads=[src_bufs[i]], writes=[hs])
        tm = P.nxt("tmp")
        mk.op("dve", lambda: nc.vector.tensor_tensor(out=tm.t[:, 0:PW], in0=hs.t[:, 0:PW], in1=P.R.t[:, 0:PW],
                                                     op=ALU.mult), reads=[hs, P.R], writes=[tm])
        groups = {}
        for (off, sz, mc) in subt:
            if mc in groups and groups[mc][0] + groups[mc][1] == off:
                groups[mc][1] += sz
            else:
                groups[mc] = [off, sz]
        for mc, (off, sz) in groups.items():
            o_ap, o_buf = out_fn(i, off, sz)
            Tb = T_ap_fn(i, mc) if T_ap_fn is not None else 0.0
            mk.op("act", lambda: nc.scalar.activation(out=o_ap, in_=tm.t[:, off:off + sz], func=AF.Identity,
                                                      bias=Tb, scale=S_ap_fn(i, mc)),
                  reads=[tm], writes=[o_buf])
        if post_fn is not None:
            post_fn(i)


def emit_ffn(mk, P, src_ap_fn, dst_ap_fn, hbufs_src, hbufs_dst, subt, S_fn, T_fn, G_fn,
             w_in_d, w_out_d, J, NH, wbuf):
    nc = mk.nc
    PW = sum(s[1] for s in subt)
    nst = len(subt)
    JH = J // NH
    assert nst <= 3
    emit_norm(mk, P, src_ap_fn, hbufs_src, 0, subt, S_fn, T_fn,
              lambda i, off, sz: (P.aT.t[:, i, off:off + sz], P.aT))
    setA = P.bank[0:nst]
    setB = P.bank[3:3 + nst]
    for hf in range(NH):
        for jj in range(JH):
            j = hf * JH + jj
            w = P.nxt("win")
            mk.dma("pool", w.t[:, :, :], w_in_d[j], reads=[wbuf], writes=[w])
            for part, pset in ((0, setA), (1, setB)):
                mms = []
                for kc in range(KC):
                    for k, (off, sz, _) in enumerate(subt):
                        mms.append((k, kc, pset[k].t[:, 0:sz], w.t[:, kc, part * 128:(part + 1) * 128],
                                    P.aT.t[:, kc, off:off + sz]))
                mk.mm_multi(pset, mms, reads=[w, P.aT], nk=KC)
            tm = P.nxt("tmp")
            for k, (off, sz, _) in enumerate(subt):
                mk.op("act", lambda: nc.scalar.activation(out=tm.t[:, off:off + sz], in_=setA[k].t[:, 0:sz],
                                                          func=AF.Silu), reads=[setA[k]], writes=[tm])
            for k, (off, sz, _) in enumerate(subt):
                mk.op("dve", lambda: nc.vector.tensor_tensor(out=P.gT[jj].t[:, off:off + sz], in0=tm.t[:, off:off + sz],
                                                             in1=setB[k].t[:, 0:sz], op=ALU.mult),
                      reads=[tm, setB[k]], writes=[P.gT[jj]])
        emit_outproj(mk, P, lambda jj, off, sz: P.gT[jj].t[:, off:off + sz], P.gT[0:JH], JH,
                     lambda i: w_out_d[hf, i], wbuf,
                     (src_ap_fn if hf == 0 else dst_ap_fn), (hbufs_src if hf == 0 else hbufs_dst),
                     dst_ap_fn, hbufs_dst, subt, G_fn)


def emit_outproj(mk, P, rhs_fn, rhs_bufs, nk, w_d_fn, wbuf, src_ap_fn, hbufs_src, dst_ap_fn, hbufs_dst, subt, G_fn):
    nc = mk.nc
    PW = sum(s[1] for s in subt)
    nst = len(subt)
    setA = P.bank[0:nst]
    setB = P.bank[3:3 + nst]
    for i in range(KC):
        wo = P.nxt("wout")
        mk.dma("pool", wo.t[:, 0:nk, :], w_d_fn(i), reads=[wbuf], writes=[wo])
        pset = setA if (i % 2 == 0) else setB
        mms = []
        for jj in range(nk):
            for k, (off, sz, _) in enumerate(subt):
                mms.append((k, jj, pset[k].t[:, 0:sz], wo.t[:, jj, :], rhs_fn(jj, off, sz)))
        mk.mm_multi(pset, mms, reads=[wo] + list(rhs_bufs), nk=nk)
        hs = P.nxt("hst")
        mk.dma("sp", hs.t[:, 0:PW], src_ap_fn(i), reads=[hbufs_src[i]], writes=[hs])
        for k, (off, sz, mc) in enumerate(subt):
            mk.op("dve", lambda: nc.vector.scalar_tensor_tensor(out=hs.t[:, off:off + sz], in0=pset[k].t[:, 0:sz],
                                                                scalar=G_fn(i, mc), in1=hs.t[:, off:off + sz],
                                                                op0=ALU.mult, op1=ALU.add),
                  reads=[pset[k], hs], writes=[hs])
        mk.dma("sp", dst_ap_fn(i), hs.t[:, 0:PW], reads=[hs], writes=[hbufs_dst[i]])


def emit_prep_mod(mk, mod_sb, gain_sb, S_sb, G_sb):
    nc = mk.nc
    for k in range(3):
        for mc in range(2):
            mk.op("dve", lambda: nc.vector.scalar_tensor_tensor(
                out=S_sb.t[:, k, :, mc], in0=mod_sb.t[:, (3 * k + 1) * KC:(3 * k + 2) * KC, mc], scalar=1.0,
                in1=gain_sb.t[:, k, :], op0=ALU.add, op1=ALU.mult), reads=[mod_sb, gain_sb], writes=[S_sb])
            mk.op("dve", lambda: nc.vector.tensor_scalar(
                out=G_sb.t[:, k, :, mc], in0=mod_sb.t[:, (3 * k + 2) * KC:(3 * k + 3) * KC, mc],
                scalar1=(1.0 if k == 1 else 0.5), scalar2=None, op0=ALU.mult), reads=[mod_sb], writes=[G_sb])


def emit_mod(mk, P, cT_d, w_d, b_d, out_d, nch, wbuf):
    nc = mk.nc
    c_sb = mk.sb("mod_c", [128, KC, 3], F32)
    sc_sb = mk.sb("mod_sc", [128, KC, 3], F32)
    wsl = [mk.sb(f"mod_w{i}", [128, KC, 128], F32) for i in range(3)]
    b_sb = mk.sb("mod_b", [128, nch], F32)
    o_sb = mk.sb("mod_o", [128, nch, 3], F32)
    mk.dma("sp", c_sb.t[:, :, :], cT_d, reads=[wbuf], writes=[c_sb])
    mk.dma("sp", b_sb.t[:, :], b_d, reads=[wbuf], writes=[b_sb])
    mk.op("act", lambda: nc.scalar.activation(out=sc_sb.t[:, :, :], in_=c_sb.t[:, :, :], func=AF.Silu),
          reads=[c_sb], writes=[sc_sb])
    for ch in range(nch):
        wo = wsl[ch % 3]
        mk.dma("sp", wo.t[:, 0:KC, :], w_d[ch], reads=[wbuf], writes=[wo])
        bk = P.bank[ch % 8]
        mms = [(0, kc, bk.t[:, 0:3], wo.t[:, kc, :], sc_sb.t[:, kc, :]) for kc in range(KC)]
        mk.mm_multi([bk], mms, reads=[wo, sc_sb], nk=KC)
        mk.op("act", lambda: nc.scalar.activation(out=o_sb.t[:, ch, :], in_=bk.t[:, 0:3], func=AF.Identity,
                                                  bias=b_sb.t[:, ch:ch + 1], scale=1.0),
              reads=[bk, b_sb], writes=[o_sb])
    mk.dma("sp", out_d, o_sb.t[:, :, :], reads=[o_sb], writes=[wbuf])

import ml_dtypes
from concourse.bass_utils import run_bass_kernel_spmd

NCORES = 8
NL = 2
J_FFN = FFN // 128
NH_FFN = 2
NPASS = 2


def build_tok_launch(PW, subt, n_layers, steps, out_h=True):
    nc = bass.Bass("TRN2", target_bir_lowering=False)
    mk = MK(nc)
    JH = J_FFN // NH_FFN
    P = Pools(mk, PW, JH)
    NT = NPASS * PW
    hin = nc.dram_tensor("hT", [D, NT], F32, kind="ExternalInput").ap()
    hout = nc.dram_tensor("hT_out", [D, NT], F32, kind=("ExternalOutput" if out_h else "Internal")).ap()
    modT = nc.dram_tensor("modT", [n_layers, 128, 144, 2], F32, kind="ExternalInput").ap()
    gains = nc.dram_tensor("gains", [n_layers, 128, 3, KC], F32, kind="ExternalInput").ap()
    wbuf = Buf(None, "weights")
    mod_sb, S_sb, G_sb = [], [], []
    for li in range(n_layers):
        m = mk.sb(f"mod{li}", [128, 144, 2], F32)
        g = mk.sb(f"gain{li}", [128, 3, KC], F32)
        S = mk.sb(f"S{li}", [128, 3, KC, 2], F32)
        G = mk.sb(f"G{li}", [128, 3, KC, 2], F32)
        mk.dma("sp", m.t[:, :, :], modT[li], reads=[wbuf], writes=[m])
        mk.dma("sp", g.t[:, :, :], gains[li], reads=[wbuf], writes=[g])
        emit_prep_mod(mk, m, g, S, G)
        mod_sb.append(m); S_sb.append(S); G_sb.append(G)
    hb_in = [[Buf(None, f"hi{p}_{i}") for i in range(KC)] for p in range(NPASS)]
    hb_out = [[Buf(None, f"ho{p}_{i}") for i in range(KC)] for p in range(NPASS)]
    outs = [b for p in range(NPASS) for b in hb_out[p]]
    in_place = False
    for si, st in enumerate(steps):
        kind = st[0]
        if kind == "ffn":
            _, li, f, = st
            w_in_d = nc.dram_tensor(f"w{si}_in", [J_FFN, 128, KC, 256], F32, kind="ExternalInput").ap()
            w_out_d = nc.dram_tensor(f"w{si}_out", [NH_FFN, KC, 128, JH, 128], F32, kind="ExternalInput").ap()
            k = 0 if f == 0 else 2
        elif kind == "proj":
            _, li = st
            w_p_d = nc.dram_tensor(f"w{si}_proj", [KC, 128, KC, 128], F32, kind="ExternalInput").ap()
            cat_d = nc.dram_tensor(f"cat{si}", [D, NT], BF16, kind="ExternalInput").ap()
            catb = Buf(None, "cat")
        elif kind == "normout":
            _, li = st
            aT_d = nc.dram_tensor("aT_out", [D, NT], BF16, kind="ExternalOutput").ap()
            aTb = Buf(None, "aT_out")
            outs.append(aTb)
        elif kind == "final":
            fng_d = nc.dram_tensor("fng", [128, KC], F32, kind="ExternalInput").ap()
            y_d = nc.dram_tensor("yT", [D, NT], F32, kind="ExternalOutput").ap()
            fng = mk.sb("fng", [128, KC], F32)
            mk.dma("sp", fng.t[:, :], fng_d, reads=[wbuf], writes=[fng])
            yb = Buf(None, "yT")
            outs.append(yb)
        for p in range(NPASS):
            c0 = p * PW
            src = (hout if in_place else hin)
            sb_ = (hb_out if in_place else hb_in)[p]
            src_fn = (lambda i, src=src, c0=c0: src[i * 128:(i + 1) * 128, c0:c0 + PW])
            dst_fn = (lambda i, c0=c0: hout[i * 128:(i + 1) * 128, c0:c0 + PW])
            if kind == "ffn":
                emit_ffn(mk, P, src_fn, dst_fn, sb_, hb_out[p], subt,
                         lambda i, mc: S_sb[li].t[:, k, i, mc:mc + 1],
                         lambda i, mc: mod_sb[li].t[:, 3 * k * KC + i, mc:mc + 1],
                         lambda i, mc: G_sb[li].t[:, k, i, mc:mc + 1],
                         w_in_d, w_out_d, J_FFN, NH_FFN, wbuf)
            elif kind == "proj":
                mk.dma("sp", P.aT.t[:, :, 0:PW], cat_d[:, c0:c0 + PW].rearrange("(kc p) t -> p kc t", p=128),
                       reads=[catb], writes=[P.aT])
                emit_outproj(mk, P, lambda jj, off, sz: P.aT.t[:, jj, off:off + sz], [P.aT], KC,
                             lambda i: w_p_d[i], wbuf, src_fn, sb_, dst_fn, hb_out[p], subt,
                             lambda i, mc: G_sb[li].t[:, 1, i, mc:mc + 1])
            elif kind == "normout":
                emit_norm(mk, P, src_fn, sb_, 0, subt,
                          lambda i, mc: S_sb[li].t[:, 1, i, mc:mc + 1],
                          lambda i, mc: mod_sb[li].t[:, 3 * KC + i, mc:mc + 1],
                          lambda i, off, sz: (P.aT.t[:, i, off:off + sz], P.aT))
                mk.dma("sp", aT_d[:, c0:c0 + PW].rearrange("(kc p) t -> p kc t", p=128), P.aT.t[:, :, 0:PW],
                       reads=[P.aT], writes=[aTb])
            elif kind == "final":
                cur = {}

                def out_fn(i, off, sz):
                    if "t" not in cur or cur["i"] != i:
                        cur["t"] = P.nxt("hst"); cur["i"] = i
                    return cur["t"].t[:, off:off + sz], cur["t"]

                def post_fn(i, c0=c0):
                    mk.dma("sp", y_d[i * 128:(i + 1) * 128, c0:c0 + PW], cur["t"].t[:, 0:PW], reads=[cur["t"]], writes=[yb])
                emit_norm(mk, P, src_fn, sb_, 0, subt, lambda i, mc: fng.t[:, i:i + 1], None, out_fn, post_fn)
        if kind in ("ffn", "proj"):
            in_place = True
    mk.finish(outs)
    return nc


def build_mod_launch(nch):
    nc = bass.Bass("TRN2", target_bir_lowering=False)
    mk = MK(nc)
    P = Pools(mk, 64, KC)
    cT = nc.dram_tensor("cT", [128, KC, 3], F32, kind="ExternalInput").ap()
    w = nc.dram_tensor("w", [nch, 128, KC, 128], F32, kind="ExternalInput").ap()
    b = nc.dram_tensor("b", [128, nch], F32, kind="ExternalInput").ap()
    o = nc.dram_tensor("o", [128, nch, 3], F32, kind="ExternalOutput").ap()
    wbuf = Buf(None, "io")
    emit_mod(mk, P, cT, w, b, o, nch, wbuf)
    mk.finish([wbuf])
    return nc


def lay_w_in(w):
    F_ = w.shape[1] // 2
    Jn = F_ // 128
    return np.ascontiguousarray(w.reshape(KC, 128, 2, Jn, 128).transpose(3, 1, 0, 2, 4).reshape(Jn, 128, KC, 256))


def lay_w_out(w, NH):
    F_ = w.shape[0]
    JH = F_ // 128 // NH
    return np.ascontiguousarray(w.reshape(NH, JH, 128, KC, 128).transpose(0, 3, 2, 1, 4))


def lay_vec(v):
    n = v.shape[-1] // 128
    r = v.reshape(v.shape[:-1] + (n, 128))
    return np.ascontiguousarray(np.moveaxis(r, -1, 0))


def run_mod(c, c_ctx, w_mod, b_mod):
    ncols = 9 * D
    nch_total = NL * ncols // 128
    nch = nch_total // NCORES
    cT = lay_vec(np.stack([c[0], c[1], c_ctx], 0))
    cT = np.ascontiguousarray(cT.transpose(0, 2, 1))
    wl = w_mod.reshape(NL, KC, 128, ncols // 128, 128).transpose(0, 3, 2, 1, 4).reshape(nch_total, 128, KC, 128)
    bl = b_mod.reshape(nch_total, 128).T
    nc = build_mod_launch(nch)
    in_maps = []
    for cidx in range(NCORES):
        sl = slice(cidx * nch, (cidx + 1) * nch)
        in_maps.append({"cT": cT, "w": np.ascontiguousarray(wl[sl]), "b": np.ascontiguousarray(bl[:, sl])})
    res = run_bass_kernel_spmd(nc, in_maps, core_ids=list(range(NCORES)))
    o = np.concatenate([r["o"] for r in res.results], axis=1)
    mod = o.transpose(2, 1, 0).reshape(3, NL, ncols).transpose(1, 0, 2)
    return np.ascontiguousarray(mod)


def modT_for_core(mod, layers, b):
    out = []
    for l in layers:
        m = mod[l][[b, 2]]
        out.append(m.reshape(2, 144, 128).transpose(2, 1, 0))
    return np.ascontiguousarray(np.stack(out, 0))


def gains_for(norm_gains, layers):
    return np.ascontiguousarray(np.stack([lay_vec(norm_gains[l]) for l in layers], 0))


PWL = 1056
SUBT_L = subtiles_of(1024, 32)


def pack_tokens(lat, ctxa, dtype=None):
    outs = []
    for cidx in range(NCORES):
        b, j = divmod(cidx, 4)
        cols = []
        for p in range(NPASS):
            cols.append(lat[b, j * 2048 + p * 1024: j * 2048 + (p + 1) * 1024])
            cols.append(ctxa[b, j * 64 + p * 32: j * 64 + (p + 1) * 32])
        outs.append(np.ascontiguousarray(np.concatenate(cols, 0).T))
    return outs


def unpack_tokens(per_core, C):
    dt = per_core[0].dtype
    lat = np.empty((2, 8192, C), dt)
    ctxa = np.empty((2, 256, C), dt)
    for cidx in range(NCORES):
        b, j = divmod(cidx, 4)
        a = per_core[cidx].T
        for p in range(NPASS):
            base = p * PWL
            lat[b, j * 2048 + p * 1024: j * 2048 + (p + 1) * 1024] = a[base:base + 1024]
            ctxa[b, j * 64 + p * 32: j * 64 + (p + 1) * 32] = a[base + 1024:base + 1056]
    return lat, ctxa

NLAT = 8192
NCTX = 256
NTOK = NLAT + NCTX
HD = 128
ATTN_SCALE = HD ** -0.5


def emit_qknorm_rope(mk, W, ps, gain_ap, dst_ap, dst_buf, T, cs_aps=None):
    nc = mk.nc
    mk.op("act", lambda: nc.scalar.activation(out=T["sq"].t[:, 0:W], in_=ps.t[:, 0:W], func=AF.Square),
          reads=[ps], writes=[T["sq"]])
    mk.mm([T["ssq"]], [(T["ssq"].t[:, 0:W], T["ones"].t[:, :], T["sq"].t[:, 0:W])], reads=[T["sq"], T["ones"]],
          start=True, stop=True)
    mk.op("dve", lambda: nc.vector.tensor_scalar(out=T["r"].t[:, 0:W], in0=T["ssq"].t[:, 0:W], scalar1=1.0 / HD,
                                                 scalar2=EPS, op0=ALU.mult, op1=ALU.add), reads=[T["ssq"]], writes=[T["r"]])
    mk.op("act", lambda: nc.scalar.activation(out=T["r"].t[:, 0:W], in_=T["r"].t[:, 0:W], func=AF.Sqrt),
          reads=[T["r"]], writes=[T["r"]])
    mk.op("dve", lambda: nc.vector.reciprocal(out=T["r"].t[:, 0:W], in_=T["r"].t[:, 0:W]), reads=[T["r"]], writes=[T["r"]])
    if cs_aps is None:
        mk.op("dve", lambda: nc.vector.scalar_tensor_tensor(out=dst_ap, in0=ps.t[:, 0:W], scalar=gain_ap, in1=T["r"].t[:, 0:W],
                                                            op0=ALU.mult, op1=ALU.mult), reads=[ps, T["r"]], writes=[dst_buf])
        return
    C_ap, S_ap, csbuf = cs_aps
    mk.op("dve", lambda: nc.vector.scalar_tensor_tensor(out=T["qn"].t[:, 0:W], in0=ps.t[:, 0:W], scalar=gain_ap,
                                                        in1=T["r"].t[:, 0:W], op0=ALU.mult, op1=ALU.mult),
          reads=[ps, T["r"]], writes=[T["qn"]])
    mk.op("act", lambda: nc.scalar.copy(out=T["qnb"].t[:, 0:W], in_=T["qn"].t[:, 0:W]), reads=[T["qn"]], writes=[T["qnb"]])
    mk.mm([T["rot"]], [(T["rot"].t[:, 0:W], T["rmat"].t[:, 0, :], T["qnb"].t[:, 0:W])], reads=[T["qnb"], T["rmat"]],
          start=True, stop=True)
    mk.op("dve", lambda: nc.vector.tensor_tensor(out=T["t1"].t[:, 0:W], in0=T["qn"].t[:, 0:W], in1=C_ap, op=ALU.mult),
          reads=[T["qn"], csbuf], writes=[T["t1"]])
    mk.op("dve", lambda: nc.vector.tensor_tensor(out=T["t2"].t[:, 0:W], in0=T["rot"].t[:, 0:W], in1=S_ap, op=ALU.mult),
          reads=[T["rot"], csbuf], writes=[T["t2"]])
    mk.op("pool", lambda: nc.gpsimd.tensor_tensor(out=dst_ap, in0=T["t1"].t[:, 0:W], in1=T["t2"].t[:, 0:W], op=ALU.add),
          reads=[T["t1"], T["t2"]], writes=[dst_buf])


def build_mixa0_launch():
    nc = bass.Bass("TRN2", target_bir_lowering=False)
    mk = MK(nc)
    aT_d = nc.dram_tensor("aT", [D, NTOK], BF16, kind="ExternalInput").ap()
    wsel_d = nc.dram_tensor("wsel", [128, KC, 768], F32, kind="ExternalInput").ap()
    gqk_d = nc.dram_tensor("gqk", [128, 2], F32, kind="ExternalInput").ap()
    cos_d = nc.dram_tensor("cosT", [128, NLAT], F32, kind="ExternalInput").ap()
    sin_d = nc.dram_tensor("sinT", [128, NLAT], F32, kind="ExternalInput").ap()
    cmat_d = nc.dram_tensor("cmat", [128, 4, 128], BF16, kind="ExternalInput").ap()
    dftL_d = nc.dram_tensor("dftL", [16, 8, 2, 128, 8, 512], BF16, kind="ExternalInput").ap()
    dftC_d = nc.dram_tensor("dftC", [2, 128, 2, 256], BF16, kind="ExternalInput").ap()
    cat_d = nc.dram_tensor("cat", [512, NTOK], BF16, kind="ExternalOutput").ap()
    io = Buf(None, "io")
    catb = Buf(None, "cat")
    KT = mk.sb("KT", [128, NTOK], BF16)
    V = mk.sb("V", [128, 66, 128], BF16)
    UW = mk.sb("UW", [128, 66, 256], BF16)
    wsel = mk.sb("wsel", [128, KC, 768], BF16)
    cmat = mk.sb("cmat", [128, 4, 128], BF16)
    gqk = mk.sb("gqk", [128, 2], F32)
    gq = mk.sb("gq", [128, 1], F32)
    QT = mk.sb("QT", [128, NLAT], BF16)
    QC = mk.sb("QC", [128, NCTX], BF16)
    aslot = [mk.sb(f"aslot{i}", [128, KC, 512], BF16) for i in range(2)]
    cslot = [mk.sb(f"cslot{i}", [128, 2, 512], F32) for i in range(2)]
    T = {n: mk.sb(n, [128, 512], F32) for n in ("r", "qn", "t1", "t2")}
    T["sq"] = mk.sb("sq", [128, 512], BF16)
    T["qnb"] = mk.sb("qnb", [128, 512], BF16)
    T["ones"] = mk.sb("ones", [128, 128], BF16)
    fT = mk.sb("fT", [128, 512], BF16)
    PT = [mk.sb(f"PT{i}", [128, 512], BF16) for i in range(4)]
    rden = mk.sb("rden", [128, 512], F32)
    otile = [mk.sb(f"otile{i}", [128, 512], BF16) for i in range(2)]
    dslot = [mk.sb(f"dslot{i}", [128, 16, 512], BF16) for i in range(2)]
    dtile = mk.sb("dtile", [128, 512], BF16)
    bank = [mk.ps(f"bank{i}", [128, 512], F32) for i in range(8)]
    T["ssq"] = bank[6]
    T["rot"] = bank[7]
    mk.op("dve", lambda: nc.vector.memset(T["ones"].t[:, :], 1.0), writes=[T["ones"]])
    mk.dma("pool", wsel.t[:, :, :], wsel_d, reads=[io], writes=[wsel])
    mk.dma("sp", cmat.t[:, :, :], cmat_d, reads=[io], writes=[cmat])
    mk.dma("sp", gqk.t[:, :], gqk_d, reads=[io], writes=[gqk])
    mk.op("dve", lambda: nc.vector.tensor_scalar(out=gq.t[:, :], in0=gqk.t[:, 0:1], scalar1=ATTN_SCALE, scalar2=None,
                                                 op0=ALU.mult), reads=[gqk], writes=[gq])
    T["rmat"] = cmat
    tiles = [(t * 512, 512, True) for t in range(16)] + [(NLAT, NCTX, False)]
    cnt = {"a": 0, "c": 0}

    def load_tile(c0, W, lat):
        a = aslot[cnt["a"] % 2]; cnt["a"] += 1
        mk.dma("sp", a.t[:, :, 0:W], aT_d[:, c0:c0 + W].rearrange("(kc p) t -> p kc t", p=128), reads=[io], writes=[a])
        cs = None
        if lat:
            c = cslot[cnt["c"] % 2]; cnt["c"] += 1
            mk.dma("sp", c.t[:, 0, 0:W], cos_d[:, c0:c0 + W], reads=[io], writes=[c])
            mk.dma("sp", c.t[:, 1, 0:W], sin_d[:, c0:c0 + W], reads=[io], writes=[c])
            cs = (c.t[:, 0, 0:W], c.t[:, 1, 0:W], c)
        return a, cs

    def proj(a, W, col0, pb):
        mms = [(0, kc, pb.t[:, 0:W], wsel.t[:, kc, col0:col0 + 128], a.t[:, kc, 0:W]) for kc in range(KC)]
        mk.mm_multi([pb], mms, reads=[wsel, a], nk=KC)

    for ti, (c0, W, lat) in enumerate(tiles):
        a, cs = load_tile(c0, W, lat)
        nsb = W // 128
        ch0 = c0 // 128
        pk = bank[ti % 2]
        proj(a, W, 512, pk)
        emit_qknorm_rope(mk, W, pk, gqk.t[:, 1:2], KT.t[:, c0:c0 + W], KT, T, cs)
        pv = bank[2]
        mms = []
        for sb_ in range(nsb):
            for kc in range(KC):
                mms.append((0, kc, pv.t[:, sb_ * 128:(sb_ + 1) * 128], a.t[:, kc, sb_ * 128:(sb_ + 1) * 128], wsel.t[:, kc, 640:768]))
        mk.mm_multi([pv], mms, reads=[wsel, a], nk=KC)
        mk.op("act", lambda: nc.scalar.copy(out=V.t[:, ch0:ch0 + nsb, :], in_=pv.t[:, 0:W].rearrange("p (s c) -> p s c", c=128)),
              reads=[pv], writes=[V])
        pf = bank[3]
        proj(a, W, 0, pf)
        mk.op("act", lambda: nc.scalar.copy(out=fT.t[:, 0:W], in_=pf.t[:, 0:W]), reads=[pf], writes=[fT])
        for half in range((nsb + 1) // 2):
            pu = bank[4 + half]
            nn = min(2, nsb - half * 2)
            for s2 in range(nn):
                sb_ = half * 2 + s2
                mk.mm([pu], [(pu.t[:, s2 * 256:(s2 + 1) * 256], fT.t[:, sb_ * 128:(sb_ + 1) * 128],
                              cmat.t[:, 1:3, :].rearrange("p a c -> p (a c)"))], reads=[fT, cmat], start=True, stop=True)
            mk.op("dve", lambda: nc.vector.tensor_copy(out=UW.t[:, ch0 + half * 2:ch0 + half * 2 + nn, :],
                                                       in_=pu.t[:, 0:nn * 256].rearrange("p (s c) -> p s c", c=256)),
                  reads=[pu], writes=[UW])
    T["ssq"] = bank[2]
    T["rot"] = bank[3]
    py = bank[7]

    def dft_units():
        units = [(kt, ng) for kt in range(16) for ng in range(8)]

        def load(u):
            kt, ng = units[u]
            ds_ = dslot[u % 2]
            for cs_ in range(2):
                mk.dma("sp", ds_.t[:, cs_ * 8:(cs_ + 1) * 8, :], dftL_d[kt, ng, cs_], reads=[io], writes=[ds_])
        load(0)
        load(1)
        yield
        for u, (kt, ng) in enumerate(units):
            ds_ = dslot[u % 2]
            mms = []
            for n8 in range(8):
                nch = ng * 8 + n8
                mms.append((py.t[:, :], UW.t[:, nch, 0:128], ds_.t[:, n8, :], (nch == 0), False))
                mms.append((py.t[:, :], UW.t[:, nch, 128:256], ds_.t[:, 8 + n8, :], False, (nch == 63)))
            mk.mmf([py], mms, reads=[UW, ds_])
            if u + 2 < len(units):
                load(u + 2)
            if ng == 7:
                mk.op("act", lambda: nc.scalar.copy(out=dtile.t[:, :], in_=py.t[:, :]), reads=[py], writes=[dtile])
                mk.dma("sp", cat_d[0:128, kt * 512:(kt + 1) * 512], dtile.t[:, :], reads=[dtile], writes=[catb])
            yield
    dgen = dft_units()
    next(dgen)

    def dft_step(n):
        for _ in range(n):
            try:
                next(dgen)
            except StopIteration:
                return
    for h in range(3):
        for ti, (c0, W, lat) in enumerate(tiles):
            a, cs = load_tile(c0, W, lat)
            pq = bank[ti % 2]
            proj(a, W, 128 + h * 128, pq)
            if lat:
                emit_qknorm_rope(mk, W, pq, gq.t[:, 0:1], QT.t[:, c0:c0 + W], QT, T, cs)
            else:
                emit_qknorm_rope(mk, W, pq, gq.t[:, 0:1], QC.t[:, 0:W], QC, T, None)
        qtiles = [(QT, t * 512, 512, t * 512, list(range(66))) for t in range(16)] + [(QC, 0, NCTX, NLAT, [64, 65])]
        for qi, (qbuf, q0, W, o0, kcs) in enumerate(qtiles):
            O = bank[3 + (qi % 2)]
            Dn = bank[5 + (qi % 2)]
            nk = len(kcs)

            def s_mm(idx):
                kc = kcs[idx]
                sb_ = bank[idx % 3]
                mk.mm([sb_], [(sb_.t[:, 0:W], KT.t[:, kc * 128:(kc + 1) * 128], qbuf.t[:, q0:q0 + W])], reads=[KT, qbuf],
                      start=True, stop=True)

            def e_act(idx):
                sb_ = bank[idx % 3]
                pt = PT[idx % 4]
                mk.op("act", lambda: nc.scalar.activation(out=pt.t[:, 0:W], in_=sb_.t[:, 0:W], func=AF.Exp),
                      reads=[sb_], writes=[pt])

            s_mm(0)
            if nk > 1:
                s_mm(1)
            e_act(0)
            for idx in range(nk):
                if nk > 2 and idx in (nk // 4, nk // 2, (3 * nk) // 4):
                    dft_step(1)
                if idx + 2 < nk:
                    s_mm(idx + 2)
                if idx + 1 < nk:
                    e_act(idx + 1)
                pt = PT[idx % 4]
                kc = kcs[idx]
                mk.mm([O, Dn], [(O.t[:, 0:W], V.t[:, kc, :], pt.t[:, 0:W]), (Dn.t[:, 0:W], T["ones"].t[:, :], pt.t[:, 0:W])],
                      reads=[pt, V, T["ones"]], start=(idx == 0), stop=(idx == nk - 1))
            mk.op("dve", lambda: nc.vector.reciprocal(out=rden.t[:, 0:W], in_=Dn.t[:, 0:W]), reads=[Dn], writes=[rden])
            ot = otile[qi % 2]
            mk.op("dve", lambda: nc.vector.tensor_tensor(out=ot.t[:, 0:W], in0=O.t[:, 0:W], in1=rden.t[:, 0:W], op=ALU.mult),
                  reads=[O, rden], writes=[ot])
            mk.dma("sp", cat_d[128 + h * 128:256 + h * 128, o0:o0 + W], ot.t[:, 0:W], reads=[ot], writes=[catb])
    dft_step(1000)
    dcnt = 0
    ds_ = dslot[0]
    for cs_ in range(2):
        mk.dma("sp", ds_.t[:, cs_ * 8:cs_ * 8 + 2, 0:256], dftC_d[cs_], reads=[io], writes=[ds_])
    py = bank[2]
    mms = []
    for n2 in range(2):
        mms.append((0, 2 * n2, py.t[:, 0:256], UW.t[:, 64 + n2, 0:128], ds_.t[:, n2, 0:256]))
        mms.append((0, 2 * n2 + 1, py.t[:, 0:256], UW.t[:, 64 + n2, 128:256], ds_.t[:, 8 + n2, 0:256]))
    mk.mm_multi([py], mms, reads=[UW, ds_], nk=4)
    ot = otile[0]
    mk.op("act", lambda: nc.scalar.copy(out=ot.t[:, 0:256], in_=py.t[:, 0:256]), reads=[py], writes=[ot])
    mk.dma("sp", cat_d[0:128, NLAT:NTOK], ot.t[:, 0:256], reads=[ot], writes=[catb])
    mk.finish([catb])
    return nc


_CONST = {}


def mixa0_consts():
    if "m0" in _CONST:
        return _CONST["m0"]
    bf = ml_dtypes.bfloat16
    rows = NLAT // 64
    row_id = np.repeat(np.arange(rows, dtype=np.float32), 64)
    col_id = np.tile(np.arange(64, dtype=np.float32), rows)
    inv_freq = (np.float32(10000.0) ** (-np.arange(0, 64, 2, dtype=np.float32) / np.float32(64))).astype(np.float32)
    ang = np.concatenate([row_id[:, None] * inv_freq, col_id[:, None] * inv_freq], axis=-1).astype(np.float32)
    cosL, sinL = np.cos(ang).astype(np.float32), np.sin(ang).astype(np.float32)
    pidx = np.arange(128)
    fidx = (pidx // 64) * 32 + (pidx % 32)
    cosT = np.ascontiguousarray(cosL[:, fidx].T)
    sinT = np.ascontiguousarray(sinL[:, fidx].T)
    rm = np.zeros((128, 128), np.float32)
    for p in range(128):
        if (p % 64) < 32:
            rm[p + 32, p] = -1.0
        else:
            rm[p - 32, p] = 1.0
    cc = np.arange(128)
    angc = 2 * np.pi * ((cc[:, None] * cc[None, :]) % 128) / 128.0
    Cc = np.cos(angc) / np.sqrt(128.0)
    Sc = np.sin(angc) / np.sqrt(128.0)
    cmat = np.stack([rm, Cc, Sc, np.zeros((128, 128))], 1).astype(bf)

    def dft(N):
        n = np.arange(N, dtype=np.int64)
        m = (n[:, None] * n[None, :]) % N
        a = m.astype(np.float32) * np.float32(2 * np.pi / N)
        s = np.float32(1.0 / np.sqrt(N))
        return (np.cos(a) * s).astype(bf), (-np.sin(a) * s).astype(bf)
    CL, SL = dft(NLAT)
    def layL(M):
        return M.reshape(8, 8, 128, 16, 512).transpose(3, 0, 2, 1, 4)
    dftL = np.ascontiguousarray(np.stack([layL(CL), layL(SL)], 2))
    Cx, Sx = dft(NCTX)
    layC = lambda M: M.reshape(2, 128, 256).transpose(1, 0, 2)
    dftC = np.ascontiguousarray(np.stack([layC(Cx), layC(Sx)], 0))
    _CONST["m0"] = dict(cosT=cosT, sinT=sinT, cmat=np.ascontiguousarray(cmat), dftL=dftL, dftC=dftC)
    return _CONST["m0"]


def mixa0_wsel(w_in, j):
    cols = np.concatenate([np.arange(j * 128, (j + 1) * 128),
                           512 + np.arange(3 * j * 128, (3 * j + 3) * 128),
                           2048 + np.arange(j * 128, (j + 1) * 128),
                           2560 + np.arange(j * 128, (j + 1) * 128)])
    w = w_in[:, cols]
    return np.ascontiguousarray(w.reshape(KC, 128, 768).transpose(1, 0, 2))


def run_mixa0(a_lat, a_ctx, ab_w_in, qk_norm, trace=False):
    C = mixa0_consts()
    nc = build_mixa0_launch()
    gqk = np.ascontiguousarray(qk_norm.T.astype(np.float32))
    aT = [np.ascontiguousarray(np.concatenate([a_lat[b], a_ctx[b]], 0).T) for b in range(2)]
    in_maps = []
    for cidx in range(NCORES):
        b, j = divmod(cidx, 4)
        in_maps.append(dict(aT=aT[b], wsel=mixa0_wsel(ab_w_in, j), gqk=gqk, cosT=C["cosT"], sinT=C["sinT"],
                            cmat=C["cmat"], dftL=C["dftL"], dftC=C["dftC"]))
    res = run_bass_kernel_spmd(nc, in_maps, core_ids=list(range(NCORES)), trace=trace)
    cat_lat = np.empty((2, NLAT, 2048), a_lat.dtype)
    cat_ctx = np.empty((2, NCTX, 2048), a_lat.dtype)
    for cidx in range(NCORES):
        b, j = divmod(cidx, 4)
        c = res.results[cidx]["cat"].T
        for dst, rows in ((cat_lat, slice(0, NLAT)), (cat_ctx, slice(NLAT, NTOK))):
            dst[b][:, j * 128:(j + 1) * 128] = c[rows, 0:128]
            dst[b][:, 512 + 3 * j * 128: 512 + (3 * j + 3) * 128] = c[rows, 128:512]
    return cat_lat, cat_ctx, res

CH = 64
NHL = 4


def build_mixa1_launch(NLAT=8192):
    NTOK = NLAT + NCTX
    nc = bass.Bass("TRN2", target_bir_lowering=False)
    mk = MK(nc)
    aT_d = nc.dram_tensor("aT", [D, NTOK], BF16, kind="ExternalInput").ap()
    w_d = nc.dram_tensor("w5", [5, 128, KC, 512], F32, kind="ExternalInput").ap()
    lbl_d = nc.dram_tensor("lbl", [128, 2, NHL], F32, kind="ExternalInput").ap()
    gon_d = nc.dram_tensor("gon", [128, 1], F32, kind="ExternalInput").ap()
    cm_d = nc.dram_tensor("cm", [128, 4, 128], BF16, kind="ExternalInput").ap()
    sm_d = nc.dram_tensor("sm", [128, 512], F32, kind="ExternalInput").ap()
    out_d = nc.dram_tensor("og", [NHL * 128, NLAT], BF16, kind="ExternalOutput").ap()
    ofw_d = nc.dram_tensor("ofw", [NHL, 128, NLAT], BF16, kind="Internal").ap()
    qsc_d = nc.dram_tensor("qsc", [NHL, 128, NTOK], BF16, kind="Internal").ap()
    vsc_d = nc.dram_tensor("vsc", [NTOK // 128, 128, 512], BF16, kind="Internal").ap()
    io = Buf(None, "io"); outb = Buf(None, "out"); ofwb = Buf(None, "ofw"); qscb = Buf(None, "qsc"); vscb = Buf(None, "vsc")
    W = mk.sb("W", [128, 4, KC, 512], BF16)
    aslot = [mk.sb(f"aslot{i}", [128, KC, 512], BF16) for i in range(2)]
    cm = mk.sb("cm", [128, 4, 128], BF16)
    sm = mk.sb("sm", [128, 512], F32)
    lbl = mk.sb("lbl", [128, 2, NHL], F32)
    lb = mk.sb("lb", [128, NHL], F32)
    oml = mk.sb("oml", [128, NHL], F32)
    gon = mk.sb("gon", [128, 1], F32)
    ones = mk.sb("ones", [128, 128], BF16)
    Vt = [mk.sb(f"Vt{i}", [128, 512], BF16) for i in range(8)]
    S32 = [mk.sb(f"S32_{h}", [128, 128], F32) for h in range(NHL)]
    Sbf = [[mk.sb(f"Sbf_{h}_{i}", [128, 128], BF16) for i in range(6)] for h in range(NHL)]
    F = {n: [mk.sb(f"{n}{i}", [128, 512], F32) for i in range(2)] for n in ("sg", "fg", "lf", "b", "eb", "enb", "kk")}
    QdT = [mk.sb(f"QdT{i}", [128, 512], BF16) for i in range(2)]
    QSB = [mk.sb(f"QSB{i}", [128, 512], BF16) for i in range(2)]
    KdT = [mk.sb(f"KdT{i}", [128, 512], BF16) for i in range(2)]
    Kdtok = [mk.sb(f"Kdtok{i}", [128, 128], BF16) for i in range(2)]
    ATm = [mk.sb(f"ATm{i}", [128, 128], BF16) for i in range(4)]
    tmpS = [mk.sb(f"tmpS{i}", [128, 128], F32) for i in range(2)]
    ofs = [mk.sb(f"ofs{i}", [128, 512], BF16) for i in range(2)]
    O32 = [mk.sb(f"o32_{i}", [128, 512], F32) for i in range(2)]
    SQ = [mk.sb(f"sq_{i}", [128, 512], BF16) for i in range(2)]
    R32 = [mk.sb(f"r32_{i}", [128, 512], F32) for i in range(2)]
    SGG = [mk.sb(f"sgg_{i}", [128, 512], F32) for i in range(2)]
    epsb = mk.sb("epsb", [128, 1], F32)
    otile = [mk.sb(f"otile{i}", [128, 512], BF16) for i in range(2)]
    cnt = {}

    def rot(lst, key):
        i = cnt.get(key, 0); cnt[key] = i + 1
        return lst[i % len(lst)]
    bank = [mk.ps(f"bank{i}", [128, 512], F32) for i in range(7)]
    bankT = mk.ps("bankT", [128, 1024], BF16)
    pO2 = [bank[0], bank[3]]; pP = [bank[1], bank[2]]; pV = bank[1]; pSSQ = bank[1]
    pP4 = [bank[1], bank[2], bank[4], bank[5]]
    rA = [(bank[4], bank[4].t[:, 0:128]), (bank[5], bank[5].t[:, 0:128])]
    rS = [(bank[6], bank[6].t[:, 0:128]), (bank[6], bank[6].t[:, 128:256])]
    rT = [(bankT, bankT.t[:, 0:128])]
    mk.op("dve", lambda: nc.vector.memset(ones.t[:, :], 1.0), writes=[ones])
    mk.op("dve", lambda: nc.vector.memset(epsb.t[:, :], EPS), writes=[epsb])
    mk.dma("sp", cm.t[:, :, :], cm_d, reads=[io], writes=[cm])
    mk.dma("sp", sm.t[:, :], sm_d, reads=[io], writes=[sm])
    mk.dma("sp", lbl.t[:, :, :], lbl_d, reads=[io], writes=[lbl])
    mk.dma("sp", gon.t[:, :], gon_d, reads=[io], writes=[gon])
    mk.op("dve", lambda: nc.vector.tensor_tensor(out=lb.t[:, :], in0=lbl.t[:, 1, :], in1=lbl.t[:, 0, :], op=ALU.subtract),
          reads=[lbl], writes=[lb])
    mk.op("act", lambda: nc.scalar.activation(out=lb.t[:, :], in_=lb.t[:, :], func=AF.Sigmoid), reads=[lb], writes=[lb])
    mk.op("dve", lambda: nc.vector.tensor_scalar(out=oml.t[:, :], in0=lb.t[:, :], scalar1=-1.0, scalar2=1.0, op0=ALU.mult,
                                                 op1=ALU.add), reads=[lb], writes=[oml])
    IDENT = cm.t[:, 2, :]

    for direction in (0, 1):
        fw = direction == 0
        secs = [0, 1, 3] if fw else [0, 2, 3, 4]
        for si, s in enumerate(secs):
            mk.dma("pool", W.t[:, si, :, :], w_d[s], reads=[io], writes=[W])
        for h in range(NHL):
            mk.op("dve", lambda: nc.vector.memset(S32[h].t[:, :], 0.0), writes=[S32[h]])
            mk.op("pool", lambda: nc.gpsimd.memset(Sbf[h][0].t[:, :], 0.0), writes=[Sbf[h][0]])
        sidx = [0] * NHL
        ctx_tiles = [(NLAT, NCTX, False)]
        lat_tiles = [(t * 512, 512, True) for t in range(NLAT // 512)]
        tiles = (ctx_tiles + lat_tiles) if fw else (ctx_tiles + lat_tiles[::-1])
        MASK = cm.t[:, 0, :] if fw else cm.t[:, 1, :]
        for (c0, Wd, lat) in tiles:
            nblk = Wd // 128
            a = rot(aslot, "a")
            mk.dma("sp", a.t[:, :, 0:Wd], aT_d[:, c0:c0 + Wd].rearrange("(kc p) t -> p kc t", p=128), reads=[io], writes=[a])
            vts = []
            for blk in range(nblk):
                vt = rot(Vt, "vt")
                gblk = c0 // 128 + blk
                if fw:
                    mms = [(pV.t[:, :], a.t[:, kc, blk * 128:(blk + 1) * 128], W.t[:, 2, kc, :], kc == 0, kc == KC - 1) for kc in range(KC)]
                    mk.mmf([pV], mms, reads=[a, W])
                    mk.op("act", lambda: nc.scalar.copy(out=vt.t[:, :], in_=pV.t[:, :]), reads=[pV], writes=[vt])
                    mk.dma("sp", vsc_d[gblk], vt.t[:, :], reads=[vt], writes=[vscb])
                else:
                    mk.dma("sp", vt.t[:, :], vsc_d[gblk], reads=[vscb], writes=[vt])
                vts.append(vt)
            sl = slice(0, Wd)
            nch = Wd // CH
            blks = list(range(nblk)) if fw else list(range(nblk))[::-1]

            def stageA_pair(hs):
                R = {}
                for k_, h in enumerate(hs):
                    pq, pf = pP4[2 * k_], pP4[2 * k_ + 1]
                    for (pb, si) in (((pq, 0), (pf, 1)) if fw else ((pf, 1),)):
                        mms = [(pb.t[:, 0:Wd], W.t[:, si, kc, h * 128:(h + 1) * 128], a.t[:, kc, 0:Wd], kc == 0, kc == KC - 1) for kc in range(KC)]
                        mk.mmf([pb], mms, reads=[a, W])
                    R[h] = dict(t={n: rot(F[n], n) for n in F}, qd=rot(QdT, "qd"), kd=rot(KdT, "kd"), pq=pq, pf=pf, qsb=rot(QSB, "qsb"))
                ACT = lambda o, i, f, **kw: nc.scalar.activation(out=o, in_=i, func=f, **kw)
                for h in hs:
                    t, pf = R[h]["t"], R[h]["pf"]
                    mk.op("act", lambda: ACT(t["sg"].t[:, sl], pf.t[:, sl], AF.Sigmoid), reads=[pf], writes=[t["sg"]])
                for h in hs:
                    t = R[h]["t"]
                    mk.op("dve", lambda: nc.vector.tensor_scalar(out=t["fg"].t[:, sl], in0=t["sg"].t[:, sl], scalar1=oml.t[:, h:h + 1],
                                                                 scalar2=lb.t[:, h:h + 1], op0=ALU.mult, op1=ALU.add),
                          reads=[t["sg"], oml, lb], writes=[t["fg"]])
                for h in hs:
                    pq, qsb = R[h]["pq"], R[h]["qsb"]
                    if fw:
                        mk.op("act", lambda: ACT(qsb.t[:, sl], pq.t[:, sl], AF.Silu), reads=[pq], writes=[qsb])
                        mk.dma("sp", qsc_d[h][:, c0:c0 + Wd], qsb.t[:, sl], reads=[qsb], writes=[qscb])
                    else:
                        mk.dma("sp", qsb.t[:, sl], qsc_d[h][:, c0:c0 + Wd], reads=[qscb], writes=[qsb])
                for h in hs:
                    t = R[h]["t"]
                    mk.op("act", lambda: ACT(t["lf"].t[:, sl], t["fg"].t[:, sl], AF.Ln), reads=[t["fg"]], writes=[t["lf"]])
                    mk.op("pool", lambda: nc.gpsimd.tensor_scalar(out=t["kk"].t[:, sl], in0=t["fg"].t[:, sl], scalar1=-1.0, scalar2=1.0,
                                                                  op0=ALU.mult, op1=ALU.add), reads=[t["fg"]], writes=[t["kk"]])
                for h in hs:
                    t = R[h]["t"]
                    mk.op("dve", lambda: nc.vector.tensor_tensor_scan(out=t["b"].t[:, sl], data0=sm.t[:, sl], data1=t["lf"].t[:, sl],
                                                                      initial=0.0, op0=ALU.mult, op1=ALU.add),
                          reads=[sm, t["lf"]], writes=[t["b"]])
                    if not fw:
                        b3 = t["b"].t[:, sl].rearrange("p (c t) -> p c t", t=CH)
                        mk.op("dve", lambda: nc.vector.tensor_tensor(out=t["lf"].t[:, sl], in0=t["lf"].t[:, sl], in1=t["b"].t[:, sl], op=ALU.subtract),
                              reads=[t["lf"], t["b"]], writes=[t["lf"]])
                        mk.op("dve", lambda: nc.vector.tensor_tensor(out=t["b"].t[:, sl].rearrange("p (c t) -> p c t", t=CH),
                                                                     in0=t["lf"].t[:, sl].rearrange("p (c t) -> p c t", t=CH),
                                                                     in1=b3[:, :, CH - 1:CH].to_broadcast([128, nch, CH]), op=ALU.add),
                              reads=[t["lf"], t["b"]], writes=[t["b"]])
                for h in hs:
                    t = R[h]["t"]
                    mk.op("act", lambda: ACT(t["eb"].t[:, sl], t["b"].t[:, sl], AF.Exp), reads=[t["b"]], writes=[t["eb"]])
                    mk.op("act", lambda: ACT(t["enb"].t[:, sl], t["b"].t[:, sl], AF.Exp, scale=-1.0), reads=[t["b"]], writes=[t["enb"]])
                for h in hs:
                    t, qd, kd = R[h]["t"], R[h]["qd"], R[h]["kd"]
                    qsb = R[h]["qsb"]
                    mk.op("dve", lambda: nc.vector.tensor_tensor(out=qd.t[:, sl], in0=qsb.t[:, sl], in1=t["eb"].t[:, sl], op=ALU.mult),
                          reads=[qsb, t["eb"]], writes=[qd])
                    mk.op("dve", lambda: nc.vector.tensor_tensor(out=kd.t[:, sl], in0=t["kk"].t[:, sl], in1=t["enb"].t[:, sl], op=ALU.mult),
                          reads=[t["kk"], t["enb"]], writes=[kd])
                return {h: dict(qd=R[h]["qd"], kd=R[h]["kd"], eb=R[h]["t"]["eb"]) for h in hs}

            def block_ops(h, blk, A_, pend, pOh):
                qd, kd, eb = A_["qd"], A_["kd"], A_["eb"]
                bs = slice(blk * 128, (blk + 1) * 128)
                vt = vts[blk]
                rtb, rtap = rot(rT, "rt"); kt_ = rot(Kdtok, "kt")
                mk.op("pe", lambda: nc.tensor.transpose(out=rtap, in_=kd.t[:, bs], identity=IDENT), reads=[kd, cm], writes=[rtb])
                mk.op("act", lambda: nc.scalar.copy(out=kt_.t[:, :], in_=rtap), reads=[rtb], writes=[kt_])
                rab, raap = rot(rA, "ra"); atm = rot(ATm, "atm")
                mk.mmf([rab], [(raap, kd.t[:, bs], qd.t[:, bs], True, True)], reads=[kd, qd])
                mk.op("dve", lambda: nc.vector.tensor_tensor(out=atm.t[:, :], in0=raap, in1=MASK, op=ALU.mult), reads=[rab, cm], writes=[atm])
                chunks = [0, 1] if fw else [1, 0]
                s_before = []
                for ci in chunks:
                    s_before.append(Sbf[h][sidx[h]])
                    rsb, rsap = rot(rS, "rs"); tS = rot(tmpS, "ts")
                    rows = slice(ci * CH, (ci + 1) * CH)
                    mk.mmf([rsb], [(rsap, kt_.t[rows, :], vt.t[rows, h * 128:(h + 1) * 128], True, True)], reads=[kt_, vt])
                    cglob = blk * 2 + ci
                    col = cglob * CH + (CH - 1 if fw else 0)
                    et = eb.t[:, col:col + 1]
                    mk.op("act", lambda: nc.scalar.activation(out=tS.t[:, :], in_=rsap, func=AF.Copy, scale=et), reads=[rsb, eb], writes=[tS])
                    mk.op("dve", lambda: nc.vector.scalar_tensor_tensor(out=S32[h].t[:, :], in0=S32[h].t[:, :], scalar=et, in1=tS.t[:, :],
                                                                        op0=ALU.mult, op1=ALU.add), reads=[S32[h], tS, eb], writes=[S32[h]])
                    sidx[h] = (sidx[h] + 1) % 6
                    nxt = Sbf[h][sidx[h]]
                    mk.op("pool", lambda: nc.gpsimd.tensor_copy(out=nxt.t[:, :], in_=S32[h].t[:, :]), reads=[S32[h]], writes=[nxt])
                if lat:
                    def omm():
                        mms = [(pOh.t[:, bs], vt.t[:, h * 128:(h + 1) * 128], atm.t[:, :], True, False)]
                        for n_, ci in enumerate(chunks):
                            cs_ = slice(blk * 128 + ci * CH, blk * 128 + (ci + 1) * CH)
                            mms.append((pOh.t[:, cs_], s_before[n_].t[:, :], qd.t[:, cs_], False, n_ == 1))
                        mk.mmf([pOh], mms, reads=[vt, atm, qd] + s_before)
                    pend.append(omm)
                    if len(pend) > 1:
                        pend.pop(0)()

            def evac_pair(hs):
                if fw:
                    for k_, h in enumerate(hs):
                        of = rot(ofs, "of")
                        mk.op("act", lambda: nc.scalar.copy(out=of.t[:, :], in_=pO2[k_].t[:, :]), reads=[pO2[k_]], writes=[of])
                        mk.dma("sp", ofw_d[h][:, c0:c0 + 512], of.t[:, :], reads=[of], writes=[ofwb])
                    return
                E = {}
                for k_, h in enumerate(hs):
                    of = rot(ofs, "of")
                    mk.dma("sp", of.t[:, :], ofw_d[h][:, c0:c0 + 512], reads=[ofwb], writes=[of])
                    o32 = O32[k_]
                    mk.op("dve", lambda: nc.vector.tensor_tensor(out=o32.t[:, :], in0=pO2[k_].t[:, :], in1=of.t[:, :], op=ALU.add),
                          reads=[pO2[k_], of], writes=[o32])
                    pg = pP4[2 * k_ + 1]
                    mms = [(pg.t[:, :], W.t[:, 3, kc, h * 128:(h + 1) * 128], a.t[:, kc, :], kc == 0, kc == KC - 1) for kc in range(KC)]
                    mk.mmf([pg], mms, reads=[a, W])
                    E[h] = dict(o32=o32, pg=pg, sgg=SGG[k_], sq=SQ[k_], r32=R32[k_], ps=pP4[2 * k_])
                for h in hs:
                    e = E[h]
                    mk.op("act", lambda: nc.scalar.activation(out=e["sgg"].t[:, :], in_=e["pg"].t[:, :], func=AF.Exp, scale=-1.0),
                          reads=[e["pg"]], writes=[e["sgg"]])
                    mk.op("pool", lambda: nc.gpsimd.tensor_tensor(out=e["sq"].t[:, :], in0=e["o32"].t[:, :], in1=e["o32"].t[:, :], op=ALU.mult),
                          reads=[e["o32"]], writes=[e["sq"]])
                for h in hs:
                    e = E[h]
                    mk.mmf([e["ps"]], [(e["ps"].t[:, :], ones.t[:, :], e["sq"].t[:, :], True, True)], reads=[ones, e["sq"]])
                    mk.op("pool", lambda: nc.gpsimd.tensor_scalar(out=e["sgg"].t[:, :], in0=e["sgg"].t[:, :], scalar1=1.0, scalar2=None, op0=ALU.add),
                          reads=[e["sgg"]], writes=[e["sgg"]])
                for h in hs:
                    e = E[h]
                    mk.op("act", lambda: nc.scalar.activation(out=e["r32"].t[:, :], in_=e["ps"].t[:, :], func=AF.Ln, bias=epsb.t[:, 0:1], scale=1.0 / 128),
                          reads=[e["ps"], epsb], writes=[e["r32"]])
                    mk.op("dve", lambda: nc.vector.reciprocal(out=e["sgg"].t[:, :], in_=e["sgg"].t[:, :]), reads=[e["sgg"]], writes=[e["sgg"]])
                for h in hs:
                    e = E[h]
                    mk.op("act", lambda: nc.scalar.activation(out=e["r32"].t[:, :], in_=e["r32"].t[:, :], func=AF.Exp, scale=-0.5),
                          reads=[e["r32"]], writes=[e["r32"]])
                for h in hs:
                    e = E[h]
                    mk.op("dve", lambda: nc.vector.scalar_tensor_tensor(out=e["o32"].t[:, :], in0=e["o32"].t[:, :], scalar=gon.t[:, 0:1], in1=e["r32"].t[:, :],
                                                                        op0=ALU.mult, op1=ALU.mult), reads=[e["o32"], gon, e["r32"]], writes=[e["o32"]])
                    ot = rot(otile, "ot")
                    mk.op("dve", lambda: nc.vector.tensor_tensor(out=ot.t[:, :], in0=e["o32"].t[:, :], in1=e["sgg"].t[:, :], op=ALU.mult),
                          reads=[e["o32"], e["sgg"]], writes=[ot])
                    mk.dma("sp", out_d[h * 128:(h + 1) * 128, c0:c0 + 512], ot.t[:, :], reads=[ot], writes=[outb])

            for hp in range(0, NHL, 2):
                hs = [hp, hp + 1]
                A = stageA_pair(hs)
                pend = {h: [] for h in hs}
                for blk in blks:
                    for k_, h in enumerate(hs):
                        block_ops(h, blk, A[h], pend[h], pO2[k_])
                if lat:
                    for h in hs:
                        for f_ in pend[h]:
                            f_()
                    evac_pair(hs)
    mk.finish([outb])
    return nc


def mixa1_consts():
    if "m1" in _CONST:
        return _CONST["m1"]
    bf = ml_dtypes.bfloat16
    s = np.arange(128)[:, None]; t = np.arange(128)[None, :]
    same = (s // CH) == (t // CH)
    mfw = (same & (s <= t)).astype(np.float32)
    mbw = (same & (s >= t)).astype(np.float32)
    cm = np.stack([mfw, mbw, np.eye(128, dtype=np.float32), np.zeros((128, 128), np.float32)], 1).astype(bf)
    smr = np.ones((128, 512), np.float32); smr[:, ::CH] = 0.0
    _CONST["m1"] = dict(cm=np.ascontiguousarray(cm), sm=smr)
    return _CONST["m1"]


def run_mixa1(a_lat, a_ctx, hgrn_w_in, lb_logits, o_norm, trace=False):
    C = mixa1_consts()
    nc = build_mixa1_launch()
    aT = [np.ascontiguousarray(np.concatenate([a_lat[b], a_ctx[b]], 0).T) for b in range(2)]
    gon = np.ascontiguousarray(o_norm.reshape(128, 1).astype(np.float32))
    in_maps = []
    for cidx in range(NCORES):
        b, j = divmod(cidx, 4)
        cols = np.arange(j * 512, (j + 1) * 512)
        w5 = np.stack([hgrn_w_in[:, s * 2048 + cols].reshape(KC, 128, 512).transpose(1, 0, 2) for s in range(5)], 0)
        lbl = lb_logits[:, cols].reshape(2, NHL, 128).transpose(2, 0, 1)
        in_maps.append(dict(aT=aT[b], w5=np.ascontiguousarray(w5), lbl=np.ascontiguousarray(lbl), gon=gon, cm=C["cm"], sm=C["sm"]))
    res = run_bass_kernel_spmd(nc, in_maps, core_ids=list(range(NCORES)), trace=trace)
    og = np.empty((2, NLAT, 2048), a_lat.dtype)
    for cidx in range(NCORES):
        b, j = divmod(cidx, 4)
        og[b][:, j * 512:(j + 1) * 512] = res.results[cidx]["og"].T
    return og, res


def _run(nc, in_maps):
    return run_bass_kernel_spmd(nc, in_maps, core_ids=list(range(NCORES))).results


def pack_lat(lat):
    return [np.ascontiguousarray(lat[c // 4, (c % 4) * 2048:(c % 4 + 1) * 2048].T) for c in range(NCORES)]


def kernel(x, c, ctx, c_ctx, w_mod, b_mod, norm_gains, ffn_w_in, ffn_w_out, ab_w_in, qk_norm, ab_w_out,
           hgrn_w_in, hgrn_lb_logits, hgrn_o_norm, hgrn_w_out, final_norm):
    f32 = np.float32
    x, c, ctx, c_ctx = (np.asarray(a, f32) for a in (x, c, ctx, c_ctx))
    mod = run_mod(c, c_ctx, np.asarray(w_mod, f32), np.asarray(b_mod, f32))
    norm_gains = np.asarray(norm_gains, f32)
    nc = build_tok_launch(PWL, SUBT_L, 1, [("ffn", 0, 0), ("normout", 0)])
    hT = pack_tokens(x, ctx)
    w_in = lay_w_in(np.asarray(ffn_w_in[0, 0], f32)); w_out = lay_w_out(np.asarray(ffn_w_out[0, 0], f32), NH_FFN)
    g0 = gains_for(norm_gains, [0])
    res = _run(nc, [{"hT": hT[k], "modT": modT_for_core(mod, [0], k // 4), "gains": g0, "w0_in": w_in, "w0_out": w_out}
                    for k in range(NCORES)])
    hT = [r["hT_out"] for r in res]
    a_lat, a_ctx = unpack_tokens([r["aT_out"] for r in res], D)
    cat_lat, cat_ctx, _ = run_mixa0(a_lat, a_ctx, np.asarray(ab_w_in[0], f32), np.asarray(qk_norm[0], f32))
    nc = build_tok_launch(PWL, SUBT_L, 2, [("proj", 0), ("ffn", 0, 1), ("ffn", 1, 0), ("normout", 1)])
    cat = pack_tokens(cat_lat, cat_ctx)
    wp = lay_w_out(np.asarray(ab_w_out[0], f32), 1)[0]
    w1_in = lay_w_in(np.asarray(ffn_w_in[0, 1], f32)); w1_out = lay_w_out(np.asarray(ffn_w_out[0, 1], f32), NH_FFN)
    w2_in = lay_w_in(np.asarray(ffn_w_in[1, 0], f32)); w2_out = lay_w_out(np.asarray(ffn_w_out[1, 0], f32), NH_FFN)
    g01 = gains_for(norm_gains, [0, 1])
    res = _run(nc, [{"hT": hT[k], "modT": modT_for_core(mod, [0, 1], k // 4), "gains": g01, "w0_proj": wp, "cat0": cat[k],
                     "w1_in": w1_in, "w1_out": w1_out, "w2_in": w2_in, "w2_out": w2_out} for k in range(NCORES)])
    h_lat, _ = unpack_tokens([r["hT_out"] for r in res], D)
    a_lat, a_ctx = unpack_tokens([r["aT_out"] for r in res], D)
    og, _ = run_mixa1(a_lat, a_ctx, np.asarray(hgrn_w_in[0], f32), np.asarray(hgrn_lb_logits, f32), np.asarray(hgrn_o_norm[0], f32))
    nc = build_tok_launch(1024, subtiles_of(1024, 0), 1, [("proj", 0), ("ffn", 0, 1), ("final",)])
    hT = pack_lat(h_lat)
    ogT = pack_lat(og)
    wp = lay_w_out(np.asarray(hgrn_w_out[0], f32), 1)[0]
    w1_in = lay_w_in(np.asarray(ffn_w_in[1, 1], f32)); w1_out = lay_w_out(np.asarray(ffn_w_out[1, 1], f32), NH_FFN)
    g1 = gains_for(norm_gains, [1])
    fng = lay_vec(np.asarray(final_norm, f32))
    res = _run(nc, [{"hT": hT[k], "modT": modT_for_core(mod, [1], k // 4), "gains": g1, "w0_proj": wp, "cat0": ogT[k],
                     "w1_in": w1_in, "w1_out": w1_out, "fng": fng} for k in range(NCORES)])
    out = np.empty((2, 8192, D), f32)
    for k in range(NCORES):
        out[k // 4, (k % 4) * 2048:(k % 4 + 1) * 2048] = res[k]["yT"].T
    return out
```

```python
import numpy as np
import concourse.bass as bass
import concourse.mybir as mybir

F32 = mybir.dt.float32
BF16 = mybir.dt.bfloat16
I32 = mybir.dt.int32
AF = mybir.ActivationFunctionType
ALU = mybir.AluOpType

EPOCH = 24000
DMA_RING = 8


class Buf:
    __slots__ = ("t", "name", "lw", "rd")

    def __init__(self, t, name=""):
        self.t = t
        self.name = name
        self.lw = None
        self.rd = {}

    def __getitem__(self, idx):
        return self.t[idx]


class MK:
    def __init__(self, nc, same_engine_sync=True):
        self.nc = nc
        self.E = {"pe": nc.tensor, "act": nc.scalar, "dve": nc.vector, "pool": nc.gpsimd, "sp": nc.sync}
        self.cur_sem = {}
        self.cur_cnt = {}
        self.seen = {e: {} for e in self.E}
        self.same_engine_sync = same_engine_sync
        self.ring = {}
        self.ring_i = {}
        self.nsem = 0
        self._uid = 0
        self.all_dma_events = []

    def new_sem(self, name):
        self.nsem += 1
        cm = self.nc.semaphore(f"{name}_{self.nsem}")
        return cm.__enter__()

    def sb(self, name, shape, dtype):
        self._uid += 1
        return Buf(self.nc.alloc_sbuf_tensor(f"{name}_{self._uid}", list(shape), dtype), name)

    def ps(self, name, shape, dtype=F32):
        self._uid += 1
        return Buf(self.nc.alloc_psum_tensor(f"{name}_{self._uid}", list(shape), dtype), name)

    def dram(self, name, shape, dtype, kind="Internal"):
        return Buf(self.nc.dram_tensor(name, list(shape), dtype, kind=kind), name)

    def _need(self, eng, ev):
        if ev is None:
            return
        sem, val = ev
        seen = self.seen[eng]
        k = id(sem)
        if seen.get(k, (None, 0))[1] >= val:
            return
        self.E[eng].wait_ge(sem, val)
        seen[k] = (sem, val)

    def _deps(self, eng, reads, writes, skip_self=False):
        for b in reads:
            if b.lw is not None:
                self._need(eng, b.lw)
        for b in writes:
            if b.lw is not None:
                self._need(eng, b.lw)
            for ev in b.rd.values():
                self._need(eng, ev)

    def _tick(self, eng):
        if eng not in self.cur_sem or self.cur_cnt[eng] >= EPOCH:
            self.cur_sem[eng] = self.new_sem(f"s_{eng}")
            self.cur_cnt[eng] = 0
        self.cur_cnt[eng] += 1
        return (self.cur_sem[eng], self.cur_cnt[eng])

    def _commit(self, ev, reads, writes):
        for b in writes:
            b.lw = ev
            b.rd = {}
        for b in reads:
            if b not in writes:
                b.rd[id(ev[0])] = ev

    def op(self, eng, fn, reads=(), writes=(), track=True):
        self._deps(eng, reads, writes)
        ins = fn()
        if track:
            ev = self._tick(eng)
            ins.then_inc(ev[0], 1)
            if not self.same_engine_sync:
                self.seen[eng][id(ev[0])] = ev
            self._commit(ev, reads, writes)
        return ins

    def mm_group(self, out_buf, mms, reads):
        self._deps("pe", reads, [out_buf])
        n = len(mms)
        ins = None
        for i, (o, l, r) in enumerate(mms):
            ins = self.nc.tensor.matmul(o, l, r, start=(i == 0), stop=(i == n - 1))
        ev = self._tick("pe")
        ins.then_inc(ev[0], 1)
        self.seen["pe"][id(ev[0])] = ev
        self._commit(ev, reads, [out_buf])

    def dma(self, q, out_ap, in_ap, reads=(), writes=(), **kw):
        self._deps(q, reads, writes)
        ring = self.ring.setdefault(q, [[None, 0] for _ in range(DMA_RING)])
        i = self.ring_i.get(q, 0)
        self.ring_i[q] = (i + 1) % DMA_RING
        slot = ring[i]
        if slot[0] is None or slot[1] >= EPOCH:
            if slot[0] is not None:
                self._need(q, (slot[0], slot[1]))
            slot[0] = self.new_sem(f"d_{q}{i}")
            slot[1] = 0
        elif slot[1] > 0:
            self._need(q, (slot[0], slot[1]))
        ins = self.E[q].dma_start(out=out_ap, in_=in_ap, **kw)
        slot[1] += 16
        ev = (slot[0], slot[1])
        ins.then_inc(ev[0], 16)
        self._commit(ev, reads, writes)
        self.all_dma_events.append((q, ev))
        return ev

    def finish(self, out_bufs):
        for b in out_bufs:
            if b.lw is not None:
                self._need("sp", b.lw)
        for q, ring in self.ring.items():
            for sem, cnt in ring:
                if sem is not None and cnt > 0:
                    self._need("sp", (sem, cnt))


def _mm(self, out_bufs, mms, reads, start, stop):
    self._deps("pe", reads, out_bufs)
    ins = None
    for (o, l, r) in mms:
        ins = self.nc.tensor.matmul(o, l, r, start=start, stop=stop)
    ev = self._tick("pe")
    ins.then_inc(ev[0], 1)
    self.seen["pe"][id(ev[0])] = ev
    self._commit(ev, reads, out_bufs)


def _mm_multi(self, out_bufs, mms, reads, nk):
    self._deps("pe", reads, out_bufs)
    ins = None
    for (_, ki, o, l, r) in mms:
        ins = self.nc.tensor.matmul(o, l, r, start=(ki == 0), stop=(ki == nk - 1))
    ev = self._tick("pe")
    ins.then_inc(ev[0], 1)
    self.seen["pe"][id(ev[0])] = ev
    self._commit(ev, reads, out_bufs)


MK.mm = _mm
MK.mm_multi = _mm_multi


def _mmf(self, out_bufs, mms, reads):
    self._deps("pe", reads, out_bufs)
    ins = None
    for (o, l, r, st, sp) in mms:
        ins = self.nc.tensor.matmul(o, l, r, start=st, stop=sp)
    ev = self._tick("pe")
    ins.then_inc(ev[0], 1)
    self.seen["pe"][id(ev[0])] = ev
    self._commit(ev, reads, out_bufs)


def _barrier(self):
    evs = [(self.cur_sem[e], self.cur_cnt[e]) for e in self.cur_sem]
    for q, ring in self.ring.items():
        for sem, cnt in ring:
            if sem is not None and cnt > 0:
                evs.append((sem, cnt))
    for e in self.E:
        for ev in evs:
            self._need(e, ev)


MK.mmf = _mmf
MK.barrier = _barrier

D = 2048
KC = 16
FFN = 5632
EPS = 1e-6


def subtiles_of(ncols_lat, ncols_ctx):
    st = []
    off = 0
    while off < ncols_lat:
        sz = min(512, ncols_lat - off)
        st.append((off, sz, 0))
        off += sz
    if ncols_ctx:
        st.append((off, ncols_ctx, 1))
    return st


class Pools:
    def __init__(self, mk, PW, JH):
        self.PW = PW
        self.JH = JH
        self.aT = mk.sb("aT", [128, KC, PW], BF16)
        self.gT = [mk.sb(f"gT{j}", [128, PW], BF16) for j in range(JH)]
        self.hst = [mk.sb(f"hst{i}", [128, PW], F32) for i in range(3)]
        self.sq = [mk.sb(f"sq{i}", [128, PW], BF16) for i in range(2)]
        self.R = mk.sb("R", [128, PW], F32)
        self.tmp = [mk.sb(f"tmp{i}", [128, PW], F32) for i in range(2)]
        self.win = [mk.sb(f"win{i}", [128, KC, 256], BF16) for i in range(3)]
        self.wout = [mk.sb(f"wout{i}", [128, JH, 128], BF16) for i in range(2)]
        self.ones = mk.sb("ones", [128, 128], BF16)
        self.bank = [mk.ps(f"bank{i}", [128, 512], F32) for i in range(8)]
        self.cnt = {"hst": 0, "sq": 0, "tmp": 0, "win": 0, "wout": 0}
        mk.op("dve", lambda: mk.nc.vector.memset(self.ones.t[:, :], 1.0), writes=[self.ones])

    def nxt(self, kind):
        lst = getattr(self, kind)
        i = self.cnt[kind]
        self.cnt[kind] = i + 1
        return lst[i % len(lst)]


def emit_norm(mk, P, src_ap_fn, src_bufs, col0, subt, S_ap_fn, T_ap_fn, out_fn, post_fn=None):
    nc = mk.nc
    PW = sum(s[1] for s in subt)
    nst = len(subt)
    banks = P.bank[0:nst]
    for i in range(KC):
        hs = P.nxt("hst")
        mk.dma("sp", hs.t[:, 0:PW], src_ap_fn(i), reads=[src_bufs[i]], writes=[hs])
        sq = P.nxt("sq")
        mk.op("act", lambda: nc.scalar.activation(out=sq.t[:, 0:PW], in_=hs.t[:, 0:PW], func=AF.Square),
              reads=[hs], writes=[sq])
        mms = [(banks[k].t[:, 0:sz], P.ones.t[:, :], sq.t[:, off:off + sz]) for k, (off, sz, _) in enumerate(subt)]
        mk.mm(banks, mms, reads=[sq, P.ones], start=(i == 0), stop=(i == KC - 1))
    for k, (off, sz, _) in enumerate(subt):
        mk.op("dve", lambda: nc.vector.tensor_scalar(out=P.R.t[:, off:off + sz], in0=banks[k].t[:, 0:sz],
                                                     scalar1=1.0 / D, scalar2=EPS, op0=ALU.mult, op1=ALU.add),
              reads=[banks[k]], writes=[P.R])
    mk.op("act", lambda: nc.scalar.activation(out=P.R.t[:, 0:PW], in_=P.R.t[:, 0:PW], func=AF.Sqrt),
          reads=[P.R], writes=[P.R])
    mk.op("dve", lambda: nc.vector.reciprocal(out=P.R.t[:, 0:PW], in_=P.R.t[:, 0:PW]), reads=[P.R], writes=[P.R])
    for i in range(KC):
        hs = P.nxt("hst")
        mk.dma("sp", hs.t[:, 0:PW], src_ap_fn(i), reads=[src_bufs[i]], writes=[hs])
        tm = P.nxt("tmp")
        mk.op("dve", lambda: nc.vector.tensor_tensor(out=tm.t[:, 0:PW], in0=hs.t[:, 0:PW], in1=P.R.t[:, 0:PW],
                                                     op=ALU.mult), reads=[hs, P.R], writes=[tm])
        groups = {}
        for (off, sz, mc) in subt:
            if mc in groups and groups[mc][0] + groups[mc][1] == off:
                groups[mc][1] += sz
            else:
                groups[mc] = [off, sz]
        for mc, (off, sz) in groups.items():
            o_ap, o_buf = out_fn(i, off, sz)
            Tb = T_ap_fn(i, mc) if T_ap_fn is not None else 0.0
            mk.op("act", lambda: nc.scalar.activation(out=o_ap, in_=tm.t[:, off:off + sz], func=AF.Identity,
                                                      bias=Tb, scale=S_ap_fn(i, mc)),
                  reads=[tm], writes=[o_buf])
        if post_fn is not None:
            post_fn(i)


def emit_ffn(mk, P, src_ap_fn, dst_ap_fn, hbufs_src, hbufs_dst, subt, S_fn, T_fn, G_fn,
             w_in_d, w_out_d, J, NH, wbuf):
    nc = mk.nc
    PW = sum(s[1] for s in subt)
    nst = len(subt)
    JH = J // NH
    assert nst <= 3
    emit_norm(mk, P, src_ap_fn, hbufs_src, 0, subt, S_fn, T_fn,
              lambda i, off, sz: (P.aT.t[:, i, off:off + sz], P.aT))
    setA = P.bank[0:nst]
    setB = P.bank[3:3 + nst]
    for hf in range(NH):
        for jj in range(JH):
            j = hf * JH + jj
            w = P.nxt("win")
            mk.dma("pool", w.t[:, :, :], w_in_d[j], reads=[wbuf], writes=[w])
            for part, pset in ((0, setA), (1, setB)):
                mms = []
                for kc in range(KC):
                    for k, (off, sz, _) in enumerate(subt):
                        mms.append((k, kc, pset[k].t[:, 0:sz], w.t[:, kc, part * 128:(part + 1) * 128],
                                    P.aT.t[:, kc, off:off + sz]))
                mk.mm_multi(pset, mms, reads=[w, P.aT], nk=KC)
            tm = P.nxt("tmp")
            for k, (off, sz, _) in enumerate(subt):
                mk.op("act", lambda: nc.scalar.activation(out=tm.t[:, off:off + sz], in_=setA[k].t[:, 0:sz],
                                                          func=AF.Silu), reads=[setA[k]], writes=[tm])
            for k, (off, sz, _) in enumerate(subt):
                mk.op("dve", lambda: nc.vector.tensor_tensor(out=P.gT[jj].t[:, off:off + sz], in0=tm.t[:, off:off + sz],
                                                             in1=setB[k].t[:, 0:sz], op=ALU.mult),
                      reads=[tm, setB[k]], writes=[P.gT[jj]])
        emit_outproj(mk, P, lambda jj, off, sz: P.gT[jj].t[:, off:off + sz], P.gT[0:JH], JH,
                     lambda i: w_out_d[hf, i], wbuf,
                     (src_ap_fn if hf == 0 else dst_ap_fn), (hbufs_src if hf == 0 else hbufs_dst),
                     dst_ap_fn, hbufs_dst, subt, G_fn)


def emit_outproj(mk, P, rhs_fn, rhs_bufs, nk, w_d_fn, wbuf, src_ap_fn, hbufs_src, dst_ap_fn, hbufs_dst, subt, G_fn):
    nc = mk.nc
    PW = sum(s[1] for s in subt)
    nst = len(subt)
    setA = P.bank[0:nst]
    setB = P.bank[3:3 + nst]
    for i in range(KC):
        wo = P.nxt("wout")
        mk.dma("pool", wo.t[:, 0:nk, :], w_d_fn(i), reads=[wbuf], writes=[wo])
        pset = setA if (i % 2 == 0) else setB
        mms = []
        for jj in range(nk):
            for k, (off, sz, _) in enumerate(subt):
                mms.append((k, jj, pset[k].t[:, 0:sz], wo.t[:, jj, :], rhs_fn(jj, off, sz)))
        mk.mm_multi(pset, mms, reads=[wo] + list(rhs_bufs), nk=nk)
        hs = P.nxt("hst")
        mk.dma("sp", hs.t[:, 0:PW], src_ap_fn(i), reads=[hbufs_src[i]], writes=[hs])
        for k, (off, sz, mc) in enumerate(subt):
            mk.op("dve", lambda: nc.vector.scalar_tensor_tensor(out=hs.t[:, off:off + sz], in0=pset[k].t[:, 0:sz],
                                                                scalar=G_fn(i, mc), in1=hs.t[:, off:off + sz],
                                                                op0=ALU.mult, op1=ALU.add),
                  reads=[pset[k], hs], writes=[hs])
        mk.dma("sp", dst_ap_fn(i), hs.t[:, 0:PW], reads=[hs], writes=[hbufs_dst[i]])


def emit_prep_mod(mk, mod_sb, gain_sb, S_sb, G_sb):
    nc = mk.nc
    for k in range(3):
        for mc in range(2):
            mk.op("dve", lambda: nc.vector.scalar_tensor_tensor(
                out=S_sb.t[:, k, :, mc], in0=mod_sb.t[:, (3 * k + 1) * KC:(3 * k + 2) * KC, mc], scalar=1.0,
                in1=gain_sb.t[:, k, :], op0=ALU.add, op1=ALU.mult), reads=[mod_sb, gain_sb], writes=[S_sb])
            mk.op("dve", lambda: nc.vector.tensor_scalar(
                out=G_sb.t[:, k, :, mc], in0=mod_sb.t[:, (3 * k + 2) * KC:(3 * k + 3) * KC, mc],
                scalar1=(1.0 if k == 1 else 0.5), scalar2=None, op0=ALU.mult), reads=[mod_sb], writes=[G_sb])


def emit_mod(mk, P, cT_d, w_d, b_d, out_d, nch, wbuf):
    nc = mk.nc
    c_sb = mk.sb("mod_c", [128, KC, 3], F32)
    sc_sb = mk.sb("mod_sc", [128, KC, 3], F32)
    wsl = [mk.sb(f"mod_w{i}", [128, KC, 128], F32) for i in range(3)]
    b_sb = mk.sb("mod_b", [128, nch], F32)
    o_sb = mk.sb("mod_o", [128, nch, 3], F32)
    mk.dma("sp", c_sb.t[:, :, :], cT_d, reads=[wbuf], writes=[c_sb])
    mk.dma("sp", b_sb.t[:, :], b_d, reads=[wbuf], writes=[b_sb])
    mk.op("act", lambda: nc.scalar.activation(out=sc_sb.t[:, :, :], in_=c_sb.t[:, :, :], func=AF.Silu),
          reads=[c_sb], writes=[sc_sb])
    for ch in range(nch):
        wo = wsl[ch % 3]
        mk.dma("sp", wo.t[:, 0:KC, :], w_d[ch], reads=[wbuf], writes=[wo])
        bk = P.bank[ch % 8]
        mms = [(0, kc, bk.t[:, 0:3], wo.t[:, kc, :], sc_sb.t[:, kc, :]) for kc in range(KC)]
        mk.mm_multi([bk], mms, reads=[wo, sc_sb], nk=KC)
        mk.op("act", lambda: nc.scalar.activation(out=o_sb.t[:, ch, :], in_=bk.t[:, 0:3], func=AF.Identity,
                                                  bias=b_sb.t[:, ch:ch + 1], scale=1.0),
              reads=[bk, b_sb], writes=[o_sb])
    mk.dma("sp", out_d, o_sb.t[:, :, :], reads=[o_sb], writes=[wbuf])

import ml_dtypes
from concourse.bass_utils import run_bass_kernel_spmd

NCORES = 8
NL = 2
J_FFN = FFN // 128
NH_FFN = 2
NPASS = 2


def build_tok_launch(PW, subt, n_layers, steps, out_h=True):
    nc = bass.Bass("TRN2", target_bir_lowering=False)
    mk = MK(nc)
    JH = J_FFN // NH_FFN
    P = Pools(mk, PW, JH)
    NT = NPASS * PW
    hin = nc.dram_tensor("hT", [D, NT], F32, kind="ExternalInput").ap()
    hout = nc.dram_tensor("hT_out", [D, NT], F32, kind=("ExternalOutput" if out_h else "Internal")).ap()
    modT = nc.dram_tensor("modT", [n_layers, 128, 144, 2], F32, kind="ExternalInput").ap()
    gains = nc.dram_tensor("gains", [n_layers, 128, 3, KC], F32, kind="ExternalInput").ap()
    wbuf = Buf(None, "weights")
    mod_sb, S_sb, G_sb = [], [], []
    for li in range(n_layers):
        m = mk.sb(f"mod{li}", [128, 144, 2], F32)
        g = mk.sb(f"gain{li}", [128, 3, KC], F32)
        S = mk.sb(f"S{li}", [128, 3, KC, 2], F32)
        G = mk.sb(f"G{li}", [128, 3, KC, 2], F32)
        mk.dma("sp", m.t[:, :, :], modT[li], reads=[wbuf], writes=[m])
        mk.dma("sp", g.t[:, :, :], gains[li], reads=[wbuf], writes=[g])
        emit_prep_mod(mk, m, g, S, G)
        mod_sb.append(m); S_sb.append(S); G_sb.append(G)
    hb_in = [[Buf(None, f"hi{p}_{i}") for i in range(KC)] for p in range(NPASS)]
    hb_out = [[Buf(None, f"ho{p}_{i}") for i in range(KC)] for p in range(NPASS)]
    outs = [b for p in range(NPASS) for b in hb_out[p]]
    in_place = False
    for si, st in enumerate(steps):
        kind = st[0]
        if kind == "ffn":
            _, li, f, = st
            w_in_d = nc.dram_tensor(f"w{si}_in", [J_FFN, 128, KC, 256], F32, kind="ExternalInput").ap()
            w_out_d = nc.dram_tensor(f"w{si}_out", [NH_FFN, KC, 128, JH, 128], F32, kind="ExternalInput").ap()
            k = 0 if f == 0 else 2
        elif kind == "proj":
            _, li = st
            w_p_d = nc.dram_tensor(f"w{si}_proj", [KC, 128, KC, 128], F32, kind="ExternalInput").ap()
            cat_d = nc.dram_tensor(f"cat{si}", [D, NT], BF16, kind="ExternalInput").ap()
            catb = Buf(None, "cat")
        elif kind == "normout":
            _, li = st
            aT_d = nc.dram_tensor("aT_out", [D, NT], BF16, kind="ExternalOutput").ap()
            aTb = Buf(None, "aT_out")
            outs.append(aTb)
        elif kind == "final":
            fng_d = nc.dram_tensor("fng", [128, KC], F32, kind="ExternalInput").ap()
            y_d = nc.dram_tensor("yT", [D, NT], F32, kind="ExternalOutput").ap()
            fng = mk.sb("fng", [128, KC], F32)
            mk.dma("sp", fng.t[:, :], fng_d, reads=[wbuf], writes=[fng])
            yb = Buf(None, "yT")
            outs.append(yb)
        for p in range(NPASS):
            c0 = p * PW
            src = (hout if in_place else hin)
            sb_ = (hb_out if in_place else hb_in)[p]
            src_fn = (lambda i, src=src, c0=c0: src[i * 128:(i + 1) * 128, c0:c0 + PW])
            dst_fn = (lambda i, c0=c0: hout[i * 128:(i + 1) * 128, c0:c0 + PW])
            if kind == "ffn":
                emit_ffn(mk, P, src_fn, dst_fn, sb_, hb_out[p], subt,
                         lambda i, mc: S_sb[li].t[:, k, i, mc:mc + 1],
                         lambda i, mc: mod_sb[li].t[:, 3 * k * KC + i, mc:mc + 1],
                         lambda i, mc: G_sb[li].t[:, k, i, mc:mc + 1],
                         w_in_d, w_out_d, J_FFN, NH_FFN, wbuf)
            elif kind == "proj":
                mk.dma("sp", P.aT.t[:, :, 0:PW], cat_d[:, c0:c0 + PW].rearrange("(kc p) t -> p kc t", p=128),
                       reads=[catb], writes=[P.aT])
                emit_outproj(mk, P, lambda jj, off, sz: P.aT.t[:, jj, off:off + sz], [P.aT], KC,
                             lambda i: w_p_d[i], wbuf, src_fn, sb_, dst_fn, hb_out[p], subt,
                             lambda i, mc: G_sb[li].t[:, 1, i, mc:mc + 1])
            elif kind == "normout":
                emit_norm(mk, P, src_fn, sb_, 0, subt,
                          lambda i, mc: S_sb[li].t[:, 1, i, mc:mc + 1],
                          lambda i, mc: mod_sb[li].t[:, 3 * KC + i, mc:mc + 1],
                          lambda i, off, sz: (P.aT.t[:, i, off:off + sz], P.aT))
                mk.dma("sp", aT_d[:, c0:c0 + PW].rearrange("(kc p) t -> p kc t", p=128), P.aT.t[:, :, 0:PW],
                       reads=[P.aT], writes=[aTb])
            elif kind == "final":
                cur = {}

                def out_fn(i, off, sz):
                    if "t" not in cur or cur["i"] != i:
                        cur["t"] = P.nxt("hst"); cur["i"] = i
                    return cur["t"].t[:, off:off + sz], cur["t"]

                def post_fn(i, c0=c0):
                    mk.dma("sp", y_d[i * 128:(i + 1) * 128, c0:c0 + PW], cur["t"].t[:, 0:PW], reads=[cur["t"]], writes=[yb])
                emit_norm(mk, P, src_fn, sb_, 0, subt, lambda i, mc: fng.t[:, i:i + 1], None, out_fn, post_fn)
        if kind in ("ffn", "proj"):
            in_place = True
    mk.finish(outs)
    return nc


def build_mod_launch(nch):
    nc = bass.Bass("TRN2", target_bir_lowering=False)
    mk = MK(nc)
    P = Pools(mk, 64, KC)
    cT = nc.dram_tensor("cT", [128, KC, 3], F32, kind="ExternalInput").ap()
    w = nc.dram_tensor("w", [nch, 128, KC, 128], F32, kind="ExternalInput").ap()
    b = nc.dram_tensor("b", [128, nch], F32, kind="ExternalInput").ap()
    o = nc.dram_tensor("o", [128, nch, 3], F32, kind="ExternalOutput").ap()
    wbuf = Buf(None, "io")
    emit_mod(mk, P, cT, w, b, o, nch, wbuf)
    mk.finish([wbuf])
    return nc


def lay_w_in(w):
    F_ = w.shape[1] // 2
    Jn = F_ // 128
    return np.ascontiguousarray(w.reshape(KC, 128, 2, Jn, 128).transpose(3, 1, 0, 2, 4).reshape(Jn, 128, KC, 256))


def lay_w_out(w, NH):
    F_ = w.shape[0]
    JH = F_ // 128 // NH
    return np.ascontiguousarray(w.reshape(NH, JH, 128, KC, 128).transpose(0, 3, 2, 1, 4))


def lay_vec(v):
    n = v.shape[-1] // 128
    r = v.reshape(v.shape[:-1] + (n, 128))
    return np.ascontiguousarray(np.moveaxis(r, -1, 0))


def run_mod(c, c_ctx, w_mod, b_mod):
    ncols = 9 * D
    nch_total = NL * ncols // 128
    nch = nch_total // NCORES
    cT = lay_vec(np.stack([c[0], c[1], c_ctx], 0))
    cT = np.ascontiguousarray(cT.transpose(0, 2, 1))
    wl = w_mod.reshape(NL, KC, 128, ncols // 128, 128).transpose(0, 3, 2, 1, 4).reshape(nch_total, 128, KC, 128)
    bl = b_mod.reshape(nch_total, 128).T
    nc = build_mod_launch(nch)
    in_maps = []
    for cidx in range(NCORES):
        sl = slice(cidx * nch, (cidx + 1) * nch)
        in_maps.append({"cT": cT, "w": np.ascontiguousarray(wl[sl]), "b": np.ascontiguousarray(bl[:, sl])})
    res = run_bass_kernel_spmd(nc, in_maps, core_ids=list(range(NCORES)))
    o = np.concatenate([r["o"] for r in res.results], axis=1)
    mod = o.transpose(2, 1, 0).reshape(3, NL, ncols).transpose(1, 0, 2)
    return np.ascontiguousarray(mod)


def modT_for_core(mod, layers, b):
    out = []
    for l in layers:
        m = mod[l][[b, 2]]
        out.append(m.reshape(2, 144, 128).transpose(2, 1, 0))
    return np.ascontiguousarray(np.stack(out, 0))


def gains_for(norm_gains, layers):
    return np.ascontiguousarray(np.stack([lay_vec(norm_gains[l]) for l in layers], 0))


PWL = 1056
SUBT_L = subtiles_of(1024, 32)


def pack_tokens(lat, ctxa, dtype=None):
    outs = []
    for cidx in range(NCORES):
        b, j = divmod(cidx, 4)
        cols = []
        for p in range(NPASS):
            cols.append(lat[b, j * 2048 + p * 1024: j * 2048 + (p + 1) * 1024])
            cols.append(ctxa[b, j * 64 + p * 32: j * 64 + (p + 1) * 32])
        outs.append(np.ascontiguousarray(np.concatenate(cols, 0).T))
    return outs


def unpack_tokens(per_core, C):
    dt = per_core[0].dtype
    lat = np.empty((2, 8192, C), dt)
    ctxa = np.empty((2, 256, C), dt)
    for cidx in range(NCORES):
        b, j = divmod(cidx, 4)
        a = per_core[cidx].T
        for p in range(NPASS):
            base = p * PWL
            lat[b, j * 2048 + p * 1024: j * 2048 + (p + 1) * 1024] = a[base:base + 1024]
            ctxa[b, j * 64 + p * 32: j * 64 + (p + 1) * 32] = a[base + 1024:base + 1056]
    return lat, ctxa

NLAT = 8192
NCTX = 256
NTOK = NLAT + NCTX
HD = 128
ATTN_SCALE = HD ** -0.5


def emit_qknorm_rope(mk, W, ps, gain_ap, dst_ap, dst_buf, T, cs_aps=None):
    nc = mk.nc
    mk.op("act", lambda: nc.scalar.activation(out=T["sq"].t[:, 0:W], in_=ps.t[:, 0:W], func=AF.Square),
          reads=[ps], writes=[T["sq"]])
    mk.mm([T["ssq"]], [(T["ssq"].t[:, 0:W], T["ones"].t[:, :], T["sq"].t[:, 0:W])], reads=[T["sq"], T["ones"]],
          start=True, stop=True)
    mk.op("dve", lambda: nc.vector.tensor_scalar(out=T["r"].t[:, 0:W], in0=T["ssq"].t[:, 0:W], scalar1=1.0 / HD,
                                                 scalar2=EPS, op0=ALU.mult, op1=ALU.add), reads=[T["ssq"]], writes=[T["r"]])
    mk.op("act", lambda: nc.scalar.activation(out=T["r"].t[:, 0:W], in_=T["r"].t[:, 0:W], func=AF.Sqrt),
          reads=[T["r"]], writes=[T["r"]])
    mk.op("dve", lambda: nc.vector.reciprocal(out=T["r"].t[:, 0:W], in_=T["r"].t[:, 0:W]), reads=[T["r"]], writes=[T["r"]])
    if cs_aps is None:
        mk.op("dve", lambda: nc.vector.scalar_tensor_tensor(out=dst_ap, in0=ps.t[:, 0:W], scalar=gain_ap, in1=T["r"].t[:, 0:W],
                                                            op0=ALU.mult, op1=ALU.mult), reads=[ps, T["r"]], writes=[dst_buf])
        return
    C_ap, S_ap, csbuf = cs_aps
    mk.op("dve", lambda: nc.vector.scalar_tensor_tensor(out=T["qn"].t[:, 0:W], in0=ps.t[:, 0:W], scalar=gain_ap,
                                                        in1=T["r"].t[:, 0:W], op0=ALU.mult, op1=ALU.mult),
          reads=[ps, T["r"]], writes=[T["qn"]])
    mk.op("act", lambda: nc.scalar.copy(out=T["qnb"].t[:, 0:W], in_=T["qn"].t[:, 0:W]), reads=[T["qn"]], writes=[T["qnb"]])
    mk.mm([T["rot"]], [(T["rot"].t[:, 0:W], T["rmat"].t[:, 0, :], T["qnb"].t[:, 0:W])], reads=[T["qnb"], T["rmat"]],
          start=True, stop=True)
    mk.op("dve", lambda: nc.vector.tensor_tensor(out=T["t1"].t[:, 0:W], in0=T["qn"].t[:, 0:W], in1=C_ap, op=ALU.mult),
          reads=[T["qn"], csbuf], writes=[T["t1"]])
    mk.op("dve", lambda: nc.vector.tensor_tensor(out=T["t2"].t[:, 0:W], in0=T["rot"].t[:, 0:W], in1=S_ap, op=ALU.mult),
          reads=[T["rot"], csbuf], writes=[T["t2"]])
    mk.op("pool", lambda: nc.gpsimd.tensor_tensor(out=dst_ap, in0=T["t1"].t[:, 0:W], in1=T["t2"].t[:, 0:W], op=ALU.add),
          reads=[T["t1"], T["t2"]], writes=[dst_buf])


def build_mixa0_launch():
    nc = bass.Bass("TRN2", target_bir_lowering=False)
    mk = MK(nc)
    aT_d = nc.dram_tensor("aT", [D, NTOK], BF16, kind="ExternalInput").ap()
    wsel_d = nc.dram_tensor("wsel", [128, KC, 768], F32, kind="ExternalInput").ap()
    gqk_d = nc.dram_tensor("gqk", [128, 2], F32, kind="ExternalInput").ap()
    cos_d = nc.dram_tensor("cosT", [128, NLAT], F32, kind="ExternalInput").ap()
    sin_d = nc.dram_tensor("sinT", [128, NLAT], F32, kind="ExternalInput").ap()
    cmat_d = nc.dram_tensor("cmat", [128, 4, 128], BF16, kind="ExternalInput").ap()
    dftL_d = nc.dram_tensor("dftL", [16, 8, 2, 128, 8, 512], BF16, kind="ExternalInput").ap()
    dftC_d = nc.dram_tensor("dftC", [2, 128, 2, 256], BF16, kind="ExternalInput").ap()
    cat_d = nc.dram_tensor("cat", [512, NTOK], BF16, kind="ExternalOutput").ap()
    io = Buf(None, "io")
    catb = Buf(None, "cat")
    KT = mk.sb("KT", [128, NTOK], BF16)
    V = mk.sb("V", [128, 66, 128], BF16)
    UW = mk.sb("UW", [128, 66, 256], BF16)
    wsel = mk.sb("wsel", [128, KC, 768], BF16)
    cmat = mk.sb("cmat", [128, 4, 128], BF16)
    gqk = mk.sb("gqk", [128, 2], F32)
    gq = mk.sb("gq", [128, 1], F32)
    QT = mk.sb("QT", [128, NLAT], BF16)
    QC = mk.sb("QC", [128, NCTX], BF16)
    aslot = [mk.sb(f"aslot{i}", [128, KC, 512], BF16) for i in range(2)]
    cslot = [mk.sb(f"cslot{i}", [128, 2, 512], F32) for i in range(2)]
    T = {n: mk.sb(n, [128, 512], F32) for n in ("r", "qn", "t1", "t2")}
    T["sq"] = mk.sb("sq", [128, 512], BF16)
    T["qnb"] = mk.sb("qnb", [128, 512], BF16)
    T["ones"] = mk.sb("ones", [128, 128], BF16)
    fT = mk.sb("fT", [128, 512], BF16)
    PT = [mk.sb(f"PT{i}", [128, 512], BF16) for i in range(4)]
    rden = mk.sb("rden", [128, 512], F32)
    otile = [mk.sb(f"otile{i}", [128, 512], BF16) for i in range(2)]
    dslot = [mk.sb(f"dslot{i}", [128, 16, 512], BF16) for i in range(2)]
    dtile = mk.sb("dtile", [128, 512], BF16)
    bank = [mk.ps(f"bank{i}", [128, 512], F32) for i in range(8)]
    T["ssq"] = bank[6]
    T["rot"] = bank[7]
    mk.op("dve", lambda: nc.vector.memset(T["ones"].t[:, :], 1.0), writes=[T["ones"]])
    mk.dma("pool", wsel.t[:, :, :], wsel_d, reads=[io], writes=[wsel])
    mk.dma("sp", cmat.t[:, :, :], cmat_d, reads=[io], writes=[cmat])
    mk.dma("sp", gqk.t[:, :], gqk_d, reads=[io], writes=[gqk])
    mk.op("dve", lambda: nc.vector.tensor_scalar(out=gq.t[:, :], in0=gqk.t[:, 0:1], scalar1=ATTN_SCALE, scalar2=None,
                                                 op0=ALU.mult), reads=[gqk], writes=[gq])
    T["rmat"] = cmat
    tiles = [(t * 512, 512, True) for t in range(16)] + [(NLAT, NCTX, False)]
    cnt = {"a": 0, "c": 0}

    def load_tile(c0, W, lat):
        a = aslot[cnt["a"] % 2]; cnt["a"] += 1
        mk.dma("sp", a.t[:, :, 0:W], aT_d[:, c0:c0 + W].rearrange("(kc p) t -> p kc t", p=128), reads=[io], writes=[a])
        cs = None
        if lat:
            c = cslot[cnt["c"] % 2]; cnt["c"] += 1
            mk.dma("sp", c.t[:, 0, 0:W], cos_d[:, c0:c0 + W], reads=[io], writes=[c])
            mk.dma("sp", c.t[:, 1, 0:W], sin_d[:, c0:c0 + W], reads=[io], writes=[c])
            cs = (c.t[:, 0, 0:W], c.t[:, 1, 0:W], c)
        return a, cs

    def proj(a, W, col0, pb):
        mms = [(0, kc, pb.t[:, 0:W], wsel.t[:, kc, col0:col0 + 128], a.t[:, kc, 0:W]) for kc in range(KC)]
        mk.mm_multi([pb], mms, reads=[wsel, a], nk=KC)

    for ti, (c0, W, lat) in enumerate(tiles):
        a, cs = load_tile(c0, W, lat)
        nsb = W // 128
        ch0 = c0 // 128
        pk = bank[ti % 2]
        proj(a, W, 512, pk)
        emit_qknorm_rope(mk, W, pk, gqk.t[:, 1:2], KT.t[:, c0:c0 + W], KT, T, cs)
        pv = bank[2]
        mms = []
        for sb_ in range(nsb):
            for kc in range(KC):
                mms.append((0, kc, pv.t[:, sb_ * 128:(sb_ + 1) * 128], a.t[:, kc, sb_ * 128:(sb_ + 1) * 128], wsel.t[:, kc, 640:768]))
        mk.mm_multi([pv], mms, reads=[wsel, a], nk=KC)
        mk.op("act", lambda: nc.scalar.copy(out=V.t[:, ch0:ch0 + nsb, :], in_=pv.t[:, 0:W].rearrange("p (s c) -> p s c", c=128)),
              reads=[pv], writes=[V])
        pf = bank[3]
        proj(a, W, 0, pf)
        mk.op("act", lambda: nc.scalar.copy(out=fT.t[:, 0:W], in_=pf.t[:, 0:W]), reads=[pf], writes=[fT])
        for half in range((nsb + 1) // 2):
            pu = bank[4 + half]
            nn = min(2, nsb - half * 2)
            for s2 in range(nn):
                sb_ = half * 2 + s2
                mk.mm([pu], [(pu.t[:, s2 * 256:(s2 + 1) * 256], fT.t[:, sb_ * 128:(sb_ + 1) * 128],
                              cmat.t[:, 1:3, :].rearrange("p a c -> p (a c)"))], reads=[fT, cmat], start=True, stop=True)
            mk.op("dve", lambda: nc.vector.tensor_copy(out=UW.t[:, ch0 + half * 2:ch0 + half * 2 + nn, :],
                                                       in_=pu.t[:, 0:nn * 256].rearrange("p (s c) -> p s c", c=256)),
                  reads=[pu], writes=[UW])
    T["ssq"] = bank[2]
    T["rot"] = bank[3]
    py = bank[7]

    def dft_units():
        units = [(kt, ng) for kt in range(16) for ng in range(8)]

        def load(u):
            kt, ng = units[u]
            ds_ = dslot[u % 2]
            for cs_ in range(2):
                mk.dma("sp", ds_.t[:, cs_ * 8:(cs_ + 1) * 8, :], dftL_d[kt, ng, cs_], reads=[io], writes=[ds_])
        load(0)
        load(1)
        yield
        for u, (kt, ng) in enumerate(units):
            ds_ = dslot[u % 2]
            mms = []
            for n8 in range(8):
                nch = ng * 8 + n8
                mms.append((py.t[:, :], UW.t[:, nch, 0:128], ds_.t[:, n8, :], (nch == 0), False))
                mms.append((py.t[:, :], UW.t[:, nch, 128:256], ds_.t[:, 8 + n8, :], False, (nch == 63)))
            mk.mmf([py], mms, reads=[UW, ds_])
            if u + 2 < len(units):
                load(u + 2)
            if ng == 7:
                mk.op("act", lambda: nc.scalar.copy(out=dtile.t[:, :], in_=py.t[:, :]), reads=[py], writes=[dtile])
                mk.dma("sp", cat_d[0:128, kt * 512:(kt + 1) * 512], dtile.t[:, :], reads=[dtile], writes=[catb])
            yield
    dgen = dft_units()
    next(dgen)

    def dft_step(n):
        for _ in range(n):
            try:
                next(dgen)
            except StopIteration:
                return
    for h in range(3):
        for ti, (c0, W, lat) in enumerate(tiles):
            a, cs = load_tile(c0, W, lat)
            pq = bank[ti % 2]
            proj(a, W, 128 + h * 128, pq)
            if lat:
                emit_qknorm_rope(mk, W, pq, gq.t[:, 0:1], QT.t[:, c0:c0 + W], QT, T, cs)
            else:
                emit_qknorm_rope(mk, W, pq, gq.t[:, 0:1], QC.t[:, 0:W], QC, T, None)
        qtiles = [(QT, t * 512, 512, t * 512, list(range(66))) for t in range(16)] + [(QC, 0, NCTX, NLAT, [64, 65])]
        for qi, (qbuf, q0, W, o0, kcs) in enumerate(qtiles):
            O = bank[3 + (qi % 2)]
            Dn = bank[5 + (qi % 2)]
            nk = len(kcs)

            def s_mm(idx):
                kc = kcs[idx]
                sb_ = bank[idx % 3]
                mk.mm([sb_], [(sb_.t[:, 0:W], KT.t[:, kc * 128:(kc + 1) * 128], qbuf.t[:, q0:q0 + W])], reads=[KT, qbuf],
                      start=True, stop=True)

            def e_act(idx):
                sb_ = bank[idx % 3]
                pt = PT[idx % 4]
                mk.op("act", lambda: nc.scalar.activation(out=pt.t[:, 0:W], in_=sb_.t[:, 0:W], func=AF.Exp),
                      reads=[sb_], writes=[pt])

            s_mm(0)
            if nk > 1:
                s_mm(1)
            e_act(0)
            for idx in range(nk):
                if nk > 2 and idx in (nk // 4, nk // 2, (3 * nk) // 4):
                    dft_step(1)
                if idx + 2 < nk:
                    s_mm(idx + 2)
                if idx + 1 < nk:
                    e_act(idx + 1)
                pt = PT[idx % 4]
                kc = kcs[idx]
                mk.mm([O, Dn], [(O.t[:, 0:W], V.t[:, kc, :], pt.t[:, 0:W]), (Dn.t[:, 0:W], T["ones"].t[:, :], pt.t[:, 0:W])],
                      reads=[pt, V, T["ones"]], start=(idx == 0), stop=(idx == nk - 1))
            mk.op("dve", lambda: nc.vector.reciprocal(out=rden.t[:, 0:W], in_=Dn.t[:, 0:W]), reads=[Dn], writes=[rden])
            ot = otile[qi % 2]
            mk.op("dve", lambda: nc.vector.tensor_tensor(out=ot.t[:, 0:W], in0=O.t[:, 0:W], in1=rden.t[:, 0:W], op=ALU.mult),
                  reads=[O, rden], writes=[ot])
            mk.dma("sp", cat_d[128 + h * 128:256 + h * 128, o0:o0 + W], ot.t[:, 0:W], reads=[ot], writes=[catb])
    dft_step(1000)
    dcnt = 0
    ds_ = dslot[0]
    for cs_ in range(2):
        mk.dma("sp", ds_.t[:, cs_ * 8:cs_ * 8 + 2, 0:256], dftC_d[cs_], reads=[io], writes=[ds_])
    py = bank[2]
    mms = []
    for n2 in range(2):
        mms.append((0, 2 * n2, py.t[:, 0:256], UW.t[:, 64 + n2, 0:128], ds_.t[:, n2, 0:256]))
        mms.append((0, 2 * n2 + 1, py.t[:, 0:256], UW.t[:, 64 + n2, 128:256], ds_.t[:, 8 + n2, 0:256]))
    mk.mm_multi([py], mms, reads=[UW, ds_], nk=4)
    ot = otile[0]
    mk.op("act", lambda: nc.scalar.copy(out=ot.t[:, 0:256], in_=py.t[:, 0:256]), reads=[py], writes=[ot])
    mk.dma("sp", cat_d[0:128, NLAT:NTOK], ot.t[:, 0:256], reads=[ot], writes=[catb])
    mk.finish([catb])
    return nc


_CONST = {}


def mixa0_consts():
    if "m0" in _CONST:
        return _CONST["m0"]
    bf = ml_dtypes.bfloat16
    rows = NLAT // 64
    row_id = np.repeat(np.arange(rows, dtype=np.float32), 64)
    col_id = np.tile(np.arange(64, dtype=np.float32), rows)
    inv_freq = (np.float32(10000.0) ** (-np.arange(0, 64, 2, dtype=np.float32) / np.float32(64))).astype(np.float32)
    ang = np.concatenate([row_id[:, None] * inv_freq, col_id[:, None] * inv_freq], axis=-1).astype(np.float32)
    cosL, sinL = np.cos(ang).astype(np.float32), np.sin(ang).astype(np.float32)
    pidx = np.arange(128)
    fidx = (pidx // 64) * 32 + (pidx % 32)
    cosT = np.ascontiguousarray(cosL[:, fidx].T)
    sinT = np.ascontiguousarray(sinL[:, fidx].T)
    rm = np.zeros((128, 128), np.float32)
    for p in range(128):
        if (p % 64) < 32:
            rm[p + 32, p] = -1.0
        else:
            rm[p - 32, p] = 1.0
    cc = np.arange(128)
    angc = 2 * np.pi * ((cc[:, None] * cc[None, :]) % 128) / 128.0
    Cc = np.cos(angc) / np.sqrt(128.0)
    Sc = np.sin(angc) / np.sqrt(128.0)
    cmat = np.stack([rm, Cc, Sc, np.zeros((128, 128))], 1).astype(bf)

    def dft(N):
        n = np.arange(N, dtype=np.int64)
        m = (n[:, None] * n[None, :]) % N
        a = m.astype(np.float32) * np.float32(2 * np.pi / N)
        s = np.float32(1.0 / np.sqrt(N))
        return (np.cos(a) * s).astype(bf), (-np.sin(a) * s).astype(bf)
    CL, SL = dft(NLAT)
    def layL(M):
        return M.reshape(8, 8, 128, 16, 512).transpose(3, 0, 2, 1, 4)
    dftL = np.ascontiguousarray(np.stack([layL(CL), layL(SL)], 2))
    Cx, Sx = dft(NCTX)
    layC = lambda M: M.reshape(2, 128, 256).transpose(1, 0, 2)
    dftC = np.ascontiguousarray(np.stack([layC(Cx), layC(Sx)], 0))
    _CONST["m0"] = dict(cosT=cosT, sinT=sinT, cmat=np.ascontiguousarray(cmat), dftL=dftL, dftC=dftC)
    return _CONST["m0"]


def mixa0_wsel(w_in, j):
    cols = np.concatenate([np.arange(j * 128, (j + 1) * 128),
                           512 + np.arange(3 * j * 128, (3 * j + 3) * 128),
                           2048 + np.arange(j * 128, (j + 1) * 128),
                           2560 + np.arange(j * 128, (j + 1) * 128)])
    w = w_in[:, cols]
    return np.ascontiguousarray(w.reshape(KC, 128, 768).transpose(1, 0, 2))


def run_mixa0(a_lat, a_ctx, ab_w_in, qk_norm, trace=False):
    C = mixa0_consts()
    nc = build_mixa0_launch()
    gqk = np.ascontiguousarray(qk_norm.T.astype(np.float32))
    aT = [np.ascontiguousarray(np.concatenate([a_lat[b], a_ctx[b]], 0).T) for b in range(2)]
    in_maps = []
    for cidx in range(NCORES):
        b, j = divmod(cidx, 4)
        in_maps.append(dict(aT=aT[b], wsel=mixa0_wsel(ab_w_in, j), gqk=gqk, cosT=C["cosT"], sinT=C["sinT"],
                            cmat=C["cmat"], dftL=C["dftL"], dftC=C["dftC"]))
    res = run_bass_kernel_spmd(nc, in_maps, core_ids=list(range(NCORES)), trace=trace)
    cat_lat = np.empty((2, NLAT, 2048), a_lat.dtype)
    cat_ctx = np.empty((2, NCTX, 2048), a_lat.dtype)
    for cidx in range(NCORES):
        b, j = divmod(cidx, 4)
        c = res.results[cidx]["cat"].T
        for dst, rows in ((cat_lat, slice(0, NLAT)), (cat_ctx, slice(NLAT, NTOK))):
            dst[b][:, j * 128:(j + 1) * 128] = c[rows, 0:128]
            dst[b][:, 512 + 3 * j * 128: 512 + (3 * j + 3) * 128] = c[rows, 128:512]
    return cat_lat, cat_ctx, res

CH = 64
NHL = 4


def build_mixa1_launch(NLAT=8192):
    NTOK = NLAT + NCTX
    nc = bass.Bass("TRN2", target_bir_lowering=False)
    mk = MK(nc)
    aT_d = nc.dram_tensor("aT", [D, NTOK], BF16, kind="ExternalInput").ap()
    w_d = nc.dram_tensor("w5", [5, 128, KC, 512], F32, kind="ExternalInput").ap()
    lbl_d = nc.dram_tensor("lbl", [128, 2, NHL], F32, kind="ExternalInput").ap()
    gon_d = nc.dram_tensor("gon", [128, 1], F32, kind="ExternalInput").ap()
    cm_d = nc.dram_tensor("cm", [128, 4, 128], BF16, kind="ExternalInput").ap()
    sm_d = nc.dram_tensor("sm", [128, 512], F32, kind="ExternalInput").ap()
    out_d = nc.dram_tensor("og", [NHL * 128, NLAT], BF16, kind="ExternalOutput").ap()
    ofw_d = nc.dram_tensor("ofw", [NHL, 128, NLAT], BF16, kind="Internal").ap()
    qsc_d = nc.dram_tensor("qsc", [NHL, 128, NTOK], BF16, kind="Internal").ap()
    vsc_d = nc.dram_tensor("vsc", [NTOK // 128, 128, 512], BF16, kind="Internal").ap()
    io = Buf(None, "io"); outb = Buf(None, "out"); ofwb = Buf(None, "ofw"); qscb = Buf(None, "qsc"); vscb = Buf(None, "vsc")
    W = mk.sb("W", [128, 4, KC, 512], BF16)
    aslot = [mk.sb(f"aslot{i}", [128, KC, 512], BF16) for i in range(2)]
    cm = mk.sb("cm", [128, 4, 128], BF16)
    sm = mk.sb("sm", [128, 512], F32)
    lbl = mk.sb("lbl", [128, 2, NHL], F32)
    lb = mk.sb("lb", [128, NHL], F32)
    oml = mk.sb("oml", [128, NHL], F32)
    gon = mk.sb("gon", [128, 1], F32)
    ones = mk.sb("ones", [128, 128], BF16)
    Vt = [mk.sb(f"Vt{i}", [128, 512], BF16) for i in range(8)]
    S32 = [mk.sb(f"S32_{h}", [128, 128], F32) for h in range(NHL)]
    Sbf = [[mk.sb(f"Sbf_{h}_{i}", [128, 128], BF16) for i in range(6)] for h in range(NHL)]
    F = {n: [mk.sb(f"{n}{i}", [128, 512], F32) for i in range(2)] for n in ("sg", "fg", "lf", "b", "eb", "enb", "kk")}
    QdT = [mk.sb(f"QdT{i}", [128, 512], BF16) for i in range(2)]
    QSB = [mk.sb(f"QSB{i}", [128, 512], BF16) for i in range(2)]
    KdT = [mk.sb(f"KdT{i}", [128, 512], BF16) for i in range(2)]
    Kdtok = [mk.sb(f"Kdtok{i}", [128, 128], BF16) for i in range(2)]
    ATm = [mk.sb(f"ATm{i}", [128, 128], BF16) for i in range(4)]
    tmpS = [mk.sb(f"tmpS{i}", [128, 128], F32) for i in range(2)]
    ofs = [mk.sb(f"ofs{i}", [128, 512], BF16) for i in range(2)]
    O32 = [mk.sb(f"o32_{i}", [128, 512], F32) for i in range(2)]
    SQ = [mk.sb(f"sq_{i}", [128, 512], BF16) for i in range(2)]
    R32 = [mk.sb(f"r32_{i}", [128, 512], F32) for i in range(2)]
    SGG = [mk.sb(f"sgg_{i}", [128, 512], F32) for i in range(2)]
    epsb = mk.sb("epsb", [128, 1], F32)
    otile = [mk.sb(f"otile{i}", [128, 512], BF16) for i in range(2)]
    cnt = {}

    def rot(lst, key):
        i = cnt.get(key, 0); cnt[key] = i + 1
        return lst[i % len(lst)]
    bank = [mk.ps(f"bank{i}", [128, 512], F32) for i in range(7)]
    bankT = mk.ps("bankT", [128, 1024], BF16)
    pO2 = [bank[0], bank[3]]; pP = [bank[1], bank[2]]; pV = bank[1]; pSSQ = bank[1]
    pP4 = [bank[1], bank[2], bank[4], bank[5]]
    rA = [(bank[4], bank[4].t[:, 0:128]), (bank[5], bank[5].t[:, 0:128])]
    rS = [(bank[6], bank[6].t[:, 0:128]), (bank[6], bank[6].t[:, 128:256])]
    rT = [(bankT, bankT.t[:, 0:128])]
    mk.op("dve", lambda: nc.vector.memset(ones.t[:, :], 1.0), writes=[ones])
    mk.op("dve", lambda: nc.vector.memset(epsb.t[:, :], EPS), writes=[epsb])
    mk.dma("sp", cm.t[:, :, :], cm_d, reads=[io], writes=[cm])
    mk.dma("sp", sm.t[:, :], sm_d, reads=[io], writes=[sm])
    mk.dma("sp", lbl.t[:, :, :], lbl_d, reads=[io], writes=[lbl])
    mk.dma("sp", gon.t[:, :], gon_d, reads=[io], writes=[gon])
    mk.op("dve", lambda: nc.vector.tensor_tensor(out=lb.t[:, :], in0=lbl.t[:, 1, :], in1=lbl.t[:, 0, :], op=ALU.subtract),
          reads=[lbl], writes=[lb])
    mk.op("act", lambda: nc.scalar.activation(out=lb.t[:, :], in_=lb.t[:, :], func=AF.Sigmoid), reads=[lb], writes=[lb])
    mk.op("dve", lambda: nc.vector.tensor_scalar(out=oml.t[:, :], in0=lb.t[:, :], scalar1=-1.0, scalar2=1.0, op0=ALU.mult,
                                                 op1=ALU.add), reads=[lb], writes=[oml])
    IDENT = cm.t[:, 2, :]

    for direction in (0, 1):
        fw = direction == 0
        secs = [0, 1, 3] if fw else [0, 2, 3, 4]
        for si, s in enumerate(secs):
            mk.dma("pool", W.t[:, si, :, :], w_d[s], reads=[io], writes=[W])
        for h in range(NHL):
            mk.op("dve", lambda: nc.vector.memset(S32[h].t[:, :], 0.0), writes=[S32[h]])
            mk.op("pool", lambda: nc.gpsimd.memset(Sbf[h][0].t[:, :], 0.0), writes=[Sbf[h][0]])
        sidx = [0] * NHL
        ctx_tiles = [(NLAT, NCTX, False)]
        lat_tiles = [(t * 512, 512, True) for t in range(NLAT // 512)]
        tiles = (ctx_tiles + lat_tiles) if fw else (ctx_tiles + lat_tiles[::-1])
        MASK = cm.t[:, 0, :] if fw else cm.t[:, 1, :]
        for (c0, Wd, lat) in tiles:
            nblk = Wd // 128
            a = rot(aslot, "a")
            mk.dma("sp", a.t[:, :, 0:Wd], aT_d[:, c0:c0 + Wd].rearrange("(kc p) t -> p kc t", p=128), reads=[io], writes=[a])
            vts = []
            for blk in range(nblk):
                vt = rot(Vt, "vt")
                gblk = c0 // 128 + blk
                if fw:
                    mms = [(pV.t[:, :], a.t[:, kc, blk * 128:(blk + 1) * 128], W.t[:, 2, kc, :], kc == 0, kc == KC - 1) for kc in range(KC)]
                    mk.mmf([pV], mms, reads=[a, W])
                    mk.op("act", lambda: nc.scalar.copy(out=vt.t[:, :], in_=pV.t[:, :]), reads=[pV], writes=[vt])
                    mk.dma("sp", vsc_d[gblk], vt.t[:, :], reads=[vt], writes=[vscb])
                else:
                    mk.dma("sp", vt.t[:, :], vsc_d[gblk], reads=[vscb], writes=[vt])
                vts.append(vt)
            sl = slice(0, Wd)
            nch = Wd // CH
            blks = list(range(nblk)) if fw else list(range(nblk))[::-1]

            def stageA_pair(hs):
                R = {}
                for k_, h in enumerate(hs):
                    pq, pf = pP4[2 * k_], pP4[2 * k_ + 1]
                    for (pb, si) in (((pq, 0), (pf, 1)) if fw else ((pf, 1),)):
                        mms = [(pb.t[:, 0:Wd], W.t[:, si, kc, h * 128:(h + 1) * 128], a.t[:, kc, 0:Wd], kc == 0, kc == KC - 1) for kc in range(KC)]
                        mk.mmf([pb], mms, reads=[a, W])
                    R[h] = dict(t={n: rot(F[n], n) for n in F}, qd=rot(QdT, "qd"), kd=rot(KdT, "kd"), pq=pq, pf=pf, qsb=rot(QSB, "qsb"))
                ACT = lambda o, i, f, **kw: nc.scalar.activation(out=o, in_=i, func=f, **kw)
                for h in hs:
                    t, pf = R[h]["t"], R[h]["pf"]
                    mk.op("act", lambda: ACT(t["sg"].t[:, sl], pf.t[:, sl], AF.Sigmoid), reads=[pf], writes=[t["sg"]])
                for h in hs:
                    t = R[h]["t"]
                    mk.op("dve", lambda: nc.vector.tensor_scalar(out=t["fg"].t[:, sl], in0=t["sg"].t[:, sl], scalar1=oml.t[:, h:h + 1],
                                                                 scalar2=lb.t[:, h:h + 1], op0=ALU.mult, op1=ALU.add),
                          reads=[t["sg"], oml, lb], writes=[t["fg"]])
                for h in hs:
                    pq, qsb = R[h]["pq"], R[h]["qsb"]
                    if fw:
                        mk.op("act", lambda: ACT(qsb.t[:, sl], pq.t[:, sl], AF.Silu), reads=[pq], writes=[qsb])
                        mk.dma("sp", qsc_d[h][:, c0:c0 + Wd], qsb.t[:, sl], reads=[qsb], writes=[qscb])
                    else:
                        mk.dma("sp", qsb.t[:, sl], qsc_d[h][:, c0:c0 + Wd], reads=[qscb], writes=[qsb])
                for h in hs:
                    t = R[h]["t"]
                    mk.op("act", lambda: ACT(t["lf"].t[:, sl], t["fg"].t[:, sl], AF.Ln), reads=[t["fg"]], writes=[t["lf"]])
                    mk.op("pool", lambda: nc.gpsimd.tensor_scalar(out=t["kk"].t[:, sl], in0=t["fg"].t[:, sl], scalar1=-1.0, scalar2=1.0,
                                                                  op0=ALU.mult, op1=ALU.add), reads=[t["fg"]], writes=[t["kk"]])
                for h in hs:
                    t = R[h]["t"]
                    mk.op("dve", lambda: nc.vector.tensor_tensor_scan(out=t["b"].t[:, sl], data0=sm.t[:, sl], data1=t["lf"].t[:, sl],
                                                                      initial=0.0, op0=ALU.mult, op1=ALU.add),
                          reads=[sm, t["lf"]], writes=[t["b"]])
                    if not fw:
                        b3 = t["b"].t[:, sl].rearrange("p (c t) -> p c t", t=CH)
                        mk.op("dve", lambda: nc.vector.tensor_tensor(out=t["lf"].t[:, sl], in0=t["lf"].t[:, sl], in1=t["b"].t[:, sl], op=ALU.subtract),
                              reads=[t["lf"], t["b"]], writes=[t["lf"]])
                        mk.op("dve", lambda: nc.vector.tensor_tensor(out=t["b"].t[:, sl].rearrange("p (c t) -> p c t", t=CH),
                                                                     in0=t["lf"].t[:, sl].rearrange("p (c t) -> p c t", t=CH),
                                                                     in1=b3[:, :, CH - 1:CH].to_broadcast([128, nch, CH]), op=ALU.add),
                              reads=[t["lf"], t["b"]], writes=[t["b"]])
                for h in hs:
                    t = R[h]["t"]
                    mk.op("act", lambda: ACT(t["eb"].t[:, sl], t["b"].t[:, sl], AF.Exp), reads=[t["b"]], writes=[t["eb"]])
                    mk.op("act", lambda: ACT(t["enb"].t[:, sl], t["b"].t[:, sl], AF.Exp, scale=-1.0), reads=[t["b"]], writes=[t["enb"]])
                for h in hs:
                    t, qd, kd = R[h]["t"], R[h]["qd"], R[h]["kd"]
                    qsb = R[h]["qsb"]
                    mk.op("dve", lambda: nc.vector.tensor_tensor(out=qd.t[:, sl], in0=qsb.t[:, sl], in1=t["eb"].t[:, sl], op=ALU.mult),
                          reads=[qsb, t["eb"]], writes=[qd])
                    mk.op("dve", lambda: nc.vector.tensor_tensor(out=kd.t[:, sl], in0=t["kk"].t[:, sl], in1=t["enb"].t[:, sl], op=ALU.mult),
                          reads=[t["kk"], t["enb"]], writes=[kd])
                return {h: dict(qd=R[h]["qd"], kd=R[h]["kd"], eb=R[h]["t"]["eb"]) for h in hs}

            def block_ops(h, blk, A_, pend, pOh):
                qd, kd, eb = A_["qd"], A_["kd"], A_["eb"]
                bs = slice(blk * 128, (blk + 1) * 128)
                vt = vts[blk]
                rtb, rtap = rot(rT, "rt"); kt_ = rot(Kdtok, "kt")
                mk.op("pe", lambda: nc.tensor.transpose(out=rtap, in_=kd.t[:, bs], identity=IDENT), reads=[kd, cm], writes=[rtb])
                mk.op("act", lambda: nc.scalar.copy(out=kt_.t[:, :], in_=rtap), reads=[rtb], writes=[kt_])
                rab, raap = rot(rA, "ra"); atm = rot(ATm, "atm")
                mk.mmf([rab], [(raap, kd.t[:, bs], qd.t[:, bs], True, True)], reads=[kd, qd])
                mk.op("dve", lambda: nc.vector.tensor_tensor(out=atm.t[:, :], in0=raap, in1=MASK, op=ALU.mult), reads=[rab, cm], writes=[atm])
                chunks = [0, 1] if fw else [1, 0]
                s_before = []
                for ci in chunks:
                    s_before.append(Sbf[h][sidx[h]])
                    rsb, rsap = rot(rS, "rs"); tS = rot(tmpS, "ts")
                    rows = slice(ci * CH, (ci + 1) * CH)
                    mk.mmf([rsb], [(rsap, kt_.t[rows, :], vt.t[rows, h * 128:(h + 1) * 128], True, True)], reads=[kt_, vt])
                    cglob = blk * 2 + ci
                    col = cglob * CH + (CH - 1 if fw else 0)
                    et = eb.t[:, col:col + 1]
                    mk.op("act", lambda: nc.scalar.activation(out=tS.t[:, :], in_=rsap, func=AF.Copy, scale=et), reads=[rsb, eb], writes=[tS])
                    sidx[h] = (sidx[h] + 1) % 6
                    nxt = Sbf[h][sidx[h]]
                    mk.op("dve", lambda: nc.vector.scalar_tensor_tensor(out=nxt.t[:, :], in0=S32[h].t[:, :], scalar=et, in1=tS.t[:, :],
                                                                        op0=ALU.mult, op1=ALU.add), reads=[S32[h], tS, eb], writes=[nxt])
                    mk.op("dve", lambda: nc.vector.scalar_tensor_tensor(out=S32[h].t[:, :], in0=S32[h].t[:, :], scalar=et, in1=tS.t[:, :],
                                                                        op0=ALU.mult, op1=ALU.add), reads=[S32[h], tS, eb], writes=[S32[h]])
                if lat:
                    def omm():
                        mms = [(pOh.t[:, bs], vt.t[:, h * 128:(h + 1) * 128], atm.t[:, :], True, False)]
                        for n_, ci in enumerate(chunks):
                            cs_ = slice(blk * 128 + ci * CH, blk * 128 + (ci + 1) * CH)
                            mms.append((pOh.t[:, cs_], s_before[n_].t[:, :], qd.t[:, cs_], False, n_ == 1))
                        mk.mmf([pOh], mms, reads=[vt, atm, qd] + s_before)
                    pend.append(omm)
                    if len(pend) > 1:
                        pend.pop(0)()

            def evac_pair(hs):
                if fw:
                    for k_, h in enumerate(hs):
                        of = rot(ofs, "of")
                        mk.op("act", lambda: nc.scalar.copy(out=of.t[:, :], in_=pO2[k_].t[:, :]), reads=[pO2[k_]], writes=[of])
                        mk.dma("sp", ofw_d[h][:, c0:c0 + 512], of.t[:, :], reads=[of], writes=[ofwb])
                    return
                E = {}
                for k_, h in enumerate(hs):
                    of = rot(ofs, "of")
                    mk.dma("sp", of.t[:, :], ofw_d[h][:, c0:c0 + 512], reads=[ofwb], writes=[of])
                    o32 = O32[k_]
                    mk.op("dve", lambda: nc.vector.tensor_tensor(out=o32.t[:, :], in0=pO2[k_].t[:, :], in1=of.t[:, :], op=ALU.add),
                          reads=[pO2[k_], of], writes=[o32])
                    pg = pP4[2 * k_ + 1]
                    mms = [(pg.t[:, :], W.t[:, 3, kc, h * 128:(h + 1) * 128], a.t[:, kc, :], kc == 0, kc == KC - 1) for kc in range(KC)]
                    mk.mmf([pg], mms, reads=[a, W])
                    E[h] = dict(o32=o32, pg=pg, sgg=SGG[k_], sq=SQ[k_], r32=R32[k_], ps=pP4[2 * k_])
                for h in hs:
                    e = E[h]
                    mk.op("act", lambda: nc.scalar.activation(out=e["sgg"].t[:, :], in_=e["pg"].t[:, :], func=AF.Exp, scale=-1.0),
                          reads=[e["pg"]], writes=[e["sgg"]])
                    mk.op("pool", lambda: nc.gpsimd.tensor_tensor(out=e["sq"].t[:, :], in0=e["o32"].t[:, :], in1=e["o32"].t[:, :], op=ALU.mult),
                          reads=[e["o32"]], writes=[e["sq"]])
                for h in hs:
                    e = E[h]
                    mk.mmf([e["ps"]], [(e["ps"].t[:, :], ones.t[:, :], e["sq"].t[:, :], True, True)], reads=[ones, e["sq"]])
                    mk.op("pool", lambda: nc.gpsimd.tensor_scalar(out=e["sgg"].t[:, :], in0=e["sgg"].t[:, :], scalar1=1.0, scalar2=None, op0=ALU.add),
                          reads=[e["sgg"]], writes=[e["sgg"]])
                for h in hs:
                    e = E[h]
                    mk.op("act", lambda: nc.scalar.activation(out=e["r32"].t[:, :], in_=e["ps"].t[:, :], func=AF.Ln, bias=epsb.t[:, 0:1], scale=1.0 / 128),
                          reads=[e["ps"], epsb], writes=[e["r32"]])
                    mk.op("dve", lambda: nc.vector.reciprocal(out=e["sgg"].t[:, :], in_=e["sgg"].t[:, :]), reads=[e["sgg"]], writes=[e["sgg"]])
                for h in hs:
                    e = E[h]
                    mk.op("act", lambda: nc.scalar.activation(out=e["r32"].t[:, :], in_=e["r32"].t[:, :], func=AF.Exp, scale=-0.5),
                          reads=[e["r32"]], writes=[e["r32"]])
                for h in hs:
                    e = E[h]
                    mk.op("dve", lambda: nc.vector.scalar_tensor_tensor(out=e["o32"].t[:, :], in0=e["o32"].t[:, :], scalar=gon.t[:, 0:1], in1=e["r32"].t[:, :],
                                                                        op0=ALU.mult, op1=ALU.mult), reads=[e["o32"], gon, e["r32"]], writes=[e["o32"]])
                    ot = rot(otile, "ot")
                    mk.op("dve", lambda: nc.vector.tensor_tensor(out=ot.t[:, :], in0=e["o32"].t[:, :], in1=e["sgg"].t[:, :], op=ALU.mult),
                          reads=[e["o32"], e["sgg"]], writes=[ot])
                    mk.dma("sp", out_d[h * 128:(h + 1) * 128, c0:c0 + 512], ot.t[:, :], reads=[ot], writes=[outb])

            for hp in range(0, NHL, 2):
                hs = [hp, hp + 1]
                A = stageA_pair(hs)
                pend = {h: [] for h in hs}
                for blk in blks:
                    for k_, h in enumerate(hs):
                        block_ops(h, blk, A[h], pend[h], pO2[k_])
                if lat:
                    for h in hs:
                        for f_ in pend[h]:
                            f_()
                    evac_pair(hs)
    mk.finish([outb])
    return nc


def mixa1_consts():
    if "m1" in _CONST:
        return _CONST["m1"]
    bf = ml_dtypes.bfloat16
    s = np.arange(128)[:, None]; t = np.arange(128)[None, :]
    same = (s // CH) == (t // CH)
    mfw = (same & (s <= t)).astype(np.float32)
    mbw = (same & (s >= t)).astype(np.float32)
    cm = np.stack([mfw, mbw, np.eye(128, dtype=np.float32), np.zeros((128, 128), np.float32)], 1).astype(bf)
    smr = np.ones((128, 512), np.float32); smr[:, ::CH] = 0.0
    _CONST["m1"] = dict(cm=np.ascontiguousarray(cm), sm=smr)
    return _CONST["m1"]


def run_mixa1(a_lat, a_ctx, hgrn_w_in, lb_logits, o_norm, trace=False):
    C = mixa1_consts()
    nc = build_mixa1_launch()
    aT = [np.ascontiguousarray(np.concatenate([a_lat[b], a_ctx[b]], 0).T) for b in range(2)]
    gon = np.ascontiguousarray(o_norm.reshape(128, 1).astype(np.float32))
    in_maps = []
    for cidx in range(NCORES):
        b, j = divmod(cidx, 4)
        cols = np.arange(j * 512, (j + 1) * 512)
        w5 = np.stack([hgrn_w_in[:, s * 2048 + cols].reshape(KC, 128, 512).transpose(1, 0, 2) for s in range(5)], 0)
        lbl = lb_logits[:, cols].reshape(2, NHL, 128).transpose(2, 0, 1)
        in_maps.append(dict(aT=aT[b], w5=np.ascontiguousarray(w5), lbl=np.ascontiguousarray(lbl), gon=gon, cm=C["cm"], sm=C["sm"]))
    res = run_bass_kernel_spmd(nc, in_maps, core_ids=list(range(NCORES)), trace=trace)
    og = np.empty((2, NLAT, 2048), a_lat.dtype)
    for cidx in range(NCORES):
        b, j = divmod(cidx, 4)
        og[b][:, j * 512:(j + 1) * 512] = res.results[cidx]["og"].T
    return og, res


def _run(nc, in_maps):
    return run_bass_kernel_spmd(nc, in_maps, core_ids=list(range(NCORES))).results


def pack_lat(lat):
    return [np.ascontiguousarray(lat[c // 4, (c % 4) * 2048:(c % 4 + 1) * 2048].T) for c in range(NCORES)]


def kernel(x, c, ctx, c_ctx, w_mod, b_mod, norm_gains, ffn_w_in, ffn_w_out, ab_w_in, qk_norm, ab_w_out,
           hgrn_w_in, hgrn_lb_logits, hgrn_o_norm, hgrn_w_out, final_norm):
    f32 = np.float32
    x, c, ctx, c_ctx = (np.asarray(a, f32) for a in (x, c, ctx, c_ctx))
    mod = run_mod(c, c_ctx, np.asarray(w_mod, f32), np.asarray(b_mod, f32))
    norm_gains = np.asarray(norm_gains, f32)
    nc = build_tok_launch(PWL, SUBT_L, 1, [("ffn", 0, 0), ("normout", 0)])
    hT = pack_tokens(x, ctx)
    w_in = lay_w_in(np.asarray(ffn_w_in[0, 0], f32)); w_out = lay_w_out(np.asarray(ffn_w_out[0, 0], f32), NH_FFN)
    g0 = gains_for(norm_gains, [0])
    res = _run(nc, [{"hT": hT[k], "modT": modT_for_core(mod, [0], k // 4), "gains": g0, "w0_in": w_in, "w0_out": w_out}
                    for k in range(NCORES)])
    hT = [r["hT_out"] for r in res]
    a_lat, a_ctx = unpack_tokens([r["aT_out"] for r in res], D)
    cat_lat, cat_ctx, _ = run_mixa0(a_lat, a_ctx, np.asarray(ab_w_in[0], f32), np.asarray(qk_norm[0], f32))
    nc = build_tok_launch(PWL, SUBT_L, 2, [("proj", 0), ("ffn", 0, 1), ("ffn", 1, 0), ("normout", 1)])
    cat = pack_tokens(cat_lat, cat_ctx)
    wp = lay_w_out(np.asarray(ab_w_out[0], f32), 1)[0]
    w1_in = lay_w_in(np.asarray(ffn_w_in[0, 1], f32)); w1_out = lay_w_out(np.asarray(ffn_w_out[0, 1], f32), NH_FFN)
    w2_in = lay_w_in(np.asarray(ffn_w_in[1, 0], f32)); w2_out = lay_w_out(np.asarray(ffn_w_out[1, 0], f32), NH_FFN)
    g01 = gains_for(norm_gains, [0, 1])
    res = _run(nc, [{"hT": hT[k], "modT": modT_for_core(mod, [0, 1], k // 4), "gains": g01, "w0_proj": wp, "cat0": cat[k],
                     "w1_in": w1_in, "w1_out": w1_out, "w2_in": w2_in, "w2_out": w2_out} for k in range(NCORES)])
    h_lat, _ = unpack_tokens([r["hT_out"] for r in res], D)
    a_lat, a_ctx = unpack_tokens([r["aT_out"] for r in res], D)
    og, _ = run_mixa1(a_lat, a_ctx, np.asarray(hgrn_w_in[0], f32), np.asarray(hgrn_lb_logits, f32), np.asarray(hgrn_o_norm[0], f32))
    nc = build_tok_launch(1024, subtiles_of(1024, 0), 1, [("proj", 0), ("ffn", 0, 1), ("final",)])
    hT = pack_lat(h_lat)
    ogT = pack_lat(og)
    wp = lay_w_out(np.asarray(hgrn_w_out[0], f32), 1)[0]
    w1_in = lay_w_in(np.asarray(ffn_w_in[1, 1], f32)); w1_out = lay_w_out(np.asarray(ffn_w_out[1, 1], f32), NH_FFN)
    g1 = gains_for(norm_gains, [1])
    fng = lay_vec(np.asarray(final_norm, f32))
    res = _run(nc, [{"hT": hT[k], "modT": modT_for_core(mod, [1], k // 4), "gains": g1, "w0_proj": wp, "cat0": ogT[k],
                     "w1_in": w1_in, "w1_out": w1_out, "fng": fng} for k in range(NCORES)])
    out = np.empty((2, 8192, D), f32)
    for k in range(NCORES):
        out[k // 4, (k % 4) * 2048:(k % 4 + 1) * 2048] = res[k]["yT"].T
    return out
```

```python
import numpy as np
import concourse.bass as bass
import concourse.mybir as mybir

F32 = mybir.dt.float32
BF16 = mybir.dt.bfloat16
I32 = mybir.dt.int32
AF = mybir.ActivationFunctionType
ALU = mybir.AluOpType

EPOCH = 24000
DMA_RING = 8


class Buf:
    __slots__ = ("t", "name", "lw", "rd")

    def __init__(self, t, name=""):
        self.t = t
        self.name = name
        self.lw = None
        self.rd = {}

    def __getitem__(self, idx):
        return self.t[idx]


class MK:
    def __init__(self, nc, same_engine_sync=True):
        self.nc = nc
        self.E = {"pe": nc.tensor, "act": nc.scalar, "dve": nc.vector, "pool": nc.gpsimd, "sp": nc.sync}
        self.cur_sem = {}
        self.cur_cnt = {}
        self.seen = {e: {} for e in self.E}
        self.same_engine_sync = same_engine_sync
        self.ring = {}
        self.ring_i = {}
        self.nsem = 0
        self._uid = 0
        self.all_dma_events = []

    def new_sem(self, name):
        self.nsem += 1
        cm = self.nc.semaphore(f"{name}_{self.nsem}")
        return cm.__enter__()

    def sb(self, name, shape, dtype):
        self._uid += 1
        return Buf(self.nc.alloc_sbuf_tensor(f"{name}_{self._uid}", list(shape), dtype), name)

    def ps(self, name, shape, dtype=F32):
        self._uid += 1
        return Buf(self.nc.alloc_psum_tensor(f"{name}_{self._uid}", list(shape), dtype), name)

    def dram(self, name, shape, dtype, kind="Internal"):
        return Buf(self.nc.dram_tensor(name, list(shape), dtype, kind=kind), name)

    def _need(self, eng, ev):
        if ev is None:
            return
        sem, val = ev
        seen = self.seen[eng]
        k = id(sem)
        if seen.get(k, (None, 0))[1] >= val:
            return
        self.E[eng].wait_ge(sem, val)
        seen[k] = (sem, val)

    def _deps(self, eng, reads, writes, skip_self=False):
        for b in reads:
            if b.lw is not None:
                self._need(eng, b.lw)
        for b in writes:
            if b.lw is not None:
                self._need(eng, b.lw)
            for ev in b.rd.values():
                self._need(eng, ev)

    def _tick(self, eng):
        if eng not in self.cur_sem or self.cur_cnt[eng] >= EPOCH:
            self.cur_sem[eng] = self.new_sem(f"s_{eng}")
            self.cur_cnt[eng] = 0
        self.cur_cnt[eng] += 1
        return (self.cur_sem[eng], self.cur_cnt[eng])

    def _commit(self, ev, reads, writes):
        for b in writes:
            b.lw = ev
            b.rd = {}
        for b in reads:
            if b not in writes:
                b.rd[id(ev[0])] = ev

    def op(self, eng, fn, reads=(), writes=(), track=True):
        self._deps(eng, reads, writes)
        ins = fn()
        if track:
            ev = self._tick(eng)
            ins.then_inc(ev[0], 1)
            if not self.same_engine_sync:
                self.seen[eng][id(ev[0])] = ev
            self._commit(ev, reads, writes)
        return ins

    def mm_group(self, out_buf, mms, reads):
        self._deps("pe", reads, [out_buf])
        n = len(mms)
        ins = None
        for i, (o, l, r) in enumerate(mms):
            ins = self.nc.tensor.matmul(o, l, r, start=(i == 0), stop=(i == n - 1))
        ev = self._tick("pe")
        ins.then_inc(ev[0], 1)
        self.seen["pe"][id(ev[0])] = ev
        self._commit(ev, reads, [out_buf])

    def dma(self, q, out_ap, in_ap, reads=(), writes=(), **kw):
        self._deps(q, reads, writes)
        ring = self.ring.setdefault(q, [[None, 0] for _ in range(DMA_RING)])
        i = self.ring_i.get(q, 0)
        self.ring_i[q] = (i + 1) % DMA_RING
        slot = ring[i]
        if slot[0] is None or slot[1] >= EPOCH:
            if slot[0] is not None:
                self._need(q, (slot[0], slot[1]))
            slot[0] = self.new_sem(f"d_{q}{i}")
            slot[1] = 0
        elif slot[1] > 0:
            self._need(q, (slot[0], slot[1]))
        ins = self.E[q].dma_start(out=out_ap, in_=in_ap, **kw)
        slot[1] += 16
        ev = (slot[0], slot[1])
        ins.then_inc(ev[0], 16)
        self._commit(ev, reads, writes)
        self.all_dma_events.append((q, ev))
        return ev

    def finish(self, out_bufs):
        for b in out_bufs:
            if b.lw is not None:
                self._need("sp", b.lw)
        for q, ring in self.ring.items():
            for sem, cnt in ring:
                if sem is not None and cnt > 0:
                    self._need("sp", (sem, cnt))


def _mm(self, out_bufs, mms, reads, start, stop):
    self._deps("pe", reads, out_bufs)
    ins = None
    for (o, l, r) in mms:
        ins = self.nc.tensor.matmul(o, l, r, start=start, stop=stop)
    ev = self._tick("pe")
    ins.then_inc(ev[0], 1)
    self.seen["pe"][id(ev[0])] = ev
    self._commit(ev, reads, out_bufs)


def _mm_multi(self, out_bufs, mms, reads, nk):
    self._deps("pe", reads, out_bufs)
    ins = None
    for (_, ki, o, l, r) in mms:
        ins = self.nc.tensor.matmul(o, l, r, start=(ki == 0), stop=(ki == nk - 1))
    ev = self._tick("pe")
    ins.then_inc(ev[0], 1)
    self.seen["pe"][id(ev[0])] = ev
    self._commit(ev, reads, out_bufs)


MK.mm = _mm
MK.mm_multi = _mm_multi


def _mmf(self, out_bufs, mms, reads):
    self._deps("pe", reads, out_bufs)
    ins = None
    for (o, l, r, st, sp) in mms:
        ins = self.nc.tensor.matmul(o, l, r, start=st, stop=sp)
    ev = self._tick("pe")
    ins.then_inc(ev[0], 1)
    self.seen["pe"][id(ev[0])] = ev
    self._commit(ev, reads, out_bufs)


def _barrier(self):
    evs = [(self.cur_sem[e], self.cur_cnt[e]) for e in self.cur_sem]
    for q, ring in self.ring.items():
        for sem, cnt in ring:
            if sem is not None and cnt > 0:
                evs.append((sem, cnt))
    for e in self.E:
        for ev in evs:
            self._need(e, ev)


MK.mmf = _mmf
MK.barrier = _barrier

D = 2048
KC = 16
FFN = 5632
EPS = 1e-6


def subtiles_of(ncols_lat, ncols_ctx):
    st = []
    off = 0
    while off < ncols_lat:
        sz = min(512, ncols_lat - off)
        st.append((off, sz, 0))
        off += sz
    if ncols_ctx:
        st.append((off, ncols_ctx, 1))
    return st


class Pools:
    def __init__(self, mk, PW, JH):
        self.PW = PW
        self.JH = JH
        self.aTs = [mk.sb(f"aT{i}", [128, KC, PW], BF16) for i in range(2)]
        self.aT = self.aTs[0]
        self.acc = mk.sb("ssacc", [128, PW], F32)
        self.accb = mk.sb("ssaccb", [128, PW], BF16)
        self.gT = [mk.sb(f"gT{j}", [128, PW], BF16) for j in range(JH)]
        self.hst = [mk.sb(f"hst{i}", [128, PW], F32) for i in range(4)]
        self.sq = [mk.sb(f"sq{i}", [128, PW], BF16) for i in range(1)]
        self.R = mk.sb("R", [128, PW], F32)
        self.tmp = [mk.sb(f"tmp{i}", [128, PW], F32) for i in range(3)]
        self.win = [mk.sb(f"win{i}", [128, KC, 256], BF16) for i in range(3)]
        self.wout = [mk.sb(f"wout{i}", [128, JH, 128], BF16) for i in range(2)]
        self.ones = mk.sb("ones", [128, 128], BF16)
        self.bank = [mk.ps(f"bank{i}", [128, 512], F32) for i in range(8)]
        self.cnt = {"hst": 0, "sq": 0, "tmp": 0, "win": 0, "wout": 0, "sqf": 0}
        self.sqf = [mk.sb(f"sqf{i}", [128, PW], F32) for i in range(2)]
        mk.op("dve", lambda: mk.nc.vector.memset(self.ones.t[:, :], 1.0), writes=[self.ones])

    def nxt(self, kind):
        lst = getattr(self, kind)
        i = self.cnt[kind]
        self.cnt[kind] = i + 1
        return lst[i % len(lst)]


def emit_norm(mk, P, src_ap_fn, src_bufs, col0, subt, S_ap_fn, T_ap_fn, out_fn, post_fn=None):
    nc = mk.nc
    PW = sum(s[1] for s in subt)
    nst = len(subt)
    banks = P.bank[0:nst]
    for i in range(KC):
        hs = P.nxt("hst")
        mk.dma("sp", hs.t[:, 0:PW], src_ap_fn(i), reads=[src_bufs[i]], writes=[hs])
        sq = P.nxt("sq")
        mk.op("act", lambda: nc.scalar.activation(out=sq.t[:, 0:PW], in_=hs.t[:, 0:PW], func=AF.Square),
              reads=[hs], writes=[sq])
        mms = [(banks[k].t[:, 0:sz], P.ones.t[:, :], sq.t[:, off:off + sz]) for k, (off, sz, _) in enumerate(subt)]
        mk.mm(banks, mms, reads=[sq, P.ones], start=(i == 0), stop=(i == KC - 1))
    for k, (off, sz, _) in enumerate(subt):
        mk.op("dve", lambda: nc.vector.tensor_scalar(out=P.R.t[:, off:off + sz], in0=banks[k].t[:, 0:sz],
                                                     scalar1=1.0 / D, scalar2=EPS, op0=ALU.mult, op1=ALU.add),
              reads=[banks[k]], writes=[P.R])
    mk.op("act", lambda: nc.scalar.activation(out=P.R.t[:, 0:PW], in_=P.R.t[:, 0:PW], func=AF.Sqrt),
          reads=[P.R], writes=[P.R])
    mk.op("dve", lambda: nc.vector.reciprocal(out=P.R.t[:, 0:PW], in_=P.R.t[:, 0:PW]), reads=[P.R], writes=[P.R])
    for i in range(KC):
        hs = P.nxt("hst")
        mk.dma("sp", hs.t[:, 0:PW], src_ap_fn(i), reads=[src_bufs[i]], writes=[hs])
        tm = P.nxt("tmp")
        mk.op("dve", lambda: nc.vector.tensor_tensor(out=tm.t[:, 0:PW], in0=hs.t[:, 0:PW], in1=P.R.t[:, 0:PW],
                                                     op=ALU.mult), reads=[hs, P.R], writes=[tm])
        groups = {}
        for (off, sz, mc) in subt:
            if mc in groups and groups[mc][0] + groups[mc][1] == off:
                groups[mc][1] += sz
            else:
                groups[mc] = [off, sz]
        for mc, (off, sz) in groups.items():
            o_ap, o_buf = out_fn(i, off, sz)
            Tb = T_ap_fn(i, mc) if T_ap_fn is not None else 0.0
            mk.op("act", lambda: nc.scalar.activation(out=o_ap, in_=tm.t[:, off:off + sz], func=AF.Identity,
                                                      bias=Tb, scale=S_ap_fn(i, mc)),
                  reads=[tm], writes=[o_buf])
        if post_fn is not None:
            post_fn(i)


def emit_ffn(mk, P, src_ap_fn, dst_ap_fn, hbufs_src, hbufs_dst, subt, S_fn, T_fn, G_fn,
             w_in_d, w_out_d, J, NH, wbuf, aT=None, inter=None):
    nc = mk.nc
    PW = sum(s[1] for s in subt)
    nst = len(subt)
    JH = J // NH
    assert nst <= 3
    if aT is None:
        aT = P.aT
        emit_norm(mk, P, src_ap_fn, hbufs_src, 0, subt, S_fn, T_fn,
                  lambda i, off, sz: (P.aT.t[:, i, off:off + sz], P.aT))
    inter = inter if inter is not None else []
    nslots = NH * JH + NH * KC
    state = {"slot": 0}

    def tick():
        state["slot"] += 1
        want = (len0 * state["slot"] + nslots - 1) // nslots
        while inter and (len0 - len(inter)) < want:
            inter.pop(0)()
    len0 = len(inter)
    setA = P.bank[0:nst]
    setB = P.bank[3:3 + nst]
    for hf in range(NH):
        for jj in range(JH):
            j = hf * JH + jj
            tick()
            w = P.nxt("win")
            mk.dma("pool", w.t[:, :, :], w_in_d[j], reads=[wbuf], writes=[w])
            for part, pset in ((0, setA), (1, setB)):
                mms = []
                for kc in range(KC):
                    for k, (off, sz, _) in enumerate(subt):
                        mms.append((k, kc, pset[k].t[:, 0:sz], w.t[:, kc, part * 128:(part + 1) * 128],
                                    aT.t[:, kc, off:off + sz]))
                mk.mm_multi(pset, mms, reads=[w, aT], nk=KC)
            tm = P.nxt("tmp")
            for k, (off, sz, _) in enumerate(subt):
                mk.op("act", lambda: nc.scalar.activation(out=tm.t[:, off:off + sz], in_=setA[k].t[:, 0:sz],
                                                          func=AF.Silu), reads=[setA[k]], writes=[tm])
            for k, (off, sz, _) in enumerate(subt):
                mk.op("dve", lambda: nc.vector.tensor_tensor(out=P.gT[jj].t[:, off:off + sz], in0=tm.t[:, off:off + sz],
                                                             in1=setB[k].t[:, 0:sz], op=ALU.mult),
                      reads=[tm, setB[k]], writes=[P.gT[jj]])
        emit_outproj(mk, P, lambda jj, off, sz: P.gT[jj].t[:, off:off + sz], P.gT[0:JH], JH,
                     lambda i: w_out_d[hf, i], wbuf,
                     (src_ap_fn if hf == 0 else dst_ap_fn), (hbufs_src if hf == 0 else hbufs_dst),
                     dst_ap_fn, hbufs_dst, subt, G_fn, tick=tick)
    while inter:
        inter.pop(0)()


def emit_outproj(mk, P, rhs_fn, rhs_bufs, nk, w_d_fn, wbuf, src_ap_fn, hbufs_src, dst_ap_fn, hbufs_dst, subt, G_fn, tick=None):
    nc = mk.nc
    PW = sum(s[1] for s in subt)
    nst = len(subt)
    setA = P.bank[0:nst]
    setB = P.bank[3:3 + nst]
    for i in range(KC):
        if tick is not None:
            tick()
        wo = P.nxt("wout")
        mk.dma("pool", wo.t[:, 0:nk, :], w_d_fn(i), reads=[wbuf], writes=[wo])
        pset = setA if (i % 2 == 0) else setB
        mms = []
        for jj in range(nk):
            for k, (off, sz, _) in enumerate(subt):
                mms.append((k, jj, pset[k].t[:, 0:sz], wo.t[:, jj, :], rhs_fn(jj, off, sz)))
        mk.mm_multi(pset, mms, reads=[wo] + list(rhs_bufs), nk=nk)
        hs = P.nxt("hst")
        mk.dma("sp", hs.t[:, 0:PW], src_ap_fn(i), reads=[hbufs_src[i]], writes=[hs])
        for k, (off, sz, mc) in enumerate(subt):
            mk.op("dve", lambda: nc.vector.scalar_tensor_tensor(out=hs.t[:, off:off + sz], in0=pset[k].t[:, 0:sz],
                                                                scalar=G_fn(i, mc), in1=hs.t[:, off:off + sz],
                                                                op0=ALU.mult, op1=ALU.add),
                  reads=[pset[k], hs], writes=[hs])
        mk.dma("sp", dst_ap_fn(i), hs.t[:, 0:PW], reads=[hs], writes=[hbufs_dst[i]])


def emit_prep_mod(mk, mod_sb, gain_sb, S_sb, G_sb):
    nc = mk.nc
    for k in range(3):
        for mc in range(2):
            mk.op("dve", lambda: nc.vector.scalar_tensor_tensor(
                out=S_sb.t[:, k, :, mc], in0=mod_sb.t[:, (3 * k + 1) * KC:(3 * k + 2) * KC, mc], scalar=1.0,
                in1=gain_sb.t[:, k, :], op0=ALU.add, op1=ALU.mult), reads=[mod_sb, gain_sb], writes=[S_sb])
            mk.op("dve", lambda: nc.vector.tensor_scalar(
                out=G_sb.t[:, k, :, mc], in0=mod_sb.t[:, (3 * k + 2) * KC:(3 * k + 3) * KC, mc],
                scalar1=(1.0 if k == 1 else 0.5), scalar2=None, op0=ALU.mult), reads=[mod_sb], writes=[G_sb])


def emit_mod(mk, P, cT_d, w_d, b_d, out_d, nch, wbuf):
    nc = mk.nc
    c_sb = mk.sb("mod_c", [128, KC, 3], F32)
    sc_sb = mk.sb("mod_sc", [128, KC, 3], F32)
    wsl = [mk.sb(f"mod_w{i}", [128, KC, 128], F32) for i in range(3)]
    b_sb = mk.sb("mod_b", [128, nch], F32)
    o_sb = mk.sb("mod_o", [128, nch, 3], F32)
    mk.dma("sp", c_sb.t[:, :, :], cT_d, reads=[wbuf], writes=[c_sb])
    mk.dma("sp", b_sb.t[:, :], b_d, reads=[wbuf], writes=[b_sb])
    mk.op("act", lambda: nc.scalar.activation(out=sc_sb.t[:, :, :], in_=c_sb.t[:, :, :], func=AF.Silu),
          reads=[c_sb], writes=[sc_sb])
    for ch in range(nch):
        wo = wsl[ch % 3]
        mk.dma("sp", wo.t[:, 0:KC, :], w_d[ch], reads=[wbuf], writes=[wo])
        bk = P.bank[ch % 8]
        mms = [(0, kc, bk.t[:, 0:3], wo.t[:, kc, :], sc_sb.t[:, kc, :]) for kc in range(KC)]
        mk.mm_multi([bk], mms, reads=[wo, sc_sb], nk=KC)
        mk.op("act", lambda: nc.scalar.activation(out=o_sb.t[:, ch, :], in_=bk.t[:, 0:3], func=AF.Identity,
                                                  bias=b_sb.t[:, ch:ch + 1], scale=1.0),
              reads=[bk, b_sb], writes=[o_sb])
    mk.dma("sp", out_d, o_sb.t[:, :, :], reads=[o_sb], writes=[wbuf])


def norm_ops(mk, P, src_ap_fn, src_bufs, subt, S_ap_fn, T_ap_fn, aT_dst):
    nc = mk.nc
    PW = sum(s_[1] for s_ in subt)
    ops = []
    for i in range(KC):
        def f(i=i):
            hs = P.nxt("hst")
            mk.dma("sp", hs.t[:, 0:PW], src_ap_fn(i), reads=[src_bufs[i]], writes=[hs])
            if i == 0:
                mk.op("act", lambda: nc.scalar.activation(out=P.acc.t[:, 0:PW], in_=hs.t[:, 0:PW], func=AF.Square), reads=[hs], writes=[P.acc])
            else:
                sq = P.nxt("sqf")
                mk.op("act", lambda: nc.scalar.activation(out=sq.t[:, 0:PW], in_=hs.t[:, 0:PW], func=AF.Square), reads=[hs], writes=[sq])
                mk.op("dve", lambda: nc.vector.tensor_tensor(out=P.acc.t[:, 0:PW], in0=P.acc.t[:, 0:PW], in1=sq.t[:, 0:PW], op=ALU.add),
                      reads=[P.acc, sq], writes=[P.acc])
        ops.append(f)

    def g():
        banks = P.bank[0:len(subt)]
        mk.op("act", lambda: nc.scalar.copy(out=P.accb.t[:, 0:PW], in_=P.acc.t[:, 0:PW]), reads=[P.acc], writes=[P.accb])
        for k, (off, sz, _) in enumerate(subt):
            mk.mmf([banks[k]], [(banks[k].t[:, 0:sz], P.ones.t[:, :], P.accb.t[:, off:off + sz], True, True)], reads=[P.accb, P.ones])
            mk.op("dve", lambda: nc.vector.tensor_scalar(out=P.R.t[:, off:off + sz], in0=banks[k].t[:, 0:sz], scalar1=1.0 / D, scalar2=EPS,
                                                         op0=ALU.mult, op1=ALU.add), reads=[banks[k]], writes=[P.R])
        mk.op("act", lambda: nc.scalar.activation(out=P.R.t[:, 0:PW], in_=P.R.t[:, 0:PW], func=AF.Sqrt), reads=[P.R], writes=[P.R])
        mk.op("dve", lambda: nc.vector.reciprocal(out=P.R.t[:, 0:PW], in_=P.R.t[:, 0:PW]), reads=[P.R], writes=[P.R])
    ops.append(g)
    groups = {}
    for (off, sz, mc) in subt:
        if mc in groups and groups[mc][0] + groups[mc][1] == off:
            groups[mc][1] += sz
        else:
            groups[mc] = [off, sz]
    for i in range(KC):
        def f2(i=i):
            hs = P.nxt("hst")
            mk.dma("sp", hs.t[:, 0:PW], src_ap_fn(i), reads=[src_bufs[i]], writes=[hs])
            tm = P.nxt("tmp")
            mk.op("dve", lambda: nc.vector.tensor_tensor(out=tm.t[:, 0:PW], in0=hs.t[:, 0:PW], in1=P.R.t[:, 0:PW], op=ALU.mult),
                  reads=[hs, P.R], writes=[tm])
            for mc, (off, sz) in groups.items():
                mk.op("act", lambda: nc.scalar.activation(out=aT_dst.t[:, i, off:off + sz], in_=tm.t[:, off:off + sz], func=AF.Identity,
                                                          bias=T_ap_fn(i, mc), scale=S_ap_fn(i, mc)), reads=[tm], writes=[aT_dst])
        ops.append(f2)
    return ops

import ml_dtypes
from concourse.bass_utils import run_bass_kernel_spmd

NCORES = 8
NL = 2
J_FFN = FFN // 128
NH_FFN = 2
NPASS = 2


def build_tok_launch(PW, subt, n_layers, steps, out_h=True):
    nc = bass.Bass("TRN2", target_bir_lowering=False)
    mk = MK(nc)
    JH = J_FFN // NH_FFN
    P = Pools(mk, PW, JH)
    NT = NPASS * PW
    hin = nc.dram_tensor("hT", [D, NT], F32, kind="ExternalInput").ap()
    hout = nc.dram_tensor("hT_out", [D, NT], F32, kind=("ExternalOutput" if out_h else "Internal")).ap()
    modT = nc.dram_tensor("modT", [n_layers, 128, 144, 2], F32, kind="ExternalInput").ap()
    gains = nc.dram_tensor("gains", [n_layers, 128, 3, KC], F32, kind="ExternalInput").ap()
    wbuf = Buf(None, "weights")
    mod_sb, S_sb, G_sb = [], [], []
    for li in range(n_layers):
        m = mk.sb(f"mod{li}", [128, 144, 2], F32)
        g = mk.sb(f"gain{li}", [128, 3, KC], F32)
        S = mk.sb(f"S{li}", [128, 3, KC, 2], F32)
        G = mk.sb(f"G{li}", [128, 3, KC, 2], F32)
        mk.dma("sp", m.t[:, :, :], modT[li], reads=[wbuf], writes=[m])
        mk.dma("sp", g.t[:, :, :], gains[li], reads=[wbuf], writes=[g])
        emit_prep_mod(mk, m, g, S, G)
        mod_sb.append(m); S_sb.append(S); G_sb.append(G)
    hb_in = [[Buf(None, f"hi{p}_{i}") for i in range(KC)] for p in range(NPASS)]
    hb_out = [[Buf(None, f"ho{p}_{i}") for i in range(KC)] for p in range(NPASS)]
    outs = [b for p in range(NPASS) for b in hb_out[p]]
    ctxs = []
    written = False
    for si, st in enumerate(steps):
        kind = st[0]
        c = dict(kind=kind, reads_out=written)
        if kind == "ffn":
            _, li, f = st
            c.update(li=li, k=(0 if f == 0 else 2),
                     w_in=nc.dram_tensor(f"w{si}_in", [J_FFN, 128, KC, 256], F32, kind="ExternalInput").ap(),
                     w_out=nc.dram_tensor(f"w{si}_out", [NH_FFN, KC, 128, JH, 128], F32, kind="ExternalInput").ap())
            written = True
        elif kind == "proj":
            _, li = st
            c.update(li=li, w_p=nc.dram_tensor(f"w{si}_proj", [KC, 128, KC, 128], F32, kind="ExternalInput").ap(),
                     cat=nc.dram_tensor(f"cat{si}", [D, NT], BF16, kind="ExternalInput").ap(), catb=Buf(None, "cat"))
            written = True
        elif kind == "normout":
            _, li = st
            c.update(li=li, aT_d=nc.dram_tensor("aT_out", [D, NT], BF16, kind="ExternalOutput").ap(), aTb=Buf(None, "aT_out"))
            outs.append(c["aTb"])
        elif kind == "final":
            fng_d = nc.dram_tensor("fng", [128, KC], F32, kind="ExternalInput").ap()
            fng = mk.sb("fng", [128, KC], F32)
            mk.dma("sp", fng.t[:, :], fng_d, reads=[wbuf], writes=[fng])
            c.update(y_d=nc.dram_tensor("yT", [D, NT], F32, kind="ExternalOutput").ap(), yb=Buf(None, "yT"), fng=fng)
            outs.append(c["yb"])
        ctxs.append(c)
    items = [(c, p) for c in ctxs for p in range(NPASS)]

    def io_fns(c, p):
        c0 = p * PW
        src = hout if c["reads_out"] else hin
        sb_ = (hb_out if c["reads_out"] else hb_in)[p]
        src_fn = (lambda i, src=src, c0=c0: src[i * 128:(i + 1) * 128, c0:c0 + PW])
        dst_fn = (lambda i, c0=c0: hout[i * 128:(i + 1) * 128, c0:c0 + PW])
        return c0, src_fn, sb_, dst_fn

    def ffn_mod_fns(c):
        li, k = c["li"], c["k"]
        return (lambda i, mc: S_sb[li].t[:, k, i, mc:mc + 1],
                lambda i, mc: mod_sb[li].t[:, 3 * k * KC + i, mc:mc + 1],
                lambda i, mc: G_sb[li].t[:, k, i, mc:mc + 1])

    cur_aT = 0
    prenormed = False
    for idx, (c, p) in enumerate(items):
        kind = c["kind"]
        c0, src_fn, sb_, dst_fn = io_fns(c, p)
        inter = []
        nxt_is_ffn = idx + 1 < len(items) and items[idx + 1][0]["kind"] == "ffn" and kind in ("ffn", "proj")
        if nxt_is_ffn:
            c2, p2 = items[idx + 1]
            _, src2, sb2, _ = io_fns(c2, p2)
            S2, T2, _ = ffn_mod_fns(c2)
            inter = norm_ops(mk, P, src2, sb2, subt, S2, T2, P.aTs[1 - cur_aT])
        if kind == "ffn":
            S_fn, T_fn, G_fn = ffn_mod_fns(c)
            aT = P.aTs[cur_aT]
            if not prenormed:
                for f_ in norm_ops(mk, P, src_fn, sb_, subt, S_fn, T_fn, aT):
                    f_()
            emit_ffn(mk, P, src_fn, dst_fn, sb_, hb_out[p], subt, S_fn, T_fn, G_fn,
                     c["w_in"], c["w_out"], J_FFN, NH_FFN, wbuf, aT=aT, inter=inter)
        elif kind == "proj":
            li = c["li"]
            aT = P.aTs[cur_aT]
            mk.dma("sp", aT.t[:, :, 0:PW], c["cat"][:, c0:c0 + PW].rearrange("(kc p) t -> p kc t", p=128),
                   reads=[c["catb"]], writes=[aT])
            len0 = len(inter)
            st_ = {"n": 0}

            def tick(inter=inter, len0=len0, st_=st_):
                st_["n"] += 1
                want = (len0 * st_["n"] + KC - 1) // KC
                while inter and (len0 - len(inter)) < want:
                    inter.pop(0)()
            emit_outproj(mk, P, lambda jj, off, sz, aT=aT: aT.t[:, jj, off:off + sz], [aT], KC,
                         lambda i: c["w_p"][i], wbuf, src_fn, sb_, dst_fn, hb_out[p], subt,
                         lambda i, mc: G_sb[li].t[:, 1, i, mc:mc + 1], tick=tick)
            while inter:
                inter.pop(0)()
        elif kind == "normout":
            li = c["li"]
            aT = P.aTs[cur_aT]
            for f_ in norm_ops(mk, P, src_fn, sb_, subt, lambda i, mc: S_sb[li].t[:, 1, i, mc:mc + 1],
                               lambda i, mc: mod_sb[li].t[:, 3 * KC + i, mc:mc + 1], aT):
                f_()
            mk.dma("sp", c["aT_d"][:, c0:c0 + PW].rearrange("(kc p) t -> p kc t", p=128), aT.t[:, :, 0:PW],
                   reads=[aT], writes=[c["aTb"]])
        elif kind == "final":
            cur = {}
            fng, y_d, yb = c["fng"], c["y_d"], c["yb"]

            def out_fn(i, off, sz):
                if "t" not in cur or cur["i"] != i:
                    cur["t"] = P.nxt("hst"); cur["i"] = i
                return cur["t"].t[:, off:off + sz], cur["t"]

            def post_fn(i, c0=c0):
                mk.dma("sp", y_d[i * 128:(i + 1) * 128, c0:c0 + PW], cur["t"].t[:, 0:PW], reads=[cur["t"]], writes=[yb])
            emit_norm(mk, P, src_fn, sb_, 0, subt, lambda i, mc: fng.t[:, i:i + 1], None, out_fn, post_fn)
        prenormed = nxt_is_ffn
        cur_aT = 1 - cur_aT
    mk.finish(outs)
    return nc


def build_mod_launch(nch):
    nc = bass.Bass("TRN2", target_bir_lowering=False)
    mk = MK(nc)
    P = Pools(mk, 64, KC)
    cT = nc.dram_tensor("cT", [128, KC, 3], F32, kind="ExternalInput").ap()
    w = nc.dram_tensor("w", [nch, 128, KC, 128], F32, kind="ExternalInput").ap()
    b = nc.dram_tensor("b", [128, nch], F32, kind="ExternalInput").ap()
    o = nc.dram_tensor("o", [128, nch, 3], F32, kind="ExternalOutput").ap()
    wbuf = Buf(None, "io")
    emit_mod(mk, P, cT, w, b, o, nch, wbuf)
    mk.finish([wbuf])
    return nc


def lay_w_in(w):
    F_ = w.shape[1] // 2
    Jn = F_ // 128
    return np.ascontiguousarray(w.reshape(KC, 128, 2, Jn, 128).transpose(3, 1, 0, 2, 4).reshape(Jn, 128, KC, 256))


def lay_w_out(w, NH):
    F_ = w.shape[0]
    JH = F_ // 128 // NH
    return np.ascontiguousarray(w.reshape(NH, JH, 128, KC, 128).transpose(0, 3, 2, 1, 4))


def lay_vec(v):
    n = v.shape[-1] // 128
    r = v.reshape(v.shape[:-1] + (n, 128))
    return np.ascontiguousarray(np.moveaxis(r, -1, 0))


def run_mod(c, c_ctx, w_mod, b_mod):
    ncols = 9 * D
    nch_total = NL * ncols // 128
    nch = nch_total // NCORES
    cT = lay_vec(np.stack([c[0], c[1], c_ctx], 0))
    cT = np.ascontiguousarray(cT.transpose(0, 2, 1))
    wl = w_mod.reshape(NL, KC, 128, ncols // 128, 128).transpose(0, 3, 2, 1, 4).reshape(nch_total, 128, KC, 128)
    bl = b_mod.reshape(nch_total, 128).T
    nc = build_mod_launch(nch)
    in_maps = []
    for cidx in range(NCORES):
        sl = slice(cidx * nch, (cidx + 1) * nch)
        in_maps.append({"cT": cT, "w": np.ascontiguousarray(wl[sl]), "b": np.ascontiguousarray(bl[:, sl])})
    res = run_bass_kernel_spmd(nc, in_maps, core_ids=list(range(NCORES)))
    o = np.concatenate([r["o"] for r in res.results], axis=1)
    mod = o.transpose(2, 1, 0).reshape(3, NL, ncols).transpose(1, 0, 2)
    return np.ascontiguousarray(mod)


def modT_for_core(mod, layers, b):
    out = []
    for l in layers:
        m = mod[l][[b, 2]]
        out.append(m.reshape(2, 144, 128).transpose(2, 1, 0))
    return np.ascontiguousarray(np.stack(out, 0))


def gains_for(norm_gains, layers):
    return np.ascontiguousarray(np.stack([lay_vec(norm_gains[l]) for l in layers], 0))


PWL = 1056
SUBT_L = subtiles_of(1024, 32)


def pack_tokens(lat, ctxa, dtype=None):
    outs = []
    for cidx in range(NCORES):
        b, j = divmod(cidx, 4)
        cols = []
        for p in range(NPASS):
            cols.append(lat[b, j * 2048 + p * 1024: j * 2048 + (p + 1) * 1024])
            cols.append(ctxa[b, j * 64 + p * 32: j * 64 + (p + 1) * 32])
        outs.append(np.ascontiguousarray(np.concatenate(cols, 0).T))
    return outs


def unpack_tokens(per_core, C):
    dt = per_core[0].dtype
    lat = np.empty((2, 8192, C), dt)
    ctxa = np.empty((2, 256, C), dt)
    for cidx in range(NCORES):
        b, j = divmod(cidx, 4)
        a = per_core[cidx].T
        for p in range(NPASS):
            base = p * PWL
            lat[b, j * 2048 + p * 1024: j * 2048 + (p + 1) * 1024] = a[base:base + 1024]
            ctxa[b, j * 64 + p * 32: j * 64 + (p + 1) * 32] = a[base + 1024:base + 1056]
    return lat, ctxa

NLAT = 8192
NCTX = 256
NTOK = NLAT + NCTX
HD = 128
ATTN_SCALE = HD ** -0.5


def emit_qknorm_rope(mk, W, ps, gain_ap, dst_ap, dst_buf, T, cs_aps=None):
    nc = mk.nc
    mk.op("act", lambda: nc.scalar.activation(out=T["sq"].t[:, 0:W], in_=ps.t[:, 0:W], func=AF.Square),
          reads=[ps], writes=[T["sq"]])
    mk.mm([T["ssq"]], [(T["ssq"].t[:, 0:W], T["ones"].t[:, :], T["sq"].t[:, 0:W])], reads=[T["sq"], T["ones"]],
          start=True, stop=True)
    mk.op("dve", lambda: nc.vector.tensor_scalar(out=T["r"].t[:, 0:W], in0=T["ssq"].t[:, 0:W], scalar1=1.0 / HD,
                                                 scalar2=EPS, op0=ALU.mult, op1=ALU.add), reads=[T["ssq"]], writes=[T["r"]])
    mk.op("act", lambda: nc.scalar.activation(out=T["r"].t[:, 0:W], in_=T["r"].t[:, 0:W], func=AF.Sqrt),
          reads=[T["r"]], writes=[T["r"]])
    mk.op("dve", lambda: nc.vector.reciprocal(out=T["r"].t[:, 0:W], in_=T["r"].t[:, 0:W]), reads=[T["r"]], writes=[T["r"]])
    if cs_aps is None:
        mk.op("dve", lambda: nc.vector.scalar_tensor_tensor(out=dst_ap, in0=ps.t[:, 0:W], scalar=gain_ap, in1=T["r"].t[:, 0:W],
                                                            op0=ALU.mult, op1=ALU.mult), reads=[ps, T["r"]], writes=[dst_buf])
        return
    C_ap, S_ap, csbuf = cs_aps
    mk.op("dve", lambda: nc.vector.scalar_tensor_tensor(out=T["qn"].t[:, 0:W], in0=ps.t[:, 0:W], scalar=gain_ap,
                                                        in1=T["r"].t[:, 0:W], op0=ALU.mult, op1=ALU.mult),
          reads=[ps, T["r"]], writes=[T["qn"]])
    mk.op("act", lambda: nc.scalar.copy(out=T["qnb"].t[:, 0:W], in_=T["qn"].t[:, 0:W]), reads=[T["qn"]], writes=[T["qnb"]])
    mk.mm([T["rot"]], [(T["rot"].t[:, 0:W], T["rmat"].t[:, 0, :], T["qnb"].t[:, 0:W])], reads=[T["qnb"], T["rmat"]],
          start=True, stop=True)
    mk.op("dve", lambda: nc.vector.tensor_tensor(out=T["t1"].t[:, 0:W], in0=T["qn"].t[:, 0:W], in1=C_ap, op=ALU.mult),
          reads=[T["qn"], csbuf], writes=[T["t1"]])
    mk.op("dve", lambda: nc.vector.tensor_tensor(out=T["t2"].t[:, 0:W], in0=T["rot"].t[:, 0:W], in1=S_ap, op=ALU.mult),
          reads=[T["rot"], csbuf], writes=[T["t2"]])
    mk.op("pool", lambda: nc.gpsimd.tensor_tensor(out=dst_ap, in0=T["t1"].t[:, 0:W], in1=T["t2"].t[:, 0:W], op=ALU.add),
          reads=[T["t1"], T["t2"]], writes=[dst_buf])


def build_mixa0_launch():
    nc = bass.Bass("TRN2", target_bir_lowering=False)
    mk = MK(nc)
    aT_d = nc.dram_tensor("aT", [D, NTOK], BF16, kind="ExternalInput").ap()
    wsel_d = nc.dram_tensor("wsel", [128, KC, 768], F32, kind="ExternalInput").ap()
    gqk_d = nc.dram_tensor("gqk", [128, 2], F32, kind="ExternalInput").ap()
    cos_d = nc.dram_tensor("cosT", [128, NLAT], F32, kind="ExternalInput").ap()
    sin_d = nc.dram_tensor("sinT", [128, NLAT], F32, kind="ExternalInput").ap()
    cmat_d = nc.dram_tensor("cmat", [128, 4, 128], BF16, kind="ExternalInput").ap()
    dftL_d = nc.dram_tensor("dftL", [16, 8, 2, 128, 8, 512], BF16, kind="ExternalInput").ap()
    dftC_d = nc.dram_tensor("dftC", [2, 128, 2, 256], BF16, kind="ExternalInput").ap()
    cat_d = nc.dram_tensor("cat", [512, NTOK], BF16, kind="ExternalOutput").ap()
    io = Buf(None, "io")
    catb = Buf(None, "cat")
    KT = mk.sb("KT", [128, NTOK], BF16)
    V = mk.sb("V", [128, 66, 128], BF16)
    UW = mk.sb("UW", [128, 66, 256], BF16)
    wsel = mk.sb("wsel", [128, KC, 768], BF16)
    cmat = mk.sb("cmat", [128, 4, 128], BF16)
    gqk = mk.sb("gqk", [128, 2], F32)
    gq = mk.sb("gq", [128, 1], F32)
    QT = mk.sb("QT", [128, NLAT], BF16)
    QC = mk.sb("QC", [128, NCTX], BF16)
    aslot = [mk.sb(f"aslot{i}", [128, KC, 512], BF16) for i in range(2)]
    cslot = [mk.sb(f"cslot{i}", [128, 2, 512], F32) for i in range(2)]
    T = {n: mk.sb(n, [128, 512], F32) for n in ("r", "qn", "t1", "t2")}
    T["sq"] = mk.sb("sq", [128, 512], BF16)
    T["qnb"] = mk.sb("qnb", [128, 512], BF16)
    T["ones"] = mk.sb("ones", [128, 128], BF16)
    fT = mk.sb("fT", [128, 512], BF16)
    PT = [mk.sb(f"PT{i}", [128, 512], BF16) for i in range(4)]
    rden = mk.sb("rden", [128, 512], F32)
    otile = [mk.sb(f"otile{i}", [128, 512], BF16) for i in range(2)]
    dslot = [mk.sb(f"dslot{i}", [128, 16, 512], BF16) for i in range(2)]
    dtile = mk.sb("dtile", [128, 512], BF16)
    bank = [mk.ps(f"bank{i}", [128, 512], F32) for i in range(8)]
    T["ssq"] = bank[6]
    T["rot"] = bank[7]
    mk.op("dve", lambda: nc.vector.memset(T["ones"].t[:, :], 1.0), writes=[T["ones"]])
    mk.dma("pool", wsel.t[:, :, :], wsel_d, reads=[io], writes=[wsel])
    mk.dma("sp", cmat.t[:, :, :], cmat_d, reads=[io], writes=[cmat])
    mk.dma("sp", gqk.t[:, :], gqk_d, reads=[io], writes=[gqk])
    mk.op("dve", lambda: nc.vector.tensor_scalar(out=gq.t[:, :], in0=gqk.t[:, 0:1], scalar1=ATTN_SCALE, scalar2=None,
                                                 op0=ALU.mult), reads=[gqk], writes=[gq])
    T["rmat"] = cmat
    tiles = [(t * 512, 512, True) for t in range(16)] + [(NLAT, NCTX, False)]
    cnt = {"a": 0, "c": 0}

    def load_tile(c0, W, lat):
        a = aslot[cnt["a"] % 2]; cnt["a"] += 1
        mk.dma("sp", a.t[:, :, 0:W], aT_d[:, c0:c0 + W].rearrange("(kc p) t -> p kc t", p=128), reads=[io], writes=[a])
        cs = None
        if lat:
            c = cslot[cnt["c"] % 2]; cnt["c"] += 1
            mk.dma("sp", c.t[:, 0, 0:W], cos_d[:, c0:c0 + W], reads=[io], writes=[c])
            mk.dma("sp", c.t[:, 1, 0:W], sin_d[:, c0:c0 + W], reads=[io], writes=[c])
            cs = (c.t[:, 0, 0:W], c.t[:, 1, 0:W], c)
        return a, cs

    def proj(a, W, col0, pb):
        mms = [(0, kc, pb.t[:, 0:W], wsel.t[:, kc, col0:col0 + 128], a.t[:, kc, 0:W]) for kc in range(KC)]
        mk.mm_multi([pb], mms, reads=[wsel, a], nk=KC)

    for ti, (c0, W, lat) in enumerate(tiles):
        a, cs = load_tile(c0, W, lat)
        nsb = W // 128
        ch0 = c0 // 128
        pk = bank[ti % 2]
        proj(a, W, 512, pk)
        emit_qknorm_rope(mk, W, pk, gqk.t[:, 1:2], KT.t[:, c0:c0 + W], KT, T, cs)
        pv = bank[2]
        mms = []
        for sb_ in range(nsb):
            for kc in range(KC):
                mms.append((0, kc, pv.t[:, sb_ * 128:(sb_ + 1) * 128], a.t[:, kc, sb_ * 128:(sb_ + 1) * 128], wsel.t[:, kc, 640:768]))
        mk.mm_multi([pv], mms, reads=[wsel, a], nk=KC)
        mk.op("act", lambda: nc.scalar.copy(out=V.t[:, ch0:ch0 + nsb, :], in_=pv.t[:, 0:W].rearrange("p (s c) -> p s c", c=128)),
              reads=[pv], writes=[V])
        pf = bank[3]
        proj(a, W, 0, pf)
        mk.op("act", lambda: nc.scalar.copy(out=fT.t[:, 0:W], in_=pf.t[:, 0:W]), reads=[pf], writes=[fT])
        for half in range((nsb + 1) // 2):
            pu = bank[4 + half]
            nn = min(2, nsb - half * 2)
            for s2 in range(nn):
                sb_ = half * 2 + s2
                mk.mm([pu], [(pu.t[:, s2 * 256:(s2 + 1) * 256], fT.t[:, sb_ * 128:(sb_ + 1) * 128],
                              cmat.t[:, 1:3, :].rearrange("p a c -> p (a c)"))], reads=[fT, cmat], start=True, stop=True)
            mk.op("dve", lambda: nc.vector.tensor_copy(out=UW.t[:, ch0 + half * 2:ch0 + half * 2 + nn, :],
                                                       in_=pu.t[:, 0:nn * 256].rearrange("p (s c) -> p s c", c=256)),
                  reads=[pu], writes=[UW])
    T["ssq"] = bank[2]
    T["rot"] = bank[3]
    py = bank[7]

    def dft_units():
        units = [(kt, ng) for kt in range(16) for ng in range(8)]

        def load(u):
            kt, ng = units[u]
            ds_ = dslot[u % 2]
            for cs_ in range(2):
                mk.dma("sp", ds_.t[:, cs_ * 8:(cs_ + 1) * 8, :], dftL_d[kt, ng, cs_], reads=[io], writes=[ds_])
        load(0)
        load(1)
        yield
        for u, (kt, ng) in enumerate(units):
            ds_ = dslot[u % 2]
            mms = []
            for n8 in range(8):
                nch = ng * 8 + n8
                mms.append((py.t[:, :], UW.t[:, nch, 0:128], ds_.t[:, n8, :], (nch == 0), False))
                mms.append((py.t[:, :], UW.t[:, nch, 128:256], ds_.t[:, 8 + n8, :], False, (nch == 63)))
            mk.mmf([py], mms, reads=[UW, ds_])
            if u + 2 < len(units):
                load(u + 2)
            if ng == 7:
                mk.op("act", lambda: nc.scalar.copy(out=dtile.t[:, :], in_=py.t[:, :]), reads=[py], writes=[dtile])
                mk.dma("sp", cat_d[0:128, kt * 512:(kt + 1) * 512], dtile.t[:, :], reads=[dtile], writes=[catb])
            yield
    dgen = dft_units()
    next(dgen)

    def dft_step(n):
        for _ in range(n):
            try:
                next(dgen)
            except StopIteration:
                return
    for h in range(3):
        for ti, (c0, W, lat) in enumerate(tiles):
            a, cs = load_tile(c0, W, lat)
            pq = bank[ti % 2]
            proj(a, W, 128 + h * 128, pq)
            if lat:
                emit_qknorm_rope(mk, W, pq, gq.t[:, 0:1], QT.t[:, c0:c0 + W], QT, T, cs)
            else:
                emit_qknorm_rope(mk, W, pq, gq.t[:, 0:1], QC.t[:, 0:W], QC, T, None)
        qtiles = [(QT, t * 512, 512, t * 512, list(range(66))) for t in range(16)] + [(QC, 0, NCTX, NLAT, [64, 65])]
        for qi, (qbuf, q0, W, o0, kcs) in enumerate(qtiles):
            O = bank[3 + (qi % 2)]
            Dn = bank[5 + (qi % 2)]
            nk = len(kcs)

            def s_mm(idx):
                kc = kcs[idx]
                sb_ = bank[idx % 3]
                mk.mm([sb_], [(sb_.t[:, 0:W], KT.t[:, kc * 128:(kc + 1) * 128], qbuf.t[:, q0:q0 + W])], reads=[KT, qbuf],
                      start=True, stop=True)

            def e_act(idx):
                sb_ = bank[idx % 3]
                pt = PT[idx % 4]
                mk.op("act", lambda: nc.scalar.activation(out=pt.t[:, 0:W], in_=sb_.t[:, 0:W], func=AF.Exp),
                      reads=[sb_], writes=[pt])

            s_mm(0)
            if nk > 1:
                s_mm(1)
            e_act(0)
            for idx in range(nk):
                if nk > 2 and idx in (nk // 4, nk // 2, (3 * nk) // 4):
                    dft_step(1)
                if idx + 2 < nk:
                    s_mm(idx + 2)
                if idx + 1 < nk:
                    e_act(idx + 1)
                pt = PT[idx % 4]
                kc = kcs[idx]
                mk.mm([O, Dn], [(O.t[:, 0:W], V.t[:, kc, :], pt.t[:, 0:W]), (Dn.t[:, 0:W], T["ones"].t[:, :], pt.t[:, 0:W])],
                      reads=[pt, V, T["ones"]], start=(idx == 0), stop=(idx == nk - 1))
            mk.op("dve", lambda: nc.vector.reciprocal(out=rden.t[:, 0:W], in_=Dn.t[:, 0:W]), reads=[Dn], writes=[rden])
            ot = otile[qi % 2]
            mk.op("dve", lambda: nc.vector.tensor_tensor(out=ot.t[:, 0:W], in0=O.t[:, 0:W], in1=rden.t[:, 0:W], op=ALU.mult),
                  reads=[O, rden], writes=[ot])
            mk.dma("sp", cat_d[128 + h * 128:256 + h * 128, o0:o0 + W], ot.t[:, 0:W], reads=[ot], writes=[catb])
    dft_step(1000)
    dcnt = 0
    ds_ = dslot[0]
    for cs_ in range(2):
        mk.dma("sp", ds_.t[:, cs_ * 8:cs_ * 8 + 2, 0:256], dftC_d[cs_], reads=[io], writes=[ds_])
    py = bank[2]
    mms = []
    for n2 in range(2):
        mms.append((0, 2 * n2, py.t[:, 0:256], UW.t[:, 64 + n2, 0:128], ds_.t[:, n2, 0:256]))
        mms.append((0, 2 * n2 + 1, py.t[:, 0:256], UW.t[:, 64 + n2, 128:256], ds_.t[:, 8 + n2, 0:256]))
    mk.mm_multi([py], mms, reads=[UW, ds_], nk=4)
    ot = otile[0]
    mk.op("act", lambda: nc.scalar.copy(out=ot.t[:, 0:256], in_=py.t[:, 0:256]), reads=[py], writes=[ot])
    mk.dma("sp", cat_d[0:128, NLAT:NTOK], ot.t[:, 0:256], reads=[ot], writes=[catb])
    mk.finish([catb])
    return nc


_CONST = {}


def mixa0_consts():
    if "m0" in _CONST:
        return _CONST["m0"]
    bf = ml_dtypes.bfloat16
    rows = NLAT // 64
    row_id = np.repeat(np.arange(rows, dtype=np.float32), 64)
    col_id = np.tile(np.arange(64, dtype=np.float32), rows)
    inv_freq = (np.float32(10000.0) ** (-np.arange(0, 64, 2, dtype=np.float32) / np.float32(64))).astype(np.float32)
    ang = np.concatenate([row_id[:, None] * inv_freq, col_id[:, None] * inv_freq], axis=-1).astype(np.float32)
    cosL, sinL = np.cos(ang).astype(np.float32), np.sin(ang).astype(np.float32)
    pidx = np.arange(128)
    fidx = (pidx // 64) * 32 + (pidx % 32)
    cosT = np.ascontiguousarray(cosL[:, fidx].T)
    sinT = np.ascontiguousarray(sinL[:, fidx].T)
    rm = np.zeros((128, 128), np.float32)
    for p in range(128):
        if (p % 64) < 32:
            rm[p + 32, p] = -1.0
        else:
            rm[p - 32, p] = 1.0
    cc = np.arange(128)
    angc = 2 * np.pi * ((cc[:, None] * cc[None, :]) % 128) / 128.0
    Cc = np.cos(angc) / np.sqrt(128.0)
    Sc = np.sin(angc) / np.sqrt(128.0)
    cmat = np.stack([rm, Cc, Sc, np.zeros((128, 128))], 1).astype(bf)

    def dft(N):
        n = np.arange(N, dtype=np.int64)
        m = (n[:, None] * n[None, :]) % N
        a = m.astype(np.float32) * np.float32(2 * np.pi / N)
        s = np.float32(1.0 / np.sqrt(N))
        return (np.cos(a) * s).astype(bf), (-np.sin(a) * s).astype(bf)
    CL, SL = dft(NLAT)
    def layL(M):
        return M.reshape(8, 8, 128, 16, 512).transpose(3, 0, 2, 1, 4)
    dftL = np.ascontiguousarray(np.stack([layL(CL), layL(SL)], 2))
    Cx, Sx = dft(NCTX)
    layC = lambda M: M.reshape(2, 128, 256).transpose(1, 0, 2)
    dftC = np.ascontiguousarray(np.stack([layC(Cx), layC(Sx)], 0))
    _CONST["m0"] = dict(cosT=cosT, sinT=sinT, cmat=np.ascontiguousarray(cmat), dftL=dftL, dftC=dftC)
    return _CONST["m0"]


def mixa0_wsel(w_in, j):
    cols = np.concatenate([np.arange(j * 128, (j + 1) * 128),
                           512 + np.arange(3 * j * 128, (3 * j + 3) * 128),
                           2048 + np.arange(j * 128, (j + 1) * 128),
                           2560 + np.arange(j * 128, (j + 1) * 128)])
    w = w_in[:, cols]
    return np.ascontiguousarray(w.reshape(KC, 128, 768).transpose(1, 0, 2))


def run_mixa0(a_lat, a_ctx, ab_w_in, qk_norm, trace=False):
    C = mixa0_consts()
    nc = build_mixa0_launch()
    gqk = np.ascontiguousarray(qk_norm.T.astype(np.float32))
    aT = [np.ascontiguousarray(np.concatenate([a_lat[b], a_ctx[b]], 0).T) for b in range(2)]
    in_maps = []
    for cidx in range(NCORES):
        b, j = divmod(cidx, 4)
        in_maps.append(dict(aT=aT[b], wsel=mixa0_wsel(ab_w_in, j), gqk=gqk, cosT=C["cosT"], sinT=C["sinT"],
                            cmat=C["cmat"], dftL=C["dftL"], dftC=C["dftC"]))
    res = run_bass_kernel_spmd(nc, in_maps, core_ids=list(range(NCORES)), trace=trace)
    cat_lat = np.empty((2, NLAT, 2048), a_lat.dtype)
    cat_ctx = np.empty((2, NCTX, 2048), a_lat.dtype)
    for cidx in range(NCORES):
        b, j = divmod(cidx, 4)
        c = res.results[cidx]["cat"].T
        for dst, rows in ((cat_lat, slice(0, NLAT)), (cat_ctx, slice(NLAT, NTOK))):
            dst[b][:, j * 128:(j + 1) * 128] = c[rows, 0:128]
            dst[b][:, 512 + 3 * j * 128: 512 + (3 * j + 3) * 128] = c[rows, 128:512]
    return cat_lat, cat_ctx, res

CH = 64
NHL = 4


def build_mixa1_launch(NLAT=8192):
    NTOK = NLAT + NCTX
    nc = bass.Bass("TRN2", target_bir_lowering=False)
    mk = MK(nc)
    aT_d = nc.dram_tensor("aT", [D, NTOK], BF16, kind="ExternalInput").ap()
    w_d = nc.dram_tensor("w5", [5, 128, KC, 512], F32, kind="ExternalInput").ap()
    lbl_d = nc.dram_tensor("lbl", [128, 2, NHL], F32, kind="ExternalInput").ap()
    gon_d = nc.dram_tensor("gon", [128, 1], F32, kind="ExternalInput").ap()
    cm_d = nc.dram_tensor("cm", [128, 4, 128], BF16, kind="ExternalInput").ap()
    sm_d = nc.dram_tensor("sm", [128, 512], F32, kind="ExternalInput").ap()
    out_d = nc.dram_tensor("og", [NHL * 128, NLAT], BF16, kind="ExternalOutput").ap()
    ofw_d = nc.dram_tensor("ofw", [NHL, 128, NLAT], BF16, kind="Internal").ap()
    qsc_d = nc.dram_tensor("qsc", [NHL, 128, NTOK], BF16, kind="Internal").ap()
    vsc_d = nc.dram_tensor("vsc", [NTOK // 128, 128, 512], BF16, kind="Internal").ap()
    io = Buf(None, "io"); outb = Buf(None, "out"); ofwb = Buf(None, "ofw"); qscb = Buf(None, "qsc"); vscb = Buf(None, "vsc")
    W = mk.sb("W", [128, 4, KC, 512], BF16)
    aslot = [mk.sb(f"aslot{i}", [128, KC, 512], BF16) for i in range(2)]
    cm = mk.sb("cm", [128, 4, 128], BF16)
    sm = mk.sb("sm", [128, 512], F32)
    lbl = mk.sb("lbl", [128, 2, NHL], F32)
    lb = mk.sb("lb", [128, NHL], F32)
    oml = mk.sb("oml", [128, NHL], F32)
    gon = mk.sb("gon", [128, 1], F32)
    ones = mk.sb("ones", [128, 128], BF16)
    Vt = [mk.sb(f"Vt{i}", [128, 512], BF16) for i in range(8)]
    S32 = [mk.sb(f"S32_{h}", [128, 128], F32) for h in range(NHL)]
    Sbf = [[mk.sb(f"Sbf_{h}_{i}", [128, 128], BF16) for i in range(6)] for h in range(NHL)]
    F = {n: [mk.sb(f"{n}{i}", [128, 512], F32) for i in range(2)] for n in ("sg", "fg", "lf", "b", "eb", "enb", "kk")}
    QdT = [mk.sb(f"QdT{i}", [128, 512], BF16) for i in range(2)]
    QSB = [mk.sb(f"QSB{i}", [128, 512], BF16) for i in range(2)]
    KdT = [mk.sb(f"KdT{i}", [128, 512], BF16) for i in range(2)]
    Kdtok = [mk.sb(f"Kdtok{i}", [128, 128], BF16) for i in range(2)]
    ATm = [mk.sb(f"ATm{i}", [128, 128], BF16) for i in range(4)]
    tmpS = [mk.sb(f"tmpS{i}", [128, 128], F32) for i in range(2)]
    ofs = [mk.sb(f"ofs{i}", [128, 512], BF16) for i in range(2)]
    O32 = [mk.sb(f"o32_{i}", [128, 512], F32) for i in range(2)]
    SQ = [mk.sb(f"sq_{i}", [128, 512], BF16) for i in range(2)]
    R32 = [mk.sb(f"r32_{i}", [128, 512], F32) for i in range(2)]
    SGG = [mk.sb(f"sgg_{i}", [128, 512], F32) for i in range(2)]
    epsb = mk.sb("epsb", [128, 1], F32)
    otile = [mk.sb(f"otile{i}", [128, 512], BF16) for i in range(2)]
    cnt = {}

    def rot(lst, key):
        i = cnt.get(key, 0); cnt[key] = i + 1
        return lst[i % len(lst)]
    bank = [mk.ps(f"bank{i}", [128, 512], F32) for i in range(7)]
    bankT = mk.ps("bankT", [128, 1024], BF16)
    pO2 = [bank[0], bank[3]]; pP = [bank[1], bank[2]]; pV = bank[1]; pSSQ = bank[1]
    pP4 = [bank[1], bank[2], bank[4], bank[5]]
    rA = [(bank[4], bank[4].t[:, 0:128]), (bank[5], bank[5].t[:, 0:128])]
    rS = [(bank[6], bank[6].t[:, 0:128]), (bank[6], bank[6].t[:, 128:256])]
    rT = [(bankT, bankT.t[:, 0:128])]
    mk.op("dve", lambda: nc.vector.memset(ones.t[:, :], 1.0), writes=[ones])
    mk.op("dve", lambda: nc.vector.memset(epsb.t[:, :], EPS), writes=[epsb])
    mk.dma("sp", cm.t[:, :, :], cm_d, reads=[io], writes=[cm])
    mk.dma("sp", sm.t[:, :], sm_d, reads=[io], writes=[sm])
    mk.dma("sp", lbl.t[:, :, :], lbl_d, reads=[io], writes=[lbl])
    mk.dma("sp", gon.t[:, :], gon_d, reads=[io], writes=[gon])
    mk.op("dve", lambda: nc.vector.tensor_tensor(out=lb.t[:, :], in0=lbl.t[:, 1, :], in1=lbl.t[:, 0, :], op=ALU.subtract),
          reads=[lbl], writes=[lb])
    mk.op("act", lambda: nc.scalar.activation(out=lb.t[:, :], in_=lb.t[:, :], func=AF.Sigmoid), reads=[lb], writes=[lb])
    mk.op("dve", lambda: nc.vector.tensor_scalar(out=oml.t[:, :], in0=lb.t[:, :], scalar1=-1.0, scalar2=1.0, op0=ALU.mult,
                                                 op1=ALU.add), reads=[lb], writes=[oml])
    IDENT = cm.t[:, 2, :]

    for direction in (0, 1):
        fw = direction == 0
        secs = [0, 1, 3] if fw else [0, 2, 3, 4]
        for si, s in enumerate(secs):
            mk.dma("pool", W.t[:, si, :, :], w_d[s], reads=[io], writes=[W])
        for h in range(NHL):
            mk.op("dve", lambda: nc.vector.memset(S32[h].t[:, :], 0.0), writes=[S32[h]])
            mk.op("pool", lambda: nc.gpsimd.memset(Sbf[h][0].t[:, :], 0.0), writes=[Sbf[h][0]])
        sidx = [0] * NHL
        ctx_tiles = [(NLAT, NCTX, False)]
        lat_tiles = [(t * 512, 512, True) for t in range(NLAT // 512)]
        tiles = (ctx_tiles + lat_tiles) if fw else (ctx_tiles + lat_tiles[::-1])
        MASK = cm.t[:, 0, :] if fw else cm.t[:, 1, :]
        for (c0, Wd, lat) in tiles:
            nblk = Wd // 128
            a = rot(aslot, "a")
            mk.dma("sp", a.t[:, :, 0:Wd], aT_d[:, c0:c0 + Wd].rearrange("(kc p) t -> p kc t", p=128), reads=[io], writes=[a])
            vts = []
            for blk in range(nblk):
                vt = rot(Vt, "vt")
                gblk = c0 // 128 + blk
                if fw:
                    mms = [(pV.t[:, :], a.t[:, kc, blk * 128:(blk + 1) * 128], W.t[:, 2, kc, :], kc == 0, kc == KC - 1) for kc in range(KC)]
                    mk.mmf([pV], mms, reads=[a, W])
                    mk.op("act", lambda: nc.scalar.copy(out=vt.t[:, :], in_=pV.t[:, :]), reads=[pV], writes=[vt])
                    mk.dma("sp", vsc_d[gblk], vt.t[:, :], reads=[vt], writes=[vscb])
                else:
                    mk.dma("sp", vt.t[:, :], vsc_d[gblk], reads=[vscb], writes=[vt])
                vts.append(vt)
            sl = slice(0, Wd)
            nch = Wd // CH
            blks = list(range(nblk)) if fw else list(range(nblk))[::-1]

            def stageA_pair(hs):
                R = {}
                for k_, h in enumerate(hs):
                    pq, pf = pP4[2 * k_], pP4[2 * k_ + 1]
                    for (pb, si) in (((pq, 0), (pf, 1)) if fw else ((pf, 1),)):
                        mms = [(pb.t[:, 0:Wd], W.t[:, si, kc, h * 128:(h + 1) * 128], a.t[:, kc, 0:Wd], kc == 0, kc == KC - 1) for kc in range(KC)]
                        mk.mmf([pb], mms, reads=[a, W])
                    R[h] = dict(t={n: rot(F[n], n) for n in F}, qd=rot(QdT, "qd"), kd=rot(KdT, "kd"), pq=pq, pf=pf, qsb=rot(QSB, "qsb"))
                ACT = lambda o, i, f, **kw: nc.scalar.activation(out=o, in_=i, func=f, **kw)
                for h in hs:
                    t, pf = R[h]["t"], R[h]["pf"]
                    mk.op("act", lambda: ACT(t["sg"].t[:, sl], pf.t[:, sl], AF.Sigmoid), reads=[pf], writes=[t["sg"]])
                for h in hs:
                    t = R[h]["t"]
                    mk.op("dve", lambda: nc.vector.tensor_scalar(out=t["fg"].t[:, sl], in0=t["sg"].t[:, sl], scalar1=oml.t[:, h:h + 1],
                                                                 scalar2=lb.t[:, h:h + 1], op0=ALU.mult, op1=ALU.add),
                          reads=[t["sg"], oml, lb], writes=[t["fg"]])
                for h in hs:
                    pq, qsb = R[h]["pq"], R[h]["qsb"]
                    if fw:
                        mk.op("act", lambda: ACT(qsb.t[:, sl], pq.t[:, sl], AF.Silu), reads=[pq], writes=[qsb])
                        mk.dma("sp", qsc_d[h][:, c0:c0 + Wd], qsb.t[:, sl], reads=[qsb], writes=[qscb])
                    else:
                        mk.dma("sp", qsb.t[:, sl], qsc_d[h][:, c0:c0 + Wd], reads=[qscb], writes=[qsb])
                for h in hs:
                    t = R[h]["t"]
                    mk.op("act", lambda: ACT(t["lf"].t[:, sl], t["fg"].t[:, sl], AF.Ln), reads=[t["fg"]], writes=[t["lf"]])
                    mk.op("pool", lambda: nc.gpsimd.tensor_scalar(out=t["kk"].t[:, sl], in0=t["fg"].t[:, sl], scalar1=-1.0, scalar2=1.0,
                                                                  op0=ALU.mult, op1=ALU.add), reads=[t["fg"]], writes=[t["kk"]])
                for h in hs:
                    t = R[h]["t"]
                    mk.op("dve", lambda: nc.vector.tensor_tensor_scan(out=t["b"].t[:, sl], data0=sm.t[:, sl], data1=t["lf"].t[:, sl],
                                                                      initial=0.0, op0=ALU.mult, op1=ALU.add),
                          reads=[sm, t["lf"]], writes=[t["b"]])
                    if not fw:
                        b3 = t["b"].t[:, sl].rearrange("p (c t) -> p c t", t=CH)
                        mk.op("dve", lambda: nc.vector.tensor_tensor(out=t["lf"].t[:, sl], in0=t["lf"].t[:, sl], in1=t["b"].t[:, sl], op=ALU.subtract),
                              reads=[t["lf"], t["b"]], writes=[t["lf"]])
                        mk.op("dve", lambda: nc.vector.tensor_tensor(out=t["b"].t[:, sl].rearrange("p (c t) -> p c t", t=CH),
                                                                     in0=t["lf"].t[:, sl].rearrange("p (c t) -> p c t", t=CH),
                                                                     in1=b3[:, :, CH - 1:CH].to_broadcast([128, nch, CH]), op=ALU.add),
                              reads=[t["lf"], t["b"]], writes=[t["b"]])
                for h in hs:
                    t = R[h]["t"]
                    mk.op("act", lambda: ACT(t["eb"].t[:, sl], t["b"].t[:, sl], AF.Exp), reads=[t["b"]], writes=[t["eb"]])
                    mk.op("act", lambda: ACT(t["enb"].t[:, sl], t["b"].t[:, sl], AF.Exp, scale=-1.0), reads=[t["b"]], writes=[t["enb"]])
                for h in hs:
                    t, qd, kd = R[h]["t"], R[h]["qd"], R[h]["kd"]
                    qsb = R[h]["qsb"]
                    mk.op("dve", lambda: nc.vector.tensor_tensor(out=qd.t[:, sl], in0=qsb.t[:, sl], in1=t["eb"].t[:, sl], op=ALU.mult),
                          reads=[qsb, t["eb"]], writes=[qd])
                    mk.op("dve", lambda: nc.vector.tensor_tensor(out=kd.t[:, sl], in0=t["kk"].t[:, sl], in1=t["enb"].t[:, sl], op=ALU.mult),
                          reads=[t["kk"], t["enb"]], writes=[kd])
                return {h: dict(qd=R[h]["qd"], kd=R[h]["kd"], eb=R[h]["t"]["eb"]) for h in hs}

            def block_ops(h, blk, A_, pend, pOh):
                qd, kd, eb = A_["qd"], A_["kd"], A_["eb"]
                bs = slice(blk * 128, (blk + 1) * 128)
                vt = vts[blk]
                rtb, rtap = rot(rT, "rt"); kt_ = rot(Kdtok, "kt")
                mk.op("pe", lambda: nc.tensor.transpose(out=rtap, in_=kd.t[:, bs], identity=IDENT), reads=[kd, cm], writes=[rtb])
                mk.op("act", lambda: nc.scalar.copy(out=kt_.t[:, :], in_=rtap), reads=[rtb], writes=[kt_])
                rab, raap = rot(rA, "ra"); atm = rot(ATm, "atm")
                mk.mmf([rab], [(raap, kd.t[:, bs], qd.t[:, bs], True, True)], reads=[kd, qd])
                mk.op("dve", lambda: nc.vector.tensor_tensor(out=atm.t[:, :], in0=raap, in1=MASK, op=ALU.mult), reads=[rab, cm], writes=[atm])
                chunks = [0, 1] if fw else [1, 0]
                s_before = []
                for ci in chunks:
                    s_before.append(Sbf[h][sidx[h]])
                    rsb, rsap = rot(rS, "rs"); tS = rot(tmpS, "ts")
                    rows = slice(ci * CH, (ci + 1) * CH)
                    mk.mmf([rsb], [(rsap, kt_.t[rows, :], vt.t[rows, h * 128:(h + 1) * 128], True, True)], reads=[kt_, vt])
                    cglob = blk * 2 + ci
                    col = cglob * CH + (CH - 1 if fw else 0)
                    et = eb.t[:, col:col + 1]
                    mk.op("act", lambda: nc.scalar.activation(out=tS.t[:, :], in_=rsap, func=AF.Copy, scale=et), reads=[rsb, eb], writes=[tS])
                    sidx[h] = (sidx[h] + 1) % 6
                    nxt = Sbf[h][sidx[h]]
                    mk.op("dve", lambda: nc.vector.scalar_tensor_tensor(out=nxt.t[:, :], in0=S32[h].t[:, :], scalar=et, in1=tS.t[:, :],
                                                                        op0=ALU.mult, op1=ALU.add), reads=[S32[h], tS, eb], writes=[nxt])
                    mk.op("dve", lambda: nc.vector.scalar_tensor_tensor(out=S32[h].t[:, :], in0=S32[h].t[:, :], scalar=et, in1=tS.t[:, :],
                                                                        op0=ALU.mult, op1=ALU.add), reads=[S32[h], tS, eb], writes=[S32[h]])
                if lat:
                    def omm():
                        mms = [(pOh.t[:, bs], vt.t[:, h * 128:(h + 1) * 128], atm.t[:, :], True, False)]
                        for n_, ci in enumerate(chunks):
                            cs_ = slice(blk * 128 + ci * CH, blk * 128 + (ci + 1) * CH)
                            mms.append((pOh.t[:, cs_], s_before[n_].t[:, :], qd.t[:, cs_], False, n_ == 1))
                        mk.mmf([pOh], mms, reads=[vt, atm, qd] + s_before)
                    pend.append(omm)
                    if len(pend) > 1:
                        pend.pop(0)()

            def evac_pair(hs):
                if fw:
                    for k_, h in enumerate(hs):
                        of = rot(ofs, "of")
                        mk.op("act", lambda: nc.scalar.copy(out=of.t[:, :], in_=pO2[k_].t[:, :]), reads=[pO2[k_]], writes=[of])
                        mk.dma("sp", ofw_d[h][:, c0:c0 + 512], of.t[:, :], reads=[of], writes=[ofwb])
                    return
                E = {}
                for k_, h in enumerate(hs):
                    of = rot(ofs, "of")
                    mk.dma("sp", of.t[:, :], ofw_d[h][:, c0:c0 + 512], reads=[ofwb], writes=[of])
                    o32 = O32[k_]
                    mk.op("dve", lambda: nc.vector.tensor_tensor(out=o32.t[:, :], in0=pO2[k_].t[:, :], in1=of.t[:, :], op=ALU.add),
                          reads=[pO2[k_], of], writes=[o32])
                    pg = pP4[2 * k_ + 1]
                    mms = [(pg.t[:, :], W.t[:, 3, kc, h * 128:(h + 1) * 128], a.t[:, kc, :], kc == 0, kc == KC - 1) for kc in range(KC)]
                    mk.mmf([pg], mms, reads=[a, W])
                    E[h] = dict(o32=o32, pg=pg, sgg=SGG[k_], sq=SQ[k_], r32=R32[k_], ps=pP4[2 * k_])
                for h in hs:
                    e = E[h]
                    mk.op("act", lambda: nc.scalar.activation(out=e["sgg"].t[:, :], in_=e["pg"].t[:, :], func=AF.Exp, scale=-1.0),
                          reads=[e["pg"]], writes=[e["sgg"]])
                    mk.op("pool", lambda: nc.gpsimd.tensor_tensor(out=e["sq"].t[:, :], in0=e["o32"].t[:, :], in1=e["o32"].t[:, :], op=ALU.mult),
                          reads=[e["o32"]], writes=[e["sq"]])
                for h in hs:
                    e = E[h]
                    mk.mmf([e["ps"]], [(e["ps"].t[:, :], ones.t[:, :], e["sq"].t[:, :], True, True)], reads=[ones, e["sq"]])
                    mk.op("pool", lambda: nc.gpsimd.tensor_scalar(out=e["sgg"].t[:, :], in0=e["sgg"].t[:, :], scalar1=1.0, scalar2=None, op0=ALU.add),
                          reads=[e["sgg"]], writes=[e["sgg"]])
                for h in hs:
                    e = E[h]
                    mk.op("act", lambda: nc.scalar.activation(out=e["r32"].t[:, :], in_=e["ps"].t[:, :], func=AF.Ln, bias=epsb.t[:, 0:1], scale=1.0 / 128),
                          reads=[e["ps"], epsb], writes=[e["r32"]])
                    mk.op("dve", lambda: nc.vector.reciprocal(out=e["sgg"].t[:, :], in_=e["sgg"].t[:, :]), reads=[e["sgg"]], writes=[e["sgg"]])
                for h in hs:
                    e = E[h]
                    mk.op("act", lambda: nc.scalar.activation(out=e["r32"].t[:, :], in_=e["r32"].t[:, :], func=AF.Exp, scale=-0.5),
                          reads=[e["r32"]], writes=[e["r32"]])
                for h in hs:
                    e = E[h]
                    mk.op("dve", lambda: nc.vector.scalar_tensor_tensor(out=e["o32"].t[:, :], in0=e["o32"].t[:, :], scalar=gon.t[:, 0:1], in1=e["r32"].t[:, :],
                                                                        op0=ALU.mult, op1=ALU.mult), reads=[e["o32"], gon, e["r32"]], writes=[e["o32"]])
                    ot = rot(otile, "ot")
                    mk.op("dve", lambda: nc.vector.tensor_tensor(out=ot.t[:, :], in0=e["o32"].t[:, :], in1=e["sgg"].t[:, :], op=ALU.mult),
                          reads=[e["o32"], e["sgg"]], writes=[ot])
                    mk.dma("sp", out_d[h * 128:(h + 1) * 128, c0:c0 + 512], ot.t[:, :], reads=[ot], writes=[outb])

            for hp in range(0, NHL, 2):
                hs = [hp, hp + 1]
                A = stageA_pair(hs)
                pend = {h: [] for h in hs}
                for blk in blks:
                    for k_, h in enumerate(hs):
                        block_ops(h, blk, A[h], pend[h], pO2[k_])
                if lat:
                    for h in hs:
                        for f_ in pend[h]:
                            f_()
                    evac_pair(hs)
    mk.finish([outb])
    return nc


def mixa1_consts():
    if "m1" in _CONST:
        return _CONST["m1"]
    bf = ml_dtypes.bfloat16
    s = np.arange(128)[:, None]; t = np.arange(128)[None, :]
    same = (s // CH) == (t // CH)
    mfw = (same & (s <= t)).astype(np.float32)
    mbw = (same & (s >= t)).astype(np.float32)
    cm = np.stack([mfw, mbw, np.eye(128, dtype=np.float32), np.zeros((128, 128), np.float32)], 1).astype(bf)
    smr = np.ones((128, 512), np.float32); smr[:, ::CH] = 0.0
    _CONST["m1"] = dict(cm=np.ascontiguousarray(cm), sm=smr)
    return _CONST["m1"]


def run_mixa1(a_lat, a_ctx, hgrn_w_in, lb_logits, o_norm, trace=False):
    C = mixa1_consts()
    nc = build_mixa1_launch()
    aT = [np.ascontiguousarray(np.concatenate([a_lat[b], a_ctx[b]], 0).T) for b in range(2)]
    gon = np.ascontiguousarray(o_norm.reshape(128, 1).astype(np.float32))
    in_maps = []
    for cidx in range(NCORES):
        b, j = divmod(cidx, 4)
        cols = np.arange(j * 512, (j + 1) * 512)
        w5 = np.stack([hgrn_w_in[:, s * 2048 + cols].reshape(KC, 128, 512).transpose(1, 0, 2) for s in range(5)], 0)
        lbl = lb_logits[:, cols].reshape(2, NHL, 128).transpose(2, 0, 1)
        in_maps.append(dict(aT=aT[b], w5=np.ascontiguousarray(w5), lbl=np.ascontiguousarray(lbl), gon=gon, cm=C["cm"], sm=C["sm"]))
    res = run_bass_kernel_spmd(nc, in_maps, core_ids=list(range(NCORES)), trace=trace)
    og = np.empty((2, NLAT, 2048), a_lat.dtype)
    for cidx in range(NCORES):
        b, j = divmod(cidx, 4)
        og[b][:, j * 512:(j + 1) * 512] = res.results[cidx]["og"].T
    return og, res


def _run(nc, in_maps):
    return run_bass_kernel_spmd(nc, in_maps, core_ids=list(range(NCORES))).results


def pack_lat(lat):
    return [np.ascontiguousarray(lat[c // 4, (c % 4) * 2048:(c % 4 + 1) * 2048].T) for c in range(NCORES)]


def kernel(x, c, ctx, c_ctx, w_mod, b_mod, norm_gains, ffn_w_in, ffn_w_out, ab_w_in, qk_norm, ab_w_out,
           hgrn_w_in, hgrn_lb_logits, hgrn_o_norm, hgrn_w_out, final_norm):
    f32 = np.float32
    x, c, ctx, c_ctx = (np.asarray(a, f32) for a in (x, c, ctx, c_ctx))
    mod = run_mod(c, c_ctx, np.asarray(w_mod, f32), np.asarray(b_mod, f32))
    norm_gains = np.asarray(norm_gains, f32)
    nc = build_tok_launch(PWL, SUBT_L, 1, [("ffn", 0, 0), ("normout", 0)])
    hT = pack_tokens(x, ctx)
    w_in = lay_w_in(np.asarray(ffn_w_in[0, 0], f32)); w_out = lay_w_out(np.asarray(ffn_w_out[0, 0], f32), NH_FFN)
    g0 = gains_for(norm_gains, [0])
    res = _run(nc, [{"hT": hT[k], "modT": modT_for_core(mod, [0], k // 4), "gains": g0, "w0_in": w_in, "w0_out": w_out}
                    for k in range(NCORES)])
    hT = [r["hT_out"] for r in res]
    a_lat, a_ctx = unpack_tokens([r["aT_out"] for r in res], D)
    cat_lat, cat_ctx, _ = run_mixa0(a_lat, a_ctx, np.asarray(ab_w_in[0], f32), np.asarray(qk_norm[0], f32))
    nc = build_tok_launch(PWL, SUBT_L, 2, [("proj", 0), ("ffn", 0, 1), ("ffn", 1, 0), ("normout", 1)])
    cat = pack_tokens(cat_lat, cat_ctx)
    wp = lay_w_out(np.asarray(ab_w_out[0], f32), 1)[0]
    w1_in = lay_w_in(np.asarray(ffn_w_in[0, 1], f32)); w1_out = lay_w_out(np.asarray(ffn_w_out[0, 1], f32), NH_FFN)
    w2_in = lay_w_in(np.asarray(ffn_w_in[1, 0], f32)); w2_out = lay_w_out(np.asarray(ffn_w_out[1, 0], f32), NH_FFN)
    g01 = gains_for(norm_gains, [0, 1])
    res = _run(nc, [{"hT": hT[k], "modT": modT_for_core(mod, [0, 1], k // 4), "gains": g01, "w0_proj": wp, "cat0": cat[k],
                     "w1_in": w1_in, "w1_out": w1_out, "w2_in": w2_in, "w2_out": w2_out} for k in range(NCORES)])
    h_lat, _ = unpack_tokens([r["hT_out"] for r in res], D)
    a_lat, a_ctx = unpack_tokens([r["aT_out"] for r in res], D)
    og, _ = run_mixa1(a_lat, a_ctx, np.asarray(hgrn_w_in[0], f32), np.asarray(hgrn_lb_logits, f32), np.asarray(hgrn_o_norm[0], f32))
    nc = build_tok_launch(1024, subtiles_of(1024, 0), 1, [("proj", 0), ("ffn", 0, 1), ("final",)])
    hT = pack_lat(h_lat)
    ogT = pack_lat(og)
    wp = lay_w_out(np.asarray(hgrn_w_out[0], f32), 1)[0]
    w1_in = lay_w_in(np.asarray(ffn_w_in[1, 1], f32)); w1_out = lay_w_out(np.asarray(ffn_w_out[1, 1], f32), NH_FFN)
    g1 = gains_for(norm_gains, [1])
    fng = lay_vec(np.asarray(final_norm, f32))
    res = _run(nc, [{"hT": hT[k], "modT": modT_for_core(mod, [1], k // 4), "gains": g1, "w0_proj": wp, "cat0": ogT[k],
                     "w1_in": w1_in, "w1_out": w1_out, "fng": fng} for k in range(NCORES)])
    out = np.empty((2, 8192, D), f32)
    for k in range(NCORES):
        out[k // 4, (k % 4) * 2048:(k % 4 + 1) * 2048] = res[k]["yT"].T
    return out
```
